# Optimizing a Trainium2 kernel written in Bass

```python
import jax, jax.numpy as jnp
from jax import lax
import numpy as np

D_MODEL = 1024
BATCH = 8
SEQ = 4096
DEPTH = 1

ATTN_WIDTH = D_MODEL // 2
HGRN_WIDTH = D_MODEL - ATTN_WIDTH
MIX_WIDTH = ATTN_WIDTH + HGRN_WIDTH
ATTN_HEAD_DIM = 64
ATTN_HEADS = ATTN_WIDTH // ATTN_HEAD_DIM
HGRN_EXPAND = 128
HGRN_HEADS = HGRN_WIDTH // HGRN_EXPAND
DILATED_PATTERNS = ((128, 1), (512, 4), (2048, 16))
ATTN_BLOCK = 128
ROPE_THETA = 500000.0
ROPE_DIMS = ATTN_HEAD_DIM // 4
HGRN_CHUNK = 64
NORM_EPS = 1e-6
IN_COLS = 4 * ATTN_WIDTH + 4 * HGRN_WIDTH

kernel_name = "hymba_dilated_attn_hgrn2_hybrid"


def _rmsnorm(x, w):
    xf = x.astype(jnp.float32)
    y = xf * lax.rsqrt(jnp.mean(xf * xf, axis=-1, keepdims=True) + NORM_EPS)
    return (y * w.astype(jnp.float32)).astype(x.dtype)


def _head_rmsnorm(o, w, n_heads):
    b, s, width = o.shape
    oh = o.reshape(b, s, n_heads, width // n_heads)
    oh = oh * lax.rsqrt(jnp.mean(oh * oh, axis=-1, keepdims=True) + NORM_EPS)
    return oh.reshape(b, s, width) * w.astype(jnp.float32)


def _partial_rotary(t, positions):
    half = ROPE_DIMS // 2
    inv_freq = ROPE_THETA ** (-jnp.arange(half, dtype=jnp.float32) * (2.0 / ROPE_DIMS))
    ang = positions.astype(jnp.float32)[..., None] * inv_freq
    cos = jnp.cos(ang)[:, :, None, :]
    sin = jnp.sin(ang)[:, :, None, :]
    t = t.astype(jnp.float32)
    t1, t2, rest = t[..., :half], t[..., half:ROPE_DIMS], t[..., ROPE_DIMS:]
    return jnp.concatenate([t1 * cos - t2 * sin, t2 * cos + t1 * sin, rest], axis=-1)


def _dilated_window_attn(q, k, v, window, dilation):
    b, h, s, e = q.shape
    span = window // dilation
    sub_len = s // dilation
    n_blk = -(-sub_len // ATTN_BLOCK)
    pad = n_blk * ATTN_BLOCK - sub_len

    def to_blocks(t):
        t = t.reshape(b, h, sub_len, dilation, e).transpose(0, 1, 3, 2, 4)
        t = jnp.pad(t, ((0, 0), (0, 0), (0, 0), (0, pad), (0, 0)))
        return t.reshape(b, h, dilation, n_blk, ATTN_BLOCK, e)

    def with_prev(t):
        prev = jnp.concatenate([jnp.zeros_like(t[:, :, :, :1]), t[:, :, :, :-1]], axis=3)
        return jnp.concatenate([prev, t], axis=4)

    qb, kb, vb = to_blocks(q), to_blocks(k), to_blocks(v)
    kw, vw = with_prev(kb), with_prev(vb)
    scores = jnp.einsum('bhrnqe,bhrnke->bhrnqk', qb, kw) * (e ** -0.5)
    qi = jnp.arange(ATTN_BLOCK)[:, None]
    kj = jnp.arange(2 * ATTN_BLOCK)[None, :]
    dist = ATTN_BLOCK + qi - kj
    band = (dist >= 0) & (dist <= span)
    first = (jnp.arange(n_blk) == 0)[:, None, None]
    mask = band[None] & ~(first & (kj < ATTN_BLOCK)[None])
    scores = jnp.where(mask, scores, -jnp.inf)
    m = jnp.max(scores, axis=-1)
    p = jnp.exp(scores - m[..., None])
    l = jnp.sum(p, axis=-1)
    o = jnp.einsum('bhrnqk,bhrnke->bhrnqe', p, vw)

    def from_blocks(t):
        tail = t.shape[5:]
        t = t.reshape(b, h, dilation, n_blk * ATTN_BLOCK, *tail)[:, :, :, :sub_len]
        t = jnp.moveaxis(t, 2, 3)
        return t.reshape(b, h, s, *tail)

    return from_blocks(o), from_blocks(m), from_blocks(l)


def _longnet_mixture(q, k, v):
    outs = [_dilated_window_attn(q, k, v, w, d) for (w, d) in DILATED_PATTERNS]
    m_all = jnp.stack([o[1] for o in outs], axis=0)
    m_top = jnp.max(m_all, axis=0)
    wts = jnp.exp(m_all - m_top)
    num = sum(wts[i][..., None] * outs[i][0] for i in range(len(outs)))
    den = sum(wts[i] * outs[i][2] for i in range(len(outs)))
    return num / den[..., None]


def _hgrn2_chunked(q, k, v, log_f):
    b, h, s, e = q.shape
    ev = v.shape[-1]
    nc = s // HGRN_CHUNK
    rs = lambda t: t.reshape(b, h, nc, HGRN_CHUNK, t.shape[-1])
    q, k, v, log_f = rs(q), rs(k), rs(v), rs(log_f)
    cum = jnp.cumsum(log_f, axis=3)
    last = cum[:, :, :, -1:]
    q_dec = q * jnp.exp(cum)
    k_inv = k * jnp.exp(-cum)
    k_end = k * jnp.exp(last - cum)
    causal = jnp.tril(jnp.ones((HGRN_CHUNK, HGRN_CHUNK), dtype=bool))
    att = jnp.where(causal, jnp.einsum('bhnte,bhnse->bhnts', q_dec, k_inv), 0.0)
    o_intra = jnp.einsum('bhnts,bhnsv->bhntv', att, v)
    chunk_decay = jnp.exp(last[:, :, :, 0])

    def step(state, xs):
        qd, ke, vc, dec = xs
        o = jnp.einsum('bhte,bhev->bhtv', qd, state)
        state = dec[..., None] * state + jnp.einsum('bhte,bhtv->bhev', ke, vc)
        return state, o

    xs = (jnp.moveaxis(q_dec, 2, 0), jnp.moveaxis(k_end, 2, 0),
          jnp.moveaxis(v, 2, 0), jnp.moveaxis(chunk_decay, 2, 0))
    state0 = jnp.zeros((b, h, e, ev), jnp.float32)
    _, o_inter = lax.scan(step, state0, xs)
    o = o_intra + jnp.moveaxis(o_inter, 0, 2)
    return o.reshape(b, h, s, ev)


def setup_inputs(seed: int = 0) -> dict:
    key = jax.random.key(seed)
    ks = jax.random.split(key, 10)
    x = jax.random.normal(ks[0], (BATCH, SEQ, D_MODEL), jnp.float32)
    offset = jax.random.randint(ks[1], (BATCH, 1), 0, 4096, dtype=jnp.int32)
    positions = (offset + jnp.arange(SEQ, dtype=jnp.int32)[None, :]).astype(jnp.int32)
    w_in = jax.random.normal(ks[2], (DEPTH, D_MODEL, IN_COLS), jnp.float32) * D_MODEL ** -0.5
    w_out = jax.random.normal(ks[3], (DEPTH, MIX_WIDTH, D_MODEL), jnp.float32) * MIX_WIDTH ** -0.5
    mix_norm_w = 1.0 + 0.02 * jax.random.normal(ks[4], (DEPTH, D_MODEL), jnp.float32)
    attn_out_norm_w = 1.0 + 0.02 * jax.random.normal(ks[5], (DEPTH, ATTN_WIDTH), jnp.float32)
    hgrn_out_norm_w = 1.0 + 0.02 * jax.random.normal(ks[6], (DEPTH, HGRN_WIDTH), jnp.float32)
    hgrn_lb_raw = 0.1 * jax.random.normal(ks[7], (DEPTH + 1, HGRN_WIDTH), jnp.float32)
    final_norm_w = 1.0 + 0.02 * jax.random.normal(ks[8], (D_MODEL,), jnp.float32)
    return {"x": x, "positions": positions, "w_in": w_in, "w_out": w_out,
            "mix_norm_w": mix_norm_w, "attn_out_norm_w": attn_out_norm_w,
            "hgrn_out_norm_w": hgrn_out_norm_w, "hgrn_lb_raw": hgrn_lb_raw,
            "final_norm_w": final_norm_w}


def reference(x, positions, w_in, w_out, mix_norm_w, attn_out_norm_w,
              hgrn_out_norm_w, hgrn_lb_raw, final_norm_w):
    b, s, _ = x.shape
    f32 = jnp.float32
    lower_bounds = jnp.cumsum(jax.nn.softmax(hgrn_lb_raw.astype(f32), axis=0), axis=0)
    split_at = [ATTN_WIDTH * i for i in range(1, 5)] + \
               [4 * ATTN_WIDTH + HGRN_WIDTH * i for i in range(1, 4)]
    for layer in range(DEPTH):
        hn = _rmsnorm(x, mix_norm_w[layer])
        proj = hn @ w_in[layer]
        aq, ak, av, ag, hq, hf, hi, hg = jnp.split(proj, split_at, axis=-1)

        aq = _partial_rotary(aq.reshape(b, s, ATTN_HEADS, ATTN_HEAD_DIM), positions)
        ak = _partial_rotary(ak.reshape(b, s, ATTN_HEADS, ATTN_HEAD_DIM), positions)
        av = av.reshape(b, s, ATTN_HEADS, ATTN_HEAD_DIM).astype(f32)
        bhse = lambda t: t.transpose(0, 2, 1, 3)
        attn = _longnet_mixture(bhse(aq), bhse(ak), bhse(av))
        attn = attn.transpose(0, 2, 1, 3).reshape(b, s, ATTN_WIDTH)

        lb = lower_bounds[layer]
        f = lb + (1.0 - lb) * jax.nn.sigmoid(hf.astype(f32))
        hkey = 1.0 - f
        hquery = jax.nn.silu(hq.astype(f32))
        hh = lambda t: t.reshape(b, s, HGRN_HEADS, HGRN_EXPAND).transpose(0, 2, 1, 3)
        rec = _hgrn2_chunked(hh(hquery), hh(hkey), hh(hi.astype(f32)), hh(jnp.log(f)))
        rec = rec.transpose(0, 2, 1, 3).reshape(b, s, HGRN_WIDTH)

        y_attn = _head_rmsnorm(attn, attn_out_norm_w[layer], ATTN_HEADS) * jax.nn.silu(ag.astype(f32))
        y_hgrn = _head_rmsnorm(rec, hgrn_out_norm_w[layer], HGRN_HEADS) * jax.nn.silu(hg.astype(f32))
        mixed = jnp.concatenate([y_attn, y_hgrn], axis=-1).astype(x.dtype)
        x = x + mixed @ w_out[layer]
    return _rmsnorm(x, final_norm_w)
```

```python
import contextlib
import numpy as np
import concourse.bass as bass
import concourse.mybir as mybir
from concourse.bass_utils import run_bass_kernel_spmd

F32 = mybir.dt.float32
BF16 = mybir.dt.bfloat16
I32 = mybir.dt.int32
AF = mybir.ActivationFunctionType
ALU = mybir.AluOpType

D = 1024
EPS = 1e-6
ROPE_THETA = 500000.0
PATTERNS = (1, 4, 16)
TWO_PI = 2.0 * np.pi


def _split_2pi():
    c1 = 6.28125
    r = TWO_PI - c1
    c2 = np.float32(r)
    c2 = (c2.view(np.uint32) & np.uint32(0xFFFFF000)).view(np.float32)
    c3 = np.float32(r - float(c2))
    return float(c1), float(c2), float(c3)


C_G = 0
C_AW = 8
C_HW = 12
C_LB0 = 16
C_LB1 = 20
C_INVF = 24
C_SGN = 25
C_HPI = 26
C_EPS = 27
C_ONE = 28
C_IDENT = 32
C_PERM = 160
C_AMASK = 288
C_M2 = 544
C_RST = 672
NCST = 1184


def _const_table():
    c = np.zeros((128, NCST), np.float32)
    p = np.arange(128)
    e = p % 64
    half = 8
    invf = ROPE_THETA ** (-np.arange(half, dtype=np.float64) * (2.0 / 16.0))
    c[:, C_INVF] = np.where(e < 16, invf[e % 8], 0.0)
    c[:, C_SGN] = np.where(e < 8, -1.0, np.where(e < 16, 1.0, 0.0))
    c[:, C_HPI] = np.pi / 2
    c[:, C_EPS] = EPS
    c[:, C_ONE] = 1.0
    c[:, C_IDENT:C_IDENT + 128] = np.eye(128)
    perm = np.zeros((128, 128), np.float32)
    for m in range(128):
        em = m % 64
        if em < 8:
            perm[m + 8, m] = 1.0
        elif em < 16:
            perm[m - 8, m] = 1.0
    c[:, C_PERM:C_PERM + 128] = perm
    k = np.arange(128)[:, None]
    q = np.arange(128)[None, :]
    am = np.concatenate([(q >= k), (q <= k)], axis=1).astype(np.float32)
    c[:, C_AMASK:C_AMASK + 256] = am
    s = np.arange(128)[:, None]
    t = np.arange(128)[None, :]
    c[:, C_M2:C_M2 + 128] = ((s // 64 == t // 64) & (s <= t)).astype(np.float32)
    rst = np.ones(512, np.float32)
    rst[0::64] = 0.0
    c[:, C_RST:C_RST + 512] = rst[None, :]
    return c


class Sched:
    COMPUTE = ("pe", "act", "dve", "pool")

    def __init__(self, nc, stack):
        self.nc = nc
        self.stack = stack
        self.streams = {k: [] for k in ("pe", "act", "dve", "pool", "sp")}
        self.sems = {}
        self.cnt = {}
        self.sigs = {k: [] for k in self.COMPUTE}
        self.waited = {k: {} for k in self.streams}
        self.last_w = {}
        self.readers = {}
        for k in self.COMPUTE:
            self._sem(k)

    def _sem(self, name):
        if name not in self.sems:
            self.sems[name] = self.stack.enter_context(self.nc.semaphore("s_" + "".join(ch if ch.isalnum() else "_" for ch in str(name))))
            self.cnt[name] = 0
        return self.sems[name]

    def _resolve(self, ev):
        kind = ev[0]
        if kind == "dma":
            return ev[1], ev[2]
        eng, seq = ev[1], ev[2]
        best = None
        for s, v in reversed(self.sigs[eng]):
            if s >= seq:
                best = v
            else:
                break
        if best is not None:
            return eng, best
        rec = self.streams[eng][-1]
        assert rec["seq"] >= seq
        self.cnt[eng] += 1
        rec["inc"] = True
        self.sigs[eng].append((rec["seq"], self.cnt[eng]))
        return eng, self.cnt[eng]

    def _collect(self, eng, reads, writes, acc):
        evs = []
        for r in reads:
            ev = self.last_w.get(r)
            if ev is not None:
                evs.append(ev)
        for w in writes:
            ev = self.last_w.get(w)
            if ev is not None and not (acc and ev[0] == "c" and ev[1] == eng):
                evs.append(ev)
            for ev in self.readers.get(w, ()):
                evs.append(ev)
        waits = {}
        for ev in evs:
            name, val = self._resolve(ev)
            if self.waited[eng].get(name, 0) >= val:
                continue
            waits[name] = max(waits.get(name, 0), val)
        for name, val in waits.items():
            self.waited[eng][name] = val
        return list(waits.items())

    def _commit(self, ev, reads, writes):
        for r in reads:
            self.readers.setdefault(r, []).append(ev)
        for w in writes:
            self.last_w[w] = ev
            self.readers[w] = []

    def op(self, eng, method, R=(), W=(), sig=None, acc=False, **kw):
        if sig is None:
            sig = (eng != "pe")
        reads, writes = list(R), list(W)
        fn = (lambda e, method=method, kw=kw: getattr(e, method)(**kw))
        waits = self._collect(eng, reads, writes, acc)
        seq = len(self.streams[eng])
        rec = {"fn": fn, "waits": waits, "inc": False, "seq": seq, "dma": None}
        self.streams[eng].append(rec)
        if sig:
            self.cnt[eng] += 1
            rec["inc"] = True
            self.sigs[eng].append((seq, self.cnt[eng]))
        self._commit(("c", eng, seq), reads, writes)

    def dma(self, q, key, R=(), W=(), **kw):
        reads, writes = list(R), list(W)
        fn = (lambda e, kw=kw: e.dma_start(**kw))
        name = ("dma", key)
        self._sem(name)
        waits = self._collect(q, reads, writes, False)
        self.cnt[name] += 16
        rec = {"fn": fn, "waits": waits, "inc": False, "seq": len(self.streams[q]), "dma": name}
        self.streams[q].append(rec)
        self._commit(("dma", name, self.cnt[name]), reads, writes)

    def barrier(self):
        for eng in self.COMPUTE:
            if self.streams[eng]:
                rec = self.streams[eng][-1]
                if not rec["inc"]:
                    self.cnt[eng] += 1
                    rec["inc"] = True
                    self.sigs[eng].append((rec["seq"], self.cnt[eng]))
        for eng in self.streams:
            waits = []
            for name, val in self.cnt.items():
                if val <= 0 or name == eng:
                    continue
                if self.waited[eng].get(name, 0) >= val:
                    continue
                self.waited[eng][name] = val
                waits.append((name, val))
            self.streams[eng].append({"fn": None, "waits": waits, "inc": False, "seq": len(self.streams[eng]), "dma": None})

    def final_wait(self, q):
        waits = {}
        for name, val in self.cnt.items():
            if isinstance(name, tuple) and name[0] == "dma" and val > 0:
                waits[name] = val
        rec = {"fn": None, "waits": list(waits.items()), "inc": False, "seq": len(self.streams[q]), "dma": None}
        self.streams[q].append(rec)

    def emit(self):
        nc = self.nc
        sems = self.sems

        def replay(name):
            def run(e):
                for rec in self.streams[name]:
                    for sname, val in rec["waits"]:
                        e.wait_ge(sems[sname], val)
                    if rec["fn"] is None:
                        continue
                    ins = rec["fn"](e)
                    if rec["dma"] is not None:
                        ins.then_inc(sems[rec["dma"]], 16)
                    elif rec["inc"]:
                        ins.then_inc(sems[name], 1)
            return run

        with nc.Block() as block:
            block.tensor(replay("pe"))
            block.scalar(replay("act"))
            block.vector(replay("dve"))
            block.gpsimd(replay("pool"))
            block.sync(replay("sp"))


class Ctx:
    pass


def build(S=4096, stages=("H", "A", "B", "C"), dbg=()):
    g = Ctx()
    g.S = S
    g.NST = S // 512
    g.NT = S // 128
    g.stages = stages
    nc = bass.Bass("TRN2", target_bir_lowering=False)
    g.nc = nc
    x_d = nc.dram_tensor("x", [S, D], F32, kind="ExternalInput").ap()
    g.pos_d = nc.dram_tensor("pos", [128, S], I32, kind="ExternalInput").ap()
    win_d = nc.dram_tensor("w_in", [D, 4096], F32, kind="ExternalInput").ap()
    wout_d = nc.dram_tensor("w_out", [D, D], F32, kind="ExternalInput").ap()
    cst_d = nc.dram_tensor("cst", [128, NCST], F32, kind="ExternalInput").ap()
    g.fnw_d = nc.dram_tensor("fnw", [128, D], F32, kind="ExternalInput").ap()
    out_d = nc.dram_tensor("out", [S, D], F32, kind="ExternalOutput").ap()
    part_d = out_d if "C" in stages else nc.dram_tensor("part_scr", [S, D], F32, kind="ExternalOutput").ap()
    g.hnT_d = nc.dram_tensor("hnT_scr", [g.NST, 128, 8 * 512], BF16, kind="Internal").ap()
    g.sg_d = nc.dram_tensor("sg_scr", [4, 128, S], BF16, kind="Internal").ap()
    g.dbg_d = {}
    for name, shape in dbg:
        g.dbg_d[name] = nc.dram_tensor(name, list(shape), F32, kind="ExternalOutput").ap()

    g.x_v = x_d.rearrange("(n j p) d -> n p j d", j=4, p=128)
    g.part_v = part_d.rearrange("(n j p) d -> n p j d", j=4, p=128)
    g.out_v = out_d.rearrange("(n j p) d -> n p j d", j=4, p=128)
    g.win_v = win_d.rearrange("(dc p) c -> dc p c", p=128)
    g.wout_v = wout_d.rearrange("(mc p) c -> mc p c", p=128)

    stack = contextlib.ExitStack()
    with stack:
        S_ = Sched(nc, stack)
        g.S_ = S_

        def sb(name, shape, dt):
            return stack.enter_context(nc.sbuf_tensor(name, list(shape), dt))

        g.CST = CST = sb("CST", [128, NCST], F32)
        g.identb = identb = sb("identb", [128, 128], BF16)
        g.onesb = onesb = sb("onesb", [128, 128], BF16)
        g.m2b = m2b = sb("m2b", [128, 4, 128], BF16)
        g.lbA = lbA = sb("lbA", [128, 4], F32)
        g.lbB = lbB = sb("lbB", [128, 4], F32)
        tmp4 = sb("tmp4", [128, 4], F32)
        tmp4b = sb("tmp4b", [128, 4], F32)

        S_.dma("sp", "cst", W=["CST"], out=CST[:], in_=cst_d)
        S_.op("dve", "tensor_copy", R=["CST"], W=["identb"], out=identb[:], in_=CST[:, C_IDENT:C_IDENT + 128])
        S_.op("dve", "memset", W=["onesb"], ap=onesb[:], constant=1.0)
        for r in range(4):
            S_.op("dve", "tensor_copy", R=["CST"], W=[("m2b", r)], out=m2b[:, r, :], in_=CST[:, C_M2:C_M2 + 128])
        S_.op("dve", "tensor_tensor", R=["CST"], W=["tmp4"], out=tmp4[:], in0=CST[:, C_LB0:C_LB0 + 4], in1=CST[:, C_LB1:C_LB1 + 4], op=ALU.subtract)
        S_.op("act", "activation", R=["tmp4"], W=["tmp4b"], out=tmp4b[:], in_=tmp4[:], func=AF.Tanh, scale=0.5)
        S_.op("dve", "tensor_scalar", R=["tmp4b"], W=["lbA"], out=lbA[:], in0=tmp4b[:], scalar1=-0.25, scalar2=0.25, op0=ALU.mult, op1=ALU.add)
        S_.op("dve", "tensor_scalar", R=["tmp4b"], W=["lbB"], out=lbB[:], in0=tmp4b[:], scalar1=0.25, scalar2=0.75, op0=ALU.mult, op1=ALU.add)
        g.eps_ap = CST[:, C_EPS:C_EPS + 1]

        if "H" in stages:
            _pass_h(g)
        if "A" in stages:
            S_.barrier()
            _pass_ab(g)

        S_.final_wait("sp")
        S_.emit()
    return nc


def _pass_h(g):
    nc = g.nc; S_ = g.S_; NST = g.NST; CST = g.CST; identb = g.identb; onesb = g.onesb; m2b = g.m2b
    lbA = g.lbA; lbB = g.lbB; eps_ap = g.eps_ap

    hs = contextlib.ExitStack()
    with hs:
        def sbh(name, shape, dt):
            return hs.enter_context(nc.sbuf_tensor(name, list(shape), dt))

        def psh(name, shape, dt):
            return hs.enter_context(nc.psum_tensor(name, list(shape), dt))

        WH = sbh("WH", [128, 8, 2048], BF16)
        WOh = sbh("WOh", [128, 4, 1024], BF16)
        xt = [sbh(f"xt{i}", [128, 4, 1024], F32) for i in range(2)]
        junk = sbh("junk", [128, 1024], BF16)
        ss = sbh("ss", [128, 4], F32)
        lnr4 = sbh("lnr4", [128, 4], F32)
        rstd = sbh("rstd", [128, 4], F32)
        hn = sbh("hn", [128, 4, 1024], BF16)
        hnT = sbh("hnT", [128, 8, 512], BF16)
        TH = sbh("TH", [128, 4, 512], F32)
        QS = sbh("QS", [128, 4, 512], F32)
        SG = sbh("SG", [128, 4, 512], BF16)
        Vt = sbh("Vt", [128, 4, 512], BF16)
        KK = sbh("KK", [128, 4, 512], F32)
        CUM = sbh("CUM", [128, 4, 512], F32)
        E2 = sbh("E2", [128, 4, 512], F32)
        qdec = sbh("qdec", [128, 4, 512], BF16)
        kinvb = sbh("kinvb", [128, 4, 512], BF16)
        kend = sbh("kend", [128, 4, 512], BF16)
        kendT = sbh("kendT", [128, 4, 4, 128], BF16)
        attm = sbh("attm", [128, 4, 512], BF16)
        stS = [sbh(f"stS{i}", [128, 8, 128], F32) for i in range(3)]
        stC = sbh("stC", [128, 4, 128], F32)
        stB = [sbh(f"stB{i}", [128, 8, 128], BF16) for i in range(3)]
        SQ = sbh("SQ", [128, 512], BF16)
        LNR = sbh("LNR", [128, 512], F32)
        Rr = sbh("Rr", [128, 512], F32)
        Y1 = sbh("Y1", [128, 512], F32)
        yh = sbh("yh", [128, 4, 512], BF16)

        ptr = psh("ptr", [128, 1024], BF16)
        pp = [psh(f"pp{i}", [128, 512], F32) for i in range(2)]
        pa = psh("pa", [128, 512], F32)
        pkv = [psh(f"pkv{i}", [128, 4, 128], F32) for i in range(2)]
        po = [psh(f"po{i}", [128, 512], F32) for i in range(2)]

        PPT_LIST = [(pp[0], ("pp", 0)), (pp[1], ("pp", 1)), (pa, "pa"), (po[0], ("po", 0)), (po[1], ("po", 1))]
        TRB = [(ptr[:], "ptr"),
               (pkv[0].bitcast(BF16)[:].rearrange("p a b -> p (a b)"), ("pkv", 0)),
               (pkv[1].bitcast(BF16)[:].rearrange("p a b -> p (a b)"), ("pkv", 1)),
               (pp[0].bitcast(BF16)[:], ("pp", 0))]
        TH2 = TH[:].rearrange("p h t -> p (h t)")
        KK2 = KK[:].rearrange("p h t -> p (h t)")
        CUM2 = CUM[:].rearrange("p h t -> p (h t)")
        E22 = E2[:].rearrange("p h t -> p (h t)")
        QS2 = QS[:].rearrange("p h t -> p (h t)")
        THr = [("TH", h) for h in range(4)]
        KKr = [("KK", h) for h in range(4)]
        CUMr = [("CUM", h) for h in range(4)]
        E2r = [("E2", h) for h in range(4)]
        QSr = [("QS", h) for h in range(4)]

        stg = [(CUM2, CUMr), (E22, E2r)]
        for dc in range(8):
            w, wr = stg[dc % 2]
            S_.dma("sp", ("wst", dc % 2), W=wr, out=w, in_=g.win_v[dc][:, 2048:4096])
            S_.op("dve", "tensor_scalar", R=wr + ["CST"], W=[("WH", dc)],
                  out=WH[:, dc, :], in0=w, scalar1=CST[:, C_G + dc:C_G + dc + 1], scalar2=None, op0=ALU.mult)
        for h in range(4):
            w, wr = stg[h % 2]
            S_.dma("sp", ("wst", h % 2), W=wr, out=w[:, 0:1024], in_=g.wout_v[4 + h])
            S_.op("dve", "tensor_copy", R=wr, W=[("WOh", h)], out=WOh[:, h, :], in_=w[:, 0:1024])
        S_.op("pool", "memset", W=[("stC", h) for h in range(4)], ap=stC[:], constant=0.0)

        ppi = [0]
        PPT = PPT_LIST

        def next_pp():
            i = ppi[0] % len(PPT)
            ppi[0] += 1
            return i

        def xr(b, j=None):
            return [("xt", b, jj) for jj in (range(4) if j is None else [j])]

        def load_x(st):
            b = st % 2
            S_.dma("sp", ("xt", b), W=xr(b), out=xt[b][:], in_=g.x_v[st])

        hnTr = [("hnT", j) for j in range(4)]

        def xnorm_pre(st):
            b = st % 2
            for j in range(4):
                S_.op("dve", "scalar_tensor_tensor", R=xr(b, j), W=["junk", ("ss", j)],
                      out=junk[:], in0=xt[b][:, j, :], scalar=1.0, in1=xt[b][:, j, :], op0=ALU.mult, op1=ALU.mult, accum_out=ss[:, j:j + 1])
            S_.op("act", "activation", R=[("ss", j) for j in range(4)] + ["CST"], W=["lnr4"],
                  out=lnr4[:], in_=ss[:], func=AF.Ln, scale=1.0 / D, bias=eps_ap)
            S_.op("act", "activation", R=["lnr4"], W=["rstd"], out=rstd[:], in_=lnr4[:], func=AF.Exp, scale=-0.5)
            for j in range(4):
                S_.op("dve", "tensor_scalar", R=xr(b, j) + ["rstd"], W=[("hn", j)],
                      out=hn[:, j, :], in0=xt[b][:, j, :], scalar1=rstd[:, j:j + 1], scalar2=None, op0=ALU.mult)

        def xnorm_T(st, j):
            tbk, tbr = TRB[j]
            for dc in range(8):
                S_.op("pe", "transpose", R=[("hn", j), "identb"], W=[tbr], acc=True, sig=(dc == 7),
                      out=tbk[:, dc * 128:(dc + 1) * 128], in_=hn[:, j, dc * 128:(dc + 1) * 128], identity=identb[:])
            S_.op("act", "activation", R=[tbr], W=[("hnT", j)],
                  out=hnT[:, :, j * 128:(j + 1) * 128], in_=tbk.rearrange("p (c t) -> p c t", t=128), func=AF.Copy)

        def xnorm_spill(st):
            if "A" in g.stages or "S" in g.stages:
                S_.dma("sp", "hnTo", R=hnTr, W=[("hnT_d", st)], out=g.hnT_d[st], in_=hnT[:].rearrange("p c t -> p (c t)"))

        def proj_fm(col0):
            i = next_pp()
            for dc in range(8):
                S_.op("pe", "matmul", R=[("WH", dc)] + hnTr, W=[PPT[i][1]], acc=True, sig=(dc == 7),
                      out=PPT[i][0][:], lhsT=WH[:, dc, col0:col0 + 128], rhs=hnT[:, dc, :], start=(dc == 0), stop=(dc == 7))
            return i

        def fproj():
            for h in range(4):
                i = proj_fm(512 + h * 128)
                S_.op("act", "activation", R=[PPT[i][1]], W=[("TH", h)], out=TH[:, h, :], in_=PPT[i][0][:], func=AF.Tanh, scale=0.5)
            for h in range(4):
                S_.op("dve", "tensor_scalar", R=[("TH", h), "lbA", "lbB"], W=[("TH", h)],
                      out=TH[:, h, :], in0=TH[:, h, :], scalar1=lbA[:, h:h + 1], scalar2=lbB[:, h:h + 1], op0=ALU.mult, op1=ALU.add)
                S_.op("dve", "tensor_scalar", R=[("TH", h)], W=[("KK", h)], out=KK[:, h, :], in0=TH[:, h, :], scalar1=-1.0, scalar2=1.0, op0=ALU.mult, op1=ALU.add)


        load_x(0)
        if NST > 1:
            load_x(1)
        xnorm_pre(0)
        for j in range(4):
            xnorm_T(0, j)
        xnorm_spill(0)
        for st in range(NST):
            b = st % 2
            nxt = st + 1 < NST
            fproj()
            for h in range(4):
                i = proj_fm(0 + h * 128)
                S_.op("act", "activation", R=[PPT[i][1]], W=[("QS", h)], out=QS[:, h, :], in_=PPT[i][0][:], func=AF.Silu)
            for h in range(4):
                i = proj_fm(1536 + h * 128)
                S_.op("act", "activation", R=[PPT[i][1]], W=[("SG", h)], out=SG[:, h, :], in_=PPT[i][0][:], func=AF.Silu)
            if nxt:
                xnorm_pre(st + 1)
            for h in range(4):
                S_.op("act", "activation", R=[("TH", h)], W=[("TH", h)], out=TH[:, h, :], in_=TH[:, h, :], func=AF.Ln)
                S_.op("dve", "tensor_tensor_scan", R=[("TH", h), "CST"], W=[("CUM", h)],
                      out=CUM[:, h, :], data0=CST[:, C_RST:C_RST + 512], data1=TH[:, h, :], initial=0.0, op0=ALU.mult, op1=ALU.add)
            for j in range(4):
                h = j
                i = next_pp()
                for dc in range(8):
                    S_.op("pe", "matmul", R=[("WH", dc), ("hnT", j)], W=[PPT[i][1]], acc=True, sig=(dc == 7),
                          out=PPT[i][0][:], lhsT=hnT[:, dc, j * 128:(j + 1) * 128], rhs=WH[:, dc, 1024:1536], start=(dc == 0), stop=(dc == 7))
                S_.op("dve", "tensor_copy", R=[PPT[i][1]], W=[("Vt", j)], out=Vt[:, j, :], in_=PPT[i][0][:])
                S_.op("act", "activation", R=[("CUM", h)], W=[("TH", h)], out=TH[:, h, :], in_=CUM[:, h, :], func=AF.Exp)
                S_.op("act", "activation", R=[("CUM", h)], W=[("E2", h)], out=E2[:, h, :], in_=CUM[:, h, :], func=AF.Exp, scale=-1.0)
                S_.op("dve", "tensor_tensor", R=[("QS", h), ("TH", h)], W=[("qdec", h)], out=qdec[:, h, :], in0=QS[:, h, :], in1=TH[:, h, :], op=ALU.mult)
                S_.op("dve", "tensor_tensor", R=[("KK", h), ("E2", h)], W=[("E2", h)], out=E2[:, h, :], in0=KK[:, h, :], in1=E2[:, h, :], op=ALU.mult)
            for h in range(4):
                S_.op("dve", "tensor_copy", R=[("E2", h)], W=[("kinvb", h)], out=kinvb[:, h, :], in_=E2[:, h, :])
                cd_bc = bass.AP(TH, h * 512 + 63, [[2048, 128], [64, 8], [0, 64]])
                S_.op("dve", "tensor_tensor", R=[("E2", h), ("TH", h)], W=[("kend", h)],
                      out=kend[:, h, :].rearrange("p (c t) -> p c t", t=64), in0=E2[:, h, :].rearrange("p (c t) -> p c t", t=64), in1=cd_bc, op=ALU.mult)
            if nxt:
                for j in range(4):
                    xnorm_T(st + 1, j)
                xnorm_spill(st + 1)

            for rnd in range(2):
                for hh in range(2):
                    h = rnd * 2 + hh
                    for j in range(4):
                        col = (hh * 4 + j) * 128
                        S_.op("pe", "transpose", R=[("kend", h), "identb"], W=[TRB[rnd][1]], acc=True, sig=(hh == 1 and j == 3),
                              out=TRB[rnd][0][:, col:col + 128], in_=kend[:, h, j * 128:(j + 1) * 128], identity=identb[:])
                S_.op("act", "activation", R=[TRB[rnd][1]], W=[("kendT", rnd * 2), ("kendT", rnd * 2 + 1)],
                      out=kendT[:, rnd * 2:rnd * 2 + 2, :, :].rearrange("p h j e -> p (h j e)"), in_=TRB[rnd][0], func=AF.Copy)

            def stage1(h):
                sbuf = h % 3
                for j in range(4):
                    S_.op("pe", "matmul", R=[("kinvb", h), ("qdec", h)], W=["pa"], acc=True, sig=(j == 3),
                          out=pa[:, j * 128:(j + 1) * 128], lhsT=kinvb[:, h, j * 128:(j + 1) * 128], rhs=qdec[:, h, j * 128:(j + 1) * 128], start=(j == 0), stop=(j == 3))
                S_.op("dve", "tensor_tensor", R=["pa"] + [("m2b", r) for r in range(4)], W=[("attm", h)],
                      out=attm[:, h, :], in0=pa[:], in1=m2b[:].rearrange("p c t -> p (c t)"), op=ALU.mult)
                S_.op("act", "activation", R=[("stC", h)], W=[("stB", sbuf, 0)], out=stB[sbuf][:, 0, :], in_=stC[:, h, :], func=AF.Copy)
                for c in range(8):
                    j = c // 2
                    r0 = (c % 2) * 64
                    pk = pkv[c % 2]
                    S_.op("pe", "matmul", R=[("kendT", h), ("Vt", j)], W=[("pkv", c % 2)], acc=True, sig=(c >= 6),
                          out=pk[:, c // 2, :], lhsT=kendT[r0:r0 + 64, h, j, :], rhs=Vt[r0:r0 + 64, j, h * 128:(h + 1) * 128], start=True, stop=True)
                for c in range(8):
                    src = stC[:, h, :] if c == 0 else stS[sbuf][:, c, :]
                    srcr = ("stC", h) if c == 0 else ("stS", sbuf, c)
                    dst = stC[:, h, :] if c == 7 else stS[sbuf][:, c + 1, :]
                    dstr = ("stC", h) if c == 7 else ("stS", sbuf, c + 1)
                    S_.op("dve", "scalar_tensor_tensor", R=[("pkv", c % 2), srcr, ("TH", h)], W=[dstr],
                          out=dst, in0=src, scalar=TH[:, h, c * 64 + 63:c * 64 + 64], in1=pkv[c % 2][:, c // 2, :], op0=ALU.mult, op1=ALU.add)
                S_.op("act", "activation", R=[("stS", sbuf, c) for c in range(1, 8)], W=[("stB", sbuf, 1)],
                      out=stB[sbuf][:, 1:8, :], in_=stS[sbuf][:, 1:8, :], func=AF.Copy)

            def stage2(h):
                sbuf = h % 3
                pob = po[h % 2]
                por = ("po", h % 2)
                for c in range(8):
                    j = c // 2
                    if c % 2 == 0:
                        S_.op("pe", "matmul", R=[("Vt", j), ("attm", h)], W=[por], acc=True,
                              out=pob[:, j * 128:(j + 1) * 128], lhsT=Vt[:, j, h * 128:(h + 1) * 128], rhs=attm[:, h, j * 128:(j + 1) * 128], start=(c == 0), stop=False)
                    S_.op("pe", "matmul", R=[("stB", sbuf, 0), ("stB", sbuf, 1), ("qdec", h)], W=[por], acc=True, sig=(c == 7),
                          out=pob[:, c * 64:(c + 1) * 64], lhsT=stB[sbuf][:, c, :], rhs=qdec[:, h, c * 64:(c + 1) * 64], start=False, stop=(c == 7))
                S_.op("act", "activation", R=[por], W=["SQ"], out=SQ[:], in_=pob[:], func=AF.Square)
                S_.op("pe", "matmul", R=["onesb", "SQ"], W=[("pp", 1)], sig=True, out=pp[1][:], lhsT=onesb[:], rhs=SQ[:], start=True, stop=True)
                S_.op("act", "activation", R=[("pp", 1), "CST"], W=["LNR"], out=LNR[:], in_=pp[1][:], func=AF.Ln, scale=1.0 / 128, bias=eps_ap)
                S_.op("act", "activation", R=["LNR"], W=["Rr"], out=Rr[:], in_=LNR[:], func=AF.Exp, scale=-0.5)
                S_.op("dve", "scalar_tensor_tensor", R=[por, "Rr", "CST"], W=["Y1"],
                      out=Y1[:], in0=pob[:], scalar=CST[:, C_HW + h:C_HW + h + 1], in1=Rr[:], op0=ALU.mult, op1=ALU.mult)
                S_.op("dve", "tensor_tensor", R=["Y1", ("SG", h)], W=[("yh", h)], out=yh[:, h, :], in0=Y1[:], in1=SG[:, h, :], op=ALU.mult)

            if "h1" in g.stages:
                pass
            elif "h2" in g.stages:
                for h in range(4):
                    stage1(h)
            else:
                stage1(0)
                stage1(1)
                stage1(2)
                stage2(0)
                stage1(3)
                stage2(1)
                stage2(2)
                stage2(3)

            for j in range(4 if ("h1" not in g.stages and "h2" not in g.stages) else 0):
                for dh in range(2):
                    i = next_pp()
                    for h in range(4):
                        S_.op("pe", "matmul", R=[("yh", h), ("WOh", h)], W=[PPT[i][1]], acc=True, sig=(h == 3),
                              out=PPT[i][0][:], lhsT=yh[:, h, j * 128:(j + 1) * 128], rhs=WOh[:, h, dh * 512:(dh + 1) * 512], start=(h == 0), stop=(h == 3))
                    S_.op("dve", "tensor_tensor", R=[PPT[i][1], ("xt", b, j)], W=[("xt", b, j)],
                          out=xt[b][:, j, dh * 512:(dh + 1) * 512], in0=PPT[i][0][:], in1=xt[b][:, j, dh * 512:(dh + 1) * 512], op=ALU.add)
            S_.dma("sp", ("parto", b), R=xr(b), W=[("out", st)], out=g.part_v[st], in_=xt[b][:])
            if st + 2 < NST:
                load_x(st + 2)


def _pass_ab(g):
    nc = g.nc; S_ = g.S_; NST = g.NST; CST = g.CST; identb = g.identb; onesb = g.onesb; eps_ap = g.eps_ap
    S = g.S
    c1, c2, c3 = _split_2pi()
    qs = contextlib.ExitStack()
    with qs:
        qT = qs.enter_context(nc.sbuf_tensor("qT", [128, 4, S], BF16))
        kT = qs.enter_context(nc.sbuf_tensor("kT", [128, 4, S], BF16))
        vT = qs.enter_context(nc.sbuf_tensor("vT", [128, 4, S], BF16))

        as_ = contextlib.ExitStack()
        with as_:
            def sba(name, shape, dt):
                return as_.enter_context(nc.sbuf_tensor(name, list(shape), dt))

            def psa(name, shape, dt):
                return as_.enter_context(nc.psum_tensor(name, list(shape), dt))

            WA = sba("WA", [128, 8, 2048], BF16)
            wst = [sba(f"wsta{i}", [128, 1024], F32) for i in range(2)]
            permb = sba("permb", [128, 128], BF16)
            hnT = [sba(f"hnTa{i}", [128, 8, 512], BF16) for i in range(2)]
            posi = sba("posi", [128, 512], I32)
            posf = sba("posf", [128, 512], F32)
            ang = sba("ang", [128, 512], F32)
            ki = sba("ki", [128, 512], I32)
            kf = sba("kf", [128, 512], F32)
            r1 = sba("r1", [128, 512], F32)
            r2 = sba("r2", [128, 512], F32)
            aab = sba("aab", [128, 512], F32)
            COSb = [sba(f"COS{i}", [128, 512], F32) for i in range(2)]
            SINb = [sba(f"SIN{i}", [128, 512], F32) for i in range(2)]
            Qb = [sba(f"Qb{i}", [128, 512], BF16) for i in range(2)]
            T1 = [sba(f"T1{i}", [128, 512], F32) for i in range(2)]
            T2 = [sba(f"T2{i}", [128, 512], F32) for i in range(2)]
            sgt = [sba(f"sgt{i}", [128, 512], BF16) for i in range(2)]
            pp = [psa(f"ppa{i}", [128, 512], F32) for i in range(6)]
            pq = [psa(f"pqa{i}", [128, 512], F32) for i in range(2)]

            S_.op("dve", "tensor_copy", R=["CST"], W=["permb"], out=permb[:], in_=CST[:, C_PERM:C_PERM + 128])
            for dc in range(8):
                for hf in range(2):
                    w = wst[hf]
                    S_.dma("sp", ("wsta", hf), W=[("wsta", hf)], out=w[:], in_=g.win_v[dc][:, hf * 1024:(hf + 1) * 1024])
                    S_.op("dve", "tensor_scalar", R=[("wsta", hf), "CST"], W=[("WA", dc)],
                          out=WA[:, dc, hf * 1024:(hf + 1) * 1024], in0=w[:], scalar1=CST[:, C_G + dc:C_G + dc + 1], scalar2=None, op0=ALU.mult)

            ppi = [0]
            tb = [0]

            def load_hnT(st):
                hb_ = st % 2
                S_.dma("sp", ("hnTa", hb_), R=[("hnT_d", st)], W=[("hnTa", hb_)], out=hnT[hb_][:].rearrange("p c t -> p (c t)"), in_=g.hnT_d[st])

            def tables(st):
                tsl_ = slice(st * 512, (st + 1) * 512)
                COS_, SIN_ = COSb[st % 2], SINb[st % 2]
                cn, sn = ("COS", st % 2), ("SIN", st % 2)
                S_.dma("sp", "posi", W=["posi"], out=posi[:], in_=g.pos_d[:, tsl_])
                S_.op("dve", "tensor_copy", R=["posi"], W=["posf"], out=posf[:], in_=posi[:])
                S_.op("dve", "tensor_scalar", R=["posf", "CST"], W=["ang"], out=ang[:], in0=posf[:], scalar1=CST[:, C_INVF:C_INVF + 1], scalar2=None, op0=ALU.mult)
                S_.op("dve", "tensor_scalar", R=["ang"], W=["ki"], out=ki[:], in0=ang[:], scalar1=float(1.0 / TWO_PI), scalar2=None, op0=ALU.mult)
                S_.op("dve", "tensor_copy", R=["ki"], W=["kf"], out=kf[:], in_=ki[:])
                S_.op("dve", "scalar_tensor_tensor", R=["kf", "ang"], W=["r1"], out=r1[:], in0=kf[:], scalar=-c1, in1=ang[:], op0=ALU.mult, op1=ALU.add)
                S_.op("dve", "scalar_tensor_tensor", R=["kf", "r1"], W=["r2"], out=r2[:], in0=kf[:], scalar=-c2, in1=r1[:], op0=ALU.mult, op1=ALU.add)
                S_.op("dve", "scalar_tensor_tensor", R=["kf", "r2"], W=["r1"], out=r1[:], in0=kf[:], scalar=-c3, in1=r2[:], op0=ALU.mult, op1=ALU.add)
                S_.op("dve", "tensor_scalar", R=["r1"], W=["r2"], out=r2[:], in0=r1[:], scalar1=float(np.pi), scalar2=float(-np.pi), op0=ALU.min, op1=ALU.max)
                S_.op("dve", "scalar_tensor_tensor", R=["r2"], W=["aab"], out=aab[:], in0=r2[:], scalar=-1.0, in1=r2[:], op0=ALU.mult, op1=ALU.max)
                S_.op("act", "activation", R=["aab", "CST"], W=[cn], out=COS_[:], in_=aab[:], func=AF.Sin, scale=-1.0, bias=CST[:, C_HPI:C_HPI + 1])
                S_.op("act", "activation", R=["r2", "CST"], W=[sn], out=SIN_[:], in_=r2[:], func=AF.Sin, scale=CST[:, C_SGN:C_SGN + 1])

            load_hnT(0)
            tables(0)
            for st in range(NST if "a1" not in g.stages else 0):
                hb = st % 2
                tsl = slice(st * 512, (st + 1) * 512)
                if st + 1 < NST:
                    load_hnT(st + 1)
                COS, SIN = COSb[st % 2], SINb[st % 2]
                cosn, sinn = ("COS", st % 2), ("SIN", st % 2)

                hnTr = [("hnTa", hb)]
                if "a2" in g.stages:
                    continue

                def proj(col0):
                    i = ppi[0] % 6
                    ppi[0] += 1
                    for dc in range(8):
                        S_.op("pe", "matmul", R=[("WA", dc)] + hnTr, W=[("ppa", i)], acc=True, sig=(dc == 7),
                              out=pp[i][:], lhsT=WA[:, dc, col0:col0 + 128], rhs=hnT[hb][:, dc, :], start=(dc == 0), stop=(dc == 7))
                    return i

                pendq = []

                def rot(i_, t_, dst_, dname_, c_):
                    S_.op("pe", "matmul", R=["permb", ("Qb", t_)], W=[("pqa", t_)], sig=True,
                          out=pq[t_][:], lhsT=permb[:], rhs=Qb[t_][:], start=True, stop=True)
                    S_.op("dve", "tensor_tensor", R=[("Qb", t_), cosn], W=[("T1", t_)], out=T1[t_][:], in0=Qb[t_][:], in1=COS[:], op=ALU.mult)
                    S_.op("dve", "tensor_tensor", R=[("pqa", t_), sinn], W=[("T2", t_)], out=T2[t_][:], in0=pq[t_][:], in1=SIN[:], op=ALU.mult)
                    S_.op("pool", "tensor_tensor", R=[("T1", t_), ("T2", t_)], W=[(dname_, c_, st)], out=dst_[:, c_, tsl], in0=T1[t_][:], in1=T2[t_][:], op=ALU.add)

                for which, dst in ((0, qT), (1, kT)):
                    dname = "qT" if which == 0 else "kT"
                    for c in range(4):
                        i = proj(which * 512 + c * 128)
                        t = tb[0] % 2
                        tb[0] += 1
                        S_.op("act", "activation", R=[("ppa", i)], W=[("Qb", t)], out=Qb[t][:], in_=pp[i][:], func=AF.Copy)
                        if pendq:
                            rot(*pendq.pop())
                        pendq.append((i, t, dst, dname, c))
                if st + 1 < NST:
                    tables(st + 1)
                for c in range(4):
                    i = proj(1024 + c * 128)
                    S_.op("act", "activation", R=[("ppa", i)], W=[("vT", c, st)], out=vT[:, c, tsl], in_=pp[i][:], func=AF.Copy)
                    if pendq:
                        rot(*pendq.pop())
                for c in range(4 if "a5" not in g.stages else 0):
                    i = proj(1536 + c * 128)
                    t = (st * 4 + c) % 2
                    S_.op("act", "activation", R=[("ppa", i)], W=[("sgt", t)], out=sgt[t][:], in_=pp[i][:], func=AF.Silu)
                    S_.dma("sp", ("sgo", t), R=[("sgt", t)], W=[("sg_d", c, st)], out=g.sg_d[c][:, tsl], in_=sgt[t][:])

        if "B" not in g.stages:
            return
        S_.barrier()
        bs = contextlib.ExitStack()
        with bs:
            def sbb(name, shape, dt):
                return bs.enter_context(nc.sbuf_tensor(name, list(shape), dt))

            def psb_(name, shape, dt):
                return bs.enter_context(nc.psum_tensor(name, list(shape), dt))

            NVW, NPT, PVLAG = 5, 5, 2
            acc = sbb("accXY", [128, 2, S], F32)
            maskb = sbb("maskb", [128, 2, 2, 256], BF16)
            VW = [sbb(f"VW{i}", [128, 2, 256], BF16) for i in range(NVW)]
            PT = [sbb(f"PT{i}", [128, 2, 512], BF16) for i in range(NPT)]
            scx = sbb("scx", [128, 2], F32)
            SQX = sbb("SQX", [128, 512], BF16)
            SQY = sbb("SQY", [128, 512], BF16)
            LNR = sbb("LNRb", [128, 512], F32)
            Rr = sbb("Rrb", [128, 512], F32)
            Y1 = sbb("Y1b", [128, 512], F32)
            sgl = [sbb(f"sgl{i}", [128, 512], BF16) for i in range(2)]

            ptv = psb_("ptv", [128, 512], BF16)
            psc = [psb_(f"psc{i}", [128, 2, 512], F32) for i in range(2)]
            pxy = [psb_(f"pxy{i}", [128, 2, 256], F32) for i in range(2)]
            pU = psb_("pU", [128, 512], F32)
            PTV = [(ptv[:, 0:256], "ptv"), (pU.bitcast(BF16)[:, 0:256], "pU")]

            for hh in range(2):
                for kb in range(2):
                    S_.op("dve", "tensor_copy", R=["CST"], W=["maskb"], out=maskb[:, hh, kb, :], in_=CST[:, C_AMASK:C_AMASK + 256])
            for i in range(NVW):
                S_.op("pool", "memset", W=[("VW", i)], ap=VW[i][:], constant=1.0)
            se = float(np.sqrt(EPS))
            S_.op("pool", "memset", W=["scx"], ap=scx[0:64, 0:1], constant=1.0)
            S_.op("pool", "memset", W=["scx"], ap=scx[64:128, 0:1], constant=se)
            S_.op("pool", "memset", W=["scx"], ap=scx[0:64, 1:2], constant=se)
            S_.op("pool", "memset", W=["scx"], ap=scx[64:128, 1:2], constant=1.0)

            vwi = [0]
            pti = [0]
            gi = [0]
            SC = 1.0 / 8.0

            for c in range(4):
                first_pat = True
                pend = []
                for d in PATTERNS:
                    sub_len = S // d
                    n_blk = sub_len // 128
                    for r in range(d):
                        def tok(j, nb=1, r=r, d=d):
                            s0 = r + d * 128 * j
                            return slice(s0, s0 + d * (128 * nb - 1) + 1, d) if d > 1 else slice(s0, s0 + 128 * nb)

                        grp = {}

                        def emit_pv(j0, nkb, pbuf, vws, grp=grp, tok=tok, n_blk=n_blk, first_pat=first_pat):
                            for kb in range(nkb):
                                j = j0 + kb
                                gq = j // 2
                                if j % 2 == 0:
                                    segs = [(gq, 0, 0, min(2, n_blk - j))]
                                else:
                                    segs = [(gq, 1, 0, 1)]
                                    if j + 1 < n_blk:
                                        segs.append((gq + 1, 0, 1, 1))
                                for (gg, qpos, poff_blk, nb) in segs:
                                    if gg not in grp:
                                        tot = 2 * ((1 if gg > 0 else 0) + 1 + (1 if 2 * gg + 1 < n_blk else 0))
                                        grp[gg] = [gi[0] % 2, 0, tot]
                                        gi[0] += 1
                                    for hh in range(2):
                                        bank, cnt, tot = grp[gg]
                                        qoff = qpos * 128
                                        poff = kb * 256 + poff_blk * 128
                                        S_.op("pe", "matmul", R=[("VW", vws[kb]), ("PT", pbuf)], W=[("pxy", bank)], acc=True, sig=(cnt == tot - 1),
                                              out=pxy[bank][:, hh, qoff:qoff + nb * 128], lhsT=VW[vws[kb]][:, kb, hh * 128:(hh + 1) * 128],
                                              rhs=PT[pbuf][:, hh, poff:poff + nb * 128], start=(cnt == 0), stop=(cnt == tot - 1), skip_group_check=True)
                                        grp[gg][1] += 1
                                        if grp[gg][1] == tot:
                                            nqb = min(2, n_blk - 2 * gg)
                                            dstap = acc[:, :, tok(2 * gg, nqb)]
                                            if first_pat:
                                                S_.op("dve", "tensor_copy", R=[("pxy", bank)], W=[("acc", c)], out=dstap, in_=pxy[bank][:, :, 0:nqb * 128])
                                            else:
                                                S_.op("dve", "tensor_tensor", R=[("pxy", bank), ("acc", c)], W=[("acc", c)], out=dstap, in0=pxy[bank][:, :, 0:nqb * 128], in1=dstap, op=ALU.add)

                        for j0 in range(0, n_blk, 2):
                            nkb = min(2, n_blk - j0)
                            sbuf = (j0 // 2) % 2
                            pbuf = pti[0] % NPT
                            pti[0] += 1
                            vws = []
                            nqs = []
                            for kb in range(nkb):
                                j = j0 + kb
                                nq = 2 if j + 1 < n_blk else 1
                                nqs.append(nq)
                                for hh in range(2):
                                    S_.op("pe", "matmul", R=[("kT", c, s_) for s_ in range(NST)] + [("qT", c, s_) for s_ in range(NST)], W=[("psc", sbuf)], acc=True, sig=(kb == nkb - 1 and hh == 1),
                                          out=psc[sbuf][:, hh, kb * 256:kb * 256 + nq * 128], lhsT=kT[hh * 64:(hh + 1) * 64, c, tok(j)],
                                          rhs=qT[hh * 64:(hh + 1) * 64, c, tok(j, nq)], start=True, stop=True)
                            tvh = pti[0] % 2
                            for kb in range(nkb):
                                j = j0 + kb
                                S_.op("pe", "transpose", R=[("vT", c, s_) for s_ in range(NST)] + ["identb"], W=[PTV[tvh][1]], acc=True, sig=(kb == nkb - 1),
                                      out=PTV[tvh][0][:, kb * 128:(kb + 1) * 128], in_=vT[:, c, tok(j)], identity=identb[:])
                            ncol = (nkb - 1) * 256 + nqs[-1] * 128
                            S_.op("act", "activation", R=[("psc", sbuf)], W=[("PT", pbuf)], out=PT[pbuf][:, :, 0:ncol], in_=psc[sbuf][:, :, 0:ncol], func=AF.Exp, scale=SC)
                            meng = "dve"
                            S_.op(meng, "tensor_tensor", R=[("PT", pbuf), "maskb"], W=[("PT", pbuf)], out=PT[pbuf][:, :, 0:ncol], in0=PT[pbuf][:, :, 0:ncol],
                                  in1=maskb[:].rearrange("p h k q -> p h (k q)")[:, :, 0:ncol], op=ALU.mult)
                            v = vwi[0] % NVW
                            vwi[0] += 1
                            vws = [v] * nkb
                            dst = bass.AP(VW[v], 0, [[512, 128], [256, nkb], [192, 2], [1, 64]])
                            S_.op("act", "activation", R=[PTV[tvh][1]], W=[("VW", v)], out=dst,
                                  in_=PTV[tvh][0][:, 0:nkb * 128].rearrange("p (k h e) -> p k h e", h=2, e=64), func=AF.Copy)
                            pend.append((emit_pv, (j0, nkb, pbuf, vws)))
                            if len(pend) > PVLAG:
                                f_, a_ = pend.pop(0)
                                f_(*a_)
                    first_pat = False
                while pend:
                    f_, a_ = pend.pop(0)
                    f_(*a_)

                for st in range(NST):
                    tsl = slice(st * 512, (st + 1) * 512)
                    t = st % 2
                    S_.dma("sp", ("sgl", t), R=[("sg_d", c, st)], W=[("sgl", t)], out=sgl[t][:], in_=g.sg_d[c][:, tsl])
                    S_.op("act", "activation", R=[("acc", c), "scx"], W=["SQX"], out=SQX[:], in_=acc[:, 0, tsl], func=AF.Square, scale=scx[:, 0:1])
                    S_.op("act", "activation", R=[("acc", c), "scx"], W=["SQY"], out=SQY[:], in_=acc[:, 1, tsl], func=AF.Square, scale=scx[:, 1:2])
                    S_.op("pe", "matmul", R=["onesb", "SQX"], W=["pU"], out=pU[0:64, :], lhsT=onesb[:, 0:64], rhs=SQX[:], start=True, stop=True)
                    S_.op("pe", "matmul", R=["onesb", "SQY"], W=["pU"], sig=True, out=pU[64:128, :], lhsT=onesb[:, 0:64], rhs=SQY[:], start=True, stop=True)
                    S_.op("act", "activation", R=["pU"], W=["LNRb"], out=LNR[:], in_=pU[:], func=AF.Ln, scale=1.0 / 64)
                    S_.op("act", "activation", R=["LNRb"], W=["Rrb"], out=Rr[:], in_=LNR[:], func=AF.Exp, scale=-0.5)
                    S_.op("dve", "scalar_tensor_tensor", R=[("acc", c), "Rrb", "CST"], W=["Y1b"],
                          out=Y1[0:64, :], in0=acc[0:64, 0, tsl], scalar=CST[0:64, C_AW + c:C_AW + c + 1], in1=Rr[0:64, :], op0=ALU.mult, op1=ALU.mult)
                    S_.op("dve", "scalar_tensor_tensor", R=[("acc", c), "Rrb", "CST"], W=["Y1b"],
                          out=Y1[64:128, :], in0=acc[64:128, 1, tsl], scalar=CST[64:128, C_AW + c:C_AW + c + 1], in1=Rr[64:128, :], op0=ALU.mult, op1=ALU.mult)
                    S_.op("pool", "tensor_tensor", R=["Y1b", ("sgl", t)] , W=[("qT", c, st)], out=qT[:, c, tsl], in0=Y1[:], in1=sgl[t][:], op=ALU.mult)

        if "C" in g.stages:
            S_.barrier()
            _pass_c_inner(g, qT)


def _pass_c_inner(g, qT):
    nc = g.nc; S_ = g.S_; NST = g.NST; CST = g.CST; eps_ap = g.eps_ap
    cs = contextlib.ExitStack()
    with cs:
        def sbc(name, shape, dt):
            return cs.enter_context(nc.sbuf_tensor(name, list(shape), dt))

        def psc_(name, shape, dt):
            return cs.enter_context(nc.psum_tensor(name, list(shape), dt))

        WOa = sbc("WOa", [128, 4, 1024], BF16)
        wst = [sbc(f"wstc{i}", [128, 1024], F32) for i in range(2)]
        fnw = sbc("fnw_sb", [128, D], F32)
        zt = [sbc(f"zt{i}", [128, 4, 1024], F32) for i in range(2)]
        junk = sbc("junkc", [128, 1024], BF16)
        ss = sbc("ssc", [128, 4], F32)
        lnr4 = sbc("lnr4c", [128, 4], F32)
        rstd = sbc("rstdc", [128, 4], F32)
        pc = [psc_(f"pc{i}", [128, 512], F32) for i in range(8)]

        S_.dma("sp", "fnw", W=["fnw"], out=fnw[:], in_=g.fnw_d)
        for c in range(4):
            w = wst[c % 2]
            S_.dma("sp", ("wstc", c % 2), W=[("wstc", c % 2)], out=w[:], in_=g.wout_v[c])
            S_.op("dve", "tensor_copy", R=[("wstc", c % 2)], W=[("WOa", c)], out=WOa[:, c, :], in_=w[:])

        def load_part(st):
            b = st % 2
            S_.dma("sp", ("zt", b), R=[("out", st)], W=[("zt", b, j) for j in range(4)], out=zt[b][:], in_=g.part_v[st])

        pci = [0]
        load_part(0)
        for st in range(NST):
            b = st % 2
            if st + 1 < NST:
                load_part(st + 1)
            for j in range(4):
                tok = slice(st * 512 + j * 128, st * 512 + (j + 1) * 128)
                for dh in range(2):
                    i = pci[0] % 8
                    pci[0] += 1
                    for c in range(4):
                        S_.op("pe", "matmul", R=[("qT", c, st), ("WOa", c)], W=[("pc", i)], acc=True, sig=(c == 3),
                              out=pc[i][:], lhsT=qT[:, c, tok], rhs=WOa[:, c, dh * 512:(dh + 1) * 512], start=(c == 0), stop=(c == 3))
                    S_.op("dve", "tensor_tensor", R=[("pc", i), ("zt", b, j)], W=[("zt", b, j)],
                          out=zt[b][:, j, dh * 512:(dh + 1) * 512], in0=pc[i][:], in1=zt[b][:, j, dh * 512:(dh + 1) * 512], op=ALU.add)
                S_.op("act", "activation", R=[("zt", b, j)], W=["junkc", ("ssc", j)],
                      out=junk[:], in_=zt[b][:, j, :], func=AF.Square, accum_out=ss[:, j:j + 1])
            S_.op("act", "activation", R=[("ssc", j) for j in range(4)] + ["CST"], W=["lnr4c"], out=lnr4[:], in_=ss[:], func=AF.Ln, scale=1.0 / D, bias=eps_ap)
            S_.op("act", "activation", R=["lnr4c"], W=["rstdc"], out=rstd[:], in_=lnr4[:], func=AF.Exp, scale=-0.5)
            for j in range(4):
                S_.op("dve", "scalar_tensor_tensor", R=[("zt", b, j), "rstdc", "fnw"], W=[("zt", b, j)],
                      out=zt[b][:, j, :], in0=zt[b][:, j, :], scalar=rstd[:, j:j + 1], in1=fnw[:], op0=ALU.mult, op1=ALU.mult)
            S_.dma("sp", ("outo", b), R=[("zt", b, j) for j in range(4)], W=[("out", st)], out=g.out_v[st], in_=zt[b][:])


def prep_core_inputs(inputs, b, S=4096):
    cst = _const_table()
    cst[:, C_G:C_G + 8] = np.asarray(inputs["mix_norm_w"], np.float32)[0].reshape(8, 128).T
    cst[:, C_AW:C_AW + 4] = np.asarray(inputs["attn_out_norm_w"], np.float32)[0].reshape(4, 128).T
    cst[:, C_HW:C_HW + 4] = np.asarray(inputs["hgrn_out_norm_w"], np.float32)[0].reshape(4, 128).T
    lb = np.asarray(inputs["hgrn_lb_raw"], np.float32)
    cst[:, C_LB0:C_LB0 + 4] = lb[0].reshape(4, 128).T
    cst[:, C_LB1:C_LB1 + 4] = lb[1].reshape(4, 128).T
    pos = np.asarray(inputs["positions"])[b, :S].astype(np.int32)
    return {
        "x": np.ascontiguousarray(np.asarray(inputs["x"], np.float32)[b, :S]),
        "pos": np.ascontiguousarray(np.broadcast_to(pos[None, :], (128, S))),
        "w_in": np.ascontiguousarray(np.asarray(inputs["w_in"], np.float32)[0]),
        "w_out": np.ascontiguousarray(np.asarray(inputs["w_out"], np.float32)[0]),
        "cst": cst,
        "fnw": np.ascontiguousarray(np.broadcast_to(np.asarray(inputs["final_norm_w"], np.float32)[None, :], (128, D))),
    }


_NC_CACHE = {}


def kernel(x, positions, w_in, w_out, mix_norm_w, attn_out_norm_w, hgrn_out_norm_w, hgrn_lb_raw, final_norm_w):
    inputs = dict(x=x, positions=positions, w_in=w_in, w_out=w_out, mix_norm_w=mix_norm_w,
                  attn_out_norm_w=attn_out_norm_w, hgrn_out_norm_w=hgrn_out_norm_w,
                  hgrn_lb_raw=hgrn_lb_raw, final_norm_w=final_norm_w)
    B, S, _ = np.asarray(x).shape
    if S not in _NC_CACHE:
        _NC_CACHE[S] = build(S)
    nc = _NC_CACHE[S]
    in_maps = [prep_core_inputs(inputs, b, S) for b in range(B)]
    res = run_bass_kernel_spmd(nc, in_maps, core_ids=list(range(B)))
    return np.stack([np.asarray(r["out"], np.float32) for r in res.results], axis=0)
```

```python
import contextlib
import numpy as np
import concourse.bass as bass
import concourse.mybir as mybir
from concourse.bass_utils import run_bass_kernel_spmd

F32 = mybir.dt.float32
BF16 = mybir.dt.bfloat16
I32 = mybir.dt.int32
AF = mybir.ActivationFunctionType
ALU = mybir.AluOpType

D = 1024
EPS = 1e-6
ROPE_THETA = 500000.0
PATTERNS = (1, 4, 16)
TWO_PI = 2.0 * np.pi


def _split_2pi():
    c1 = 6.28125
    r = TWO_PI - c1
    c2 = np.float32(r)
    c2 = (c2.view(np.uint32) & np.uint32(0xFFFFF000)).view(np.float32)
    c3 = np.float32(r - float(c2))
    return float(c1), float(c2), float(c3)


C_G = 0
C_AW = 8
C_HW = 12
C_LB0 = 16
C_LB1 = 20
C_INVF = 24
C_SGN = 25
C_HPI = 26
C_EPS = 27
C_ONE = 28
C_IDENT = 32
C_PERM = 160
C_AMASK = 288
C_M2 = 544
C_RST = 672
NCST = 1184


def _const_table():
    c = np.zeros((128, NCST), np.float32)
    p = np.arange(128)
    e = p % 64
    half = 8
    invf = ROPE_THETA ** (-np.arange(half, dtype=np.float64) * (2.0 / 16.0))
    c[:, C_INVF] = np.where(e < 16, invf[e % 8], 0.0)
    c[:, C_SGN] = np.where(e < 8, -1.0, np.where(e < 16, 1.0, 0.0))
    c[:, C_HPI] = np.pi / 2
    c[:, C_EPS] = EPS
    c[:, C_ONE] = 1.0
    c[:, C_IDENT:C_IDENT + 128] = np.eye(128)
    perm = np.zeros((128, 128), np.float32)
    for m in range(128):
        em = m % 64
        if em < 8:
            perm[m + 8, m] = 1.0
        elif em < 16:
            perm[m - 8, m] = 1.0
    c[:, C_PERM:C_PERM + 128] = perm
    k = np.arange(128)[:, None]
    q = np.arange(128)[None, :]
    am = np.concatenate([(q >= k), (q <= k)], axis=1).astype(np.float32)
    c[:, C_AMASK:C_AMASK + 256] = am
    s = np.arange(128)[:, None]
    t = np.arange(128)[None, :]
    c[:, C_M2:C_M2 + 128] = ((s // 64 == t // 64) & (s <= t)).astype(np.float32)
    rst = np.ones(512, np.float32)
    rst[0::64] = 0.0
    c[:, C_RST:C_RST + 512] = rst[None, :]
    return c


class Sched:
    COMPUTE = ("pe", "act", "dve", "pool")

    def __init__(self, nc, stack):
        self.nc = nc
        self.stack = stack
        self.streams = {k: [] for k in ("pe", "act", "dve", "pool", "sp")}
        self.sems = {}
        self.cnt = {}
        self.sigs = {k: [] for k in self.COMPUTE}
        self.waited = {k: {} for k in self.streams}
        self.last_w = {}
        self.readers = {}
        for k in self.COMPUTE:
            self._sem(k)

    def _sem(self, name):
        if name not in self.sems:
            self.sems[name] = self.stack.enter_context(self.nc.semaphore("s_" + "".join(ch if ch.isalnum() else "_" for ch in str(name))))
            self.cnt[name] = 0
        return self.sems[name]

    def _resolve(self, ev):
        kind = ev[0]
        if kind == "dma":
            return ev[1], ev[2]
        eng, seq = ev[1], ev[2]
        best = None
        for s, v in reversed(self.sigs[eng]):
            if s >= seq:
                best = v
            else:
                break
        if best is not None:
            return eng, best
        rec = self.streams[eng][-1]
        assert rec["seq"] >= seq
        self.cnt[eng] += 1
        rec["inc"] = True
        self.sigs[eng].append((rec["seq"], self.cnt[eng]))
        return eng, self.cnt[eng]

    def _collect(self, eng, reads, writes, acc):
        evs = []
        for r in reads:
            ev = self.last_w.get(r)
            if ev is not None:
                evs.append(ev)
        for w in writes:
            ev = self.last_w.get(w)
            if ev is not None and not (acc and ev[0] == "c" and ev[1] == eng):
                evs.append(ev)
            for ev in self.readers.get(w, ()):
                evs.append(ev)
        waits = {}
        for ev in evs:
            name, val = self._resolve(ev)
            if self.waited[eng].get(name, 0) >= val:
                continue
            waits[name] = max(waits.get(name, 0), val)
        for name, val in waits.items():
            self.waited[eng][name] = val
        return list(waits.items())

    def _commit(self, ev, reads, writes):
        for r in reads:
            self.readers.setdefault(r, []).append(ev)
        for w in writes:
            self.last_w[w] = ev
            self.readers[w] = []

    def op(self, eng, method, R=(), W=(), sig=None, acc=False, **kw):
        if sig is None:
            sig = (eng != "pe")
        reads, writes = list(R), list(W)
        fn = (lambda e, method=method, kw=kw: getattr(e, method)(**kw))
        waits = self._collect(eng, reads, writes, acc)
        seq = len(self.streams[eng])
        rec = {"fn": fn, "waits": waits, "inc": False, "seq": seq, "dma": None}
        self.streams[eng].append(rec)
        if sig:
            self.cnt[eng] += 1
            rec["inc"] = True
            self.sigs[eng].append((seq, self.cnt[eng]))
        self._commit(("c", eng, seq), reads, writes)

    def dma(self, q, key, R=(), W=(), **kw):
        reads, writes = list(R), list(W)
        fn = (lambda e, kw=kw: e.dma_start(**kw))
        name = ("dma", key)
        self._sem(name)
        waits = self._collect(q, reads, writes, False)
        self.cnt[name] += 16
        rec = {"fn": fn, "waits": waits, "inc": False, "seq": len(self.streams[q]), "dma": name}
        self.streams[q].append(rec)
        self._commit(("dma", name, self.cnt[name]), reads, writes)

    def barrier(self):
        for eng in self.COMPUTE:
            if self.streams[eng]:
                rec = self.streams[eng][-1]
                if not rec["inc"]:
                    self.cnt[eng] += 1
                    rec["inc"] = True
                    self.sigs[eng].append((rec["seq"], self.cnt[eng]))
        for eng in self.streams:
            waits = []
            for name, val in self.cnt.items():
                if val <= 0 or name == eng:
                    continue
                if self.waited[eng].get(name, 0) >= val:
                    continue
                self.waited[eng][name] = val
                waits.append((name, val))
            self.streams[eng].append({"fn": None, "waits": waits, "inc": False, "seq": len(self.streams[eng]), "dma": None})

    def final_wait(self, q):
        waits = {}
        for name, val in self.cnt.items():
            if isinstance(name, tuple) and name[0] == "dma" and val > 0:
                waits[name] = val
        rec = {"fn": None, "waits": list(waits.items()), "inc": False, "seq": len(self.streams[q]), "dma": None}
        self.streams[q].append(rec)

    def emit(self):
        nc = self.nc
        sems = self.sems

        def replay(name):
            def run(e):
                for rec in self.streams[name]:
                    for sname, val in rec["waits"]:
                        e.wait_ge(sems[sname], val)
                    if rec["fn"] is None:
                        continue
                    ins = rec["fn"](e)
                    if rec["dma"] is not None:
                        ins.then_inc(sems[rec["dma"]], 16)
                    elif rec["inc"]:
                        ins.then_inc(sems[name], 1)
            return run

        with nc.Block() as block:
            block.tensor(replay("pe"))
            block.scalar(replay("act"))
            block.vector(replay("dve"))
            block.gpsimd(replay("pool"))
            block.sync(replay("sp"))


class Ctx:
    pass


def build(S=4096, stages=("H", "A", "B", "C"), dbg=()):
    g = Ctx()
    g.S = S
    g.NST = S // 512
    g.NT = S // 128
    g.stages = stages
    nc = bass.Bass("TRN2", target_bir_lowering=False)
    g.nc = nc
    x_d = nc.dram_tensor("x", [S, D], F32, kind="ExternalInput").ap()
    g.pos_d = nc.dram_tensor("pos", [128, S], I32, kind="ExternalInput").ap()
    win_d = nc.dram_tensor("w_in", [D, 4096], F32, kind="ExternalInput").ap()
    wout_d = nc.dram_tensor("w_out", [D, D], F32, kind="ExternalInput").ap()
    cst_d = nc.dram_tensor("cst", [128, NCST], F32, kind="ExternalInput").ap()
    g.fnw_d = nc.dram_tensor("fnw", [128, D], F32, kind="ExternalInput").ap()
    out_d = nc.dram_tensor("out", [S, D], F32, kind="ExternalOutput").ap()
    part_d = out_d if "C" in stages else nc.dram_tensor("part_scr", [S, D], F32, kind="ExternalOutput").ap()
    g.hnT_d = nc.dram_tensor("hnT_scr", [g.NST, 128, 8 * 512], BF16, kind="Internal").ap()
    g.sg_d = nc.dram_tensor("sg_scr", [4, 128, S], BF16, kind="Internal").ap()
    g.dbg_d = {}
    for name, shape in dbg:
        g.dbg_d[name] = nc.dram_tensor(name, list(shape), F32, kind="ExternalOutput").ap()

    g.x_v = x_d.rearrange("(n j p) d -> n p j d", j=4, p=128)
    g.part_v = part_d.rearrange("(n j p) d -> n p j d", j=4, p=128)
    g.out_v = out_d.rearrange("(n j p) d -> n p j d", j=4, p=128)
    g.win_v = win_d.rearrange("(dc p) c -> dc p c", p=128)
    g.wout_v = wout_d.rearrange("(mc p) c -> mc p c", p=128)

    stack = contextlib.ExitStack()
    with stack:
        S_ = Sched(nc, stack)
        g.S_ = S_

        def sb(name, shape, dt):
            return stack.enter_context(nc.sbuf_tensor(name, list(shape), dt))

        g.CST = CST = sb("CST", [128, NCST], F32)
        g.identb = identb = sb("identb", [128, 128], BF16)
        g.onesb = onesb = sb("onesb", [128, 128], BF16)
        g.m2b = m2b = sb("m2b", [128, 4, 128], BF16)
        g.lbA = lbA = sb("lbA", [128, 4], F32)
        g.lbB = lbB = sb("lbB", [128, 4], F32)
        tmp4 = sb("tmp4", [128, 4], F32)
        tmp4b = sb("tmp4b", [128, 4], F32)

        S_.dma("sp", "cst", W=["CST"], out=CST[:], in_=cst_d)
        S_.op("dve", "tensor_copy", R=["CST"], W=["identb"], out=identb[:], in_=CST[:, C_IDENT:C_IDENT + 128])
        S_.op("dve", "memset", W=["onesb"], ap=onesb[:], constant=1.0)
        for r in range(4):
            S_.op("dve", "tensor_copy", R=["CST"], W=[("m2b", r)], out=m2b[:, r, :], in_=CST[:, C_M2:C_M2 + 128])
        S_.op("dve", "tensor_tensor", R=["CST"], W=["tmp4"], out=tmp4[:], in0=CST[:, C_LB0:C_LB0 + 4], in1=CST[:, C_LB1:C_LB1 + 4], op=ALU.subtract)
        S_.op("act", "activation", R=["tmp4"], W=["tmp4b"], out=tmp4b[:], in_=tmp4[:], func=AF.Tanh, scale=0.5)
        S_.op("dve", "tensor_scalar", R=["tmp4b"], W=["lbA"], out=lbA[:], in0=tmp4b[:], scalar1=-0.25, scalar2=0.25, op0=ALU.mult, op1=ALU.add)
        S_.op("dve", "tensor_scalar", R=["tmp4b"], W=["lbB"], out=lbB[:], in0=tmp4b[:], scalar1=0.25, scalar2=0.75, op0=ALU.mult, op1=ALU.add)
        g.eps_ap = CST[:, C_EPS:C_EPS + 1]

        if "H" in stages:
            _pass_h(g)
        if "A" in stages:
            S_.barrier()
            _pass_ab(g)

        S_.final_wait("sp")
        S_.emit()
    return nc


def _pass_h(g):
    nc = g.nc; S_ = g.S_; NST = g.NST; CST = g.CST; identb = g.identb; onesb = g.onesb; m2b = g.m2b
    lbA = g.lbA; lbB = g.lbB; eps_ap = g.eps_ap

    hs = contextlib.ExitStack()
    with hs:
        def sbh(name, shape, dt):
            return hs.enter_context(nc.sbuf_tensor(name, list(shape), dt))

        def psh(name, shape, dt):
            return hs.enter_context(nc.psum_tensor(name, list(shape), dt))

        WH = sbh("WH", [128, 8, 2048], BF16)
        WOh = sbh("WOh", [128, 4, 1024], BF16)
        xt = [sbh(f"xt{i}", [128, 4, 1024], F32) for i in range(2)]
        junk = sbh("junk", [128, 1024], BF16)
        ss = sbh("ss", [128, 4], F32)
        lnr4 = sbh("lnr4", [128, 4], F32)
        rstd = sbh("rstd", [128, 4], F32)
        hn = sbh("hn", [128, 4, 1024], BF16)
        hnT = sbh("hnT", [128, 8, 512], BF16)
        TH = sbh("TH", [128, 4, 512], F32)
        QS = sbh("QS", [128, 4, 512], F32)
        SG = sbh("SG", [128, 4, 512], BF16)
        Vt = sbh("Vt", [128, 4, 512], BF16)
        KK = sbh("KK", [128, 4, 512], F32)
        CUM = sbh("CUM", [128, 4, 512], F32)
        E2 = sbh("E2", [128, 4, 512], F32)
        qdec = sbh("qdec", [128, 4, 512], BF16)
        kinvb = sbh("kinvb", [128, 4, 512], BF16)
        kend = sbh("kend", [128, 4, 512], BF16)
        kendT = sbh("kendT", [128, 4, 4, 128], BF16)
        attm = sbh("attm", [128, 4, 512], BF16)
        stS = [sbh(f"stS{i}", [128, 8, 128], F32) for i in range(3)]
        stC = sbh("stC", [128, 4, 128], F32)
        stB = [sbh(f"stB{i}", [128, 8, 128], BF16) for i in range(3)]
        SQ = sbh("SQ", [128, 512], BF16)
        LNR = sbh("LNR", [128, 512], F32)
        Rr = sbh("Rr", [128, 512], F32)
        Y1 = sbh("Y1", [128, 512], F32)
        yh = sbh("yh", [128, 4, 512], BF16)

        ptr = psh("ptr", [128, 1024], BF16)
        pp = [psh(f"pp{i}", [128, 512], F32) for i in range(2)]
        pa = psh("pa", [128, 512], F32)
        pkv = [psh(f"pkv{i}", [128, 4, 128], F32) for i in range(2)]
        po = [psh(f"po{i}", [128, 512], F32) for i in range(2)]

        PPT_LIST = [(pp[0], ("pp", 0)), (pp[1], ("pp", 1)), (pa, "pa"), (po[0], ("po", 0)), (po[1], ("po", 1))]
        TRB = [(ptr[:], "ptr"),
               (pkv[0].bitcast(BF16)[:].rearrange("p a b -> p (a b)"), ("pkv", 0)),
               (pkv[1].bitcast(BF16)[:].rearrange("p a b -> p (a b)"), ("pkv", 1)),
               (pp[0].bitcast(BF16)[:], ("pp", 0))]
        TH2 = TH[:].rearrange("p h t -> p (h t)")
        KK2 = KK[:].rearrange("p h t -> p (h t)")
        CUM2 = CUM[:].rearrange("p h t -> p (h t)")
        E22 = E2[:].rearrange("p h t -> p (h t)")
        QS2 = QS[:].rearrange("p h t -> p (h t)")
        THr = [("TH", h) for h in range(4)]
        KKr = [("KK", h) for h in range(4)]
        CUMr = [("CUM", h) for h in range(4)]
        E2r = [("E2", h) for h in range(4)]
        QSr = [("QS", h) for h in range(4)]

        stg = [(CUM2, CUMr), (E22, E2r), (KK2, KKr), (QS2, QSr)]
        for dc in range(8):
            w, wr = stg[dc % 4]
            S_.dma("sp" if dc % 2 == 0 else "act", ("wst", dc % 4), W=wr, out=w, in_=g.win_v[dc][:, 2048:4096])
            S_.op("dve", "tensor_scalar", R=wr + ["CST"], W=[("WH", dc)],
                  out=WH[:, dc, :], in0=w, scalar1=CST[:, C_G + dc:C_G + dc + 1], scalar2=None, op0=ALU.mult)
        for h in range(4):
            w, wr = stg[h % 4]
            S_.dma("sp" if h % 2 == 0 else "act", ("wst", h % 4), W=wr, out=w[:, 0:1024], in_=g.wout_v[4 + h])
            S_.op("dve", "tensor_copy", R=wr, W=[("WOh", h)], out=WOh[:, h, :], in_=w[:, 0:1024])
        S_.op("pool", "memset", W=[("stC", h) for h in range(4)], ap=stC[:], constant=0.0)

        ppi = [0]
        PPT = PPT_LIST

        def next_pp():
            i = ppi[0] % len(PPT)
            ppi[0] += 1
            return i

        def xr(b, j=None):
            return [("xt", b, jj) for jj in (range(4) if j is None else [j])]

        def load_x(st):
            b = st % 2
            S_.dma("sp", ("xt", b), W=xr(b), out=xt[b][:], in_=g.x_v[st])

        hnTr = [("hnT", j) for j in range(4)]

        def xnorm_pre(st):
            b = st % 2
            for j in range(4):
                S_.op("dve", "scalar_tensor_tensor", R=xr(b, j), W=["junk", ("ss", j)],
                      out=junk[:], in0=xt[b][:, j, :], scalar=1.0, in1=xt[b][:, j, :], op0=ALU.mult, op1=ALU.mult, accum_out=ss[:, j:j + 1])
            S_.op("act", "activation", R=[("ss", j) for j in range(4)] + ["CST"], W=["lnr4"],
                  out=lnr4[:], in_=ss[:], func=AF.Ln, scale=1.0 / D, bias=eps_ap)
            S_.op("act", "activation", R=["lnr4"], W=["rstd"], out=rstd[:], in_=lnr4[:], func=AF.Exp, scale=-0.5)
            for j in range(4):
                S_.op("dve", "tensor_scalar", R=xr(b, j) + ["rstd"], W=[("hn", j)],
                      out=hn[:, j, :], in0=xt[b][:, j, :], scalar1=rstd[:, j:j + 1], scalar2=None, op0=ALU.mult)

        def xnorm_T(st, j):
            tbk, tbr = TRB[j]
            for dc in range(8):
                S_.op("pe", "transpose", R=[("hn", j), "identb"], W=[tbr], acc=True, sig=(dc == 7),
                      out=tbk[:, dc * 128:(dc + 1) * 128], in_=hn[:, j, dc * 128:(dc + 1) * 128], identity=identb[:])
            S_.op("act", "activation", R=[tbr], W=[("hnT", j)],
                  out=hnT[:, :, j * 128:(j + 1) * 128], in_=tbk.rearrange("p (c t) -> p c t", t=128), func=AF.Copy)

        def xnorm_spill(st):
            if "A" in g.stages or "S" in g.stages:
                S_.dma("sp", "hnTo", R=hnTr, W=[("hnT_d", st)], out=g.hnT_d[st], in_=hnT[:].rearrange("p c t -> p (c t)"))

        def proj_fm(col0):
            i = next_pp()
            for dc in range(8):
                S_.op("pe", "matmul", R=[("WH", dc)] + hnTr, W=[PPT[i][1]], acc=True, sig=(dc == 7),
                      out=PPT[i][0][:], lhsT=WH[:, dc, col0:col0 + 128], rhs=hnT[:, dc, :], start=(dc == 0), stop=(dc == 7))
            return i

        def fproj():
            for h in range(4):
                i = proj_fm(512 + h * 128)
                S_.op("act", "activation", R=[PPT[i][1]], W=[("TH", h)], out=TH[:, h, :], in_=PPT[i][0][:], func=AF.Tanh, scale=0.5)
            for h in range(4):
                S_.op("dve", "tensor_scalar", R=[("TH", h), "lbA", "lbB"], W=[("TH", h)],
                      out=TH[:, h, :], in0=TH[:, h, :], scalar1=lbA[:, h:h + 1], scalar2=lbB[:, h:h + 1], op0=ALU.mult, op1=ALU.add)
                S_.op("dve", "tensor_scalar", R=[("TH", h)], W=[("KK", h)], out=KK[:, h, :], in0=TH[:, h, :], scalar1=-1.0, scalar2=1.0, op0=ALU.mult, op1=ALU.add)


        load_x(0)
        if NST > 1:
            load_x(1)
        xnorm_pre(0)
        for j in range(4):
            xnorm_T(0, j)
        xnorm_spill(0)
        for st in range(NST):
            b = st % 2
            nxt = st + 1 < NST
            fproj()
            for h in range(4):
                i = proj_fm(0 + h * 128)
                S_.op("act", "activation", R=[PPT[i][1]], W=[("QS", h)], out=QS[:, h, :], in_=PPT[i][0][:], func=AF.Silu)
            for h in range(4):
                i = proj_fm(1536 + h * 128)
                S_.op("act", "activation", R=[PPT[i][1]], W=[("SG", h)], out=SG[:, h, :], in_=PPT[i][0][:], func=AF.Silu)
            if nxt:
                xnorm_pre(st + 1)
            for h in range(4):
                S_.op("act", "activation", R=[("TH", h)], W=[("TH", h)], out=TH[:, h, :], in_=TH[:, h, :], func=AF.Ln)
                S_.op("dve", "tensor_tensor_scan", R=[("TH", h), "CST"], W=[("CUM", h)],
                      out=CUM[:, h, :], data0=CST[:, C_RST:C_RST + 512], data1=TH[:, h, :], initial=0.0, op0=ALU.mult, op1=ALU.add)
            for j in range(4):
                h = j
                i = next_pp()
                for dc in range(8):
                    S_.op("pe", "matmul", R=[("WH", dc), ("hnT", j)], W=[PPT[i][1]], acc=True, sig=(dc == 7),
                          out=PPT[i][0][:], lhsT=hnT[:, dc, j * 128:(j + 1) * 128], rhs=WH[:, dc, 1024:1536], start=(dc == 0), stop=(dc == 7))
                S_.op("dve", "tensor_copy", R=[PPT[i][1]], W=[("Vt", j)], out=Vt[:, j, :], in_=PPT[i][0][:])
                S_.op("act", "activation", R=[("CUM", h)], W=[("TH", h)], out=TH[:, h, :], in_=CUM[:, h, :], func=AF.Exp)
                S_.op("act", "activation", R=[("CUM", h)], W=[("E2", h)], out=E2[:, h, :], in_=CUM[:, h, :], func=AF.Exp, scale=-1.0)
                S_.op("dve", "tensor_tensor", R=[("QS", h), ("TH", h)], W=[("qdec", h)], out=qdec[:, h, :], in0=QS[:, h, :], in1=TH[:, h, :], op=ALU.mult)
                S_.op("dve", "tensor_tensor", R=[("KK", h), ("E2", h)], W=[("E2", h)], out=E2[:, h, :], in0=KK[:, h, :], in1=E2[:, h, :], op=ALU.mult)
            for h in range(4):
                S_.op("dve", "tensor_copy", R=[("E2", h)], W=[("kinvb", h)], out=kinvb[:, h, :], in_=E2[:, h, :])
                cd_bc = bass.AP(TH, h * 512 + 63, [[2048, 128], [64, 8], [0, 64]])
                S_.op("dve", "tensor_tensor", R=[("E2", h), ("TH", h)], W=[("kend", h)],
                      out=kend[:, h, :].rearrange("p (c t) -> p c t", t=64), in0=E2[:, h, :].rearrange("p (c t) -> p c t", t=64), in1=cd_bc, op=ALU.mult)
            if nxt:
                for j in range(4):
                    xnorm_T(st + 1, j)
                xnorm_spill(st + 1)

            for rnd in range(2):
                for hh in range(2):
                    h = rnd * 2 + hh
                    for j in range(4):
                        col = (hh * 4 + j) * 128
                        S_.op("pe", "transpose", R=[("kend", h), "identb"], W=[TRB[rnd][1]], acc=True, sig=(hh == 1 and j == 3),
                              out=TRB[rnd][0][:, col:col + 128], in_=kend[:, h, j * 128:(j + 1) * 128], identity=identb[:])
                S_.op("act", "activation", R=[TRB[rnd][1]], W=[("kendT", rnd * 2), ("kendT", rnd * 2 + 1)],
                      out=kendT[:, rnd * 2:rnd * 2 + 2, :, :].rearrange("p h j e -> p (h j e)"), in_=TRB[rnd][0], func=AF.Copy)

            def stage1(h):
                sbuf = h % 3
                for j in range(4):
                    S_.op("pe", "matmul", R=[("kinvb", h), ("qdec", h)], W=["pa"], acc=True, sig=(j == 3),
                          out=pa[:, j * 128:(j + 1) * 128], lhsT=kinvb[:, h, j * 128:(j + 1) * 128], rhs=qdec[:, h, j * 128:(j + 1) * 128], start=(j == 0), stop=(j == 3))
                S_.op("dve", "tensor_tensor", R=["pa"] + [("m2b", r) for r in range(4)], W=[("attm", h)],
                      out=attm[:, h, :], in0=pa[:], in1=m2b[:].rearrange("p c t -> p (c t)"), op=ALU.mult)
                S_.op("act", "activation", R=[("stC", h)], W=[("stB", sbuf, 0)], out=stB[sbuf][:, 0, :], in_=stC[:, h, :], func=AF.Copy)
                for c in range(8):
                    j = c // 2
                    r0 = (c % 2) * 64
                    pk = pkv[c % 2]
                    S_.op("pe", "matmul", R=[("kendT", h), ("Vt", j)], W=[("pkv", c % 2)], acc=True, sig=(c >= 6),
                          out=pk[:, c // 2, :], lhsT=kendT[r0:r0 + 64, h, j, :], rhs=Vt[r0:r0 + 64, j, h * 128:(h + 1) * 128], start=True, stop=True)
                for c in range(8):
                    src = stC[:, h, :] if c == 0 else stS[sbuf][:, c, :]
                    srcr = ("stC", h) if c == 0 else ("stS", sbuf, c)
                    dst = stC[:, h, :] if c == 7 else stS[sbuf][:, c + 1, :]
                    dstr = ("stC", h) if c == 7 else ("stS", sbuf, c + 1)
                    S_.op("dve", "scalar_tensor_tensor", R=[("pkv", c % 2), srcr, ("TH", h)], W=[dstr],
                          out=dst, in0=src, scalar=TH[:, h, c * 64 + 63:c * 64 + 64], in1=pkv[c % 2][:, c // 2, :], op0=ALU.mult, op1=ALU.add)
                S_.op("act", "activation", R=[("stS", sbuf, c) for c in range(1, 8)], W=[("stB", sbuf, 1)],
                      out=stB[sbuf][:, 1:8, :], in_=stS[sbuf][:, 1:8, :], func=AF.Copy)

            def stage2(h):
                sbuf = h % 3
                pob = po[h % 2]
                por = ("po", h % 2)
                for c in range(8):
                    j = c // 2
                    if c % 2 == 0:
                        S_.op("pe", "matmul", R=[("Vt", j), ("attm", h)], W=[por], acc=True,
                              out=pob[:, j * 128:(j + 1) * 128], lhsT=Vt[:, j, h * 128:(h + 1) * 128], rhs=attm[:, h, j * 128:(j + 1) * 128], start=(c == 0), stop=False)
                    S_.op("pe", "matmul", R=[("stB", sbuf, 0), ("stB", sbuf, 1), ("qdec", h)], W=[por], acc=True, sig=(c == 7),
                          out=pob[:, c * 64:(c + 1) * 64], lhsT=stB[sbuf][:, c, :], rhs=qdec[:, h, c * 64:(c + 1) * 64], start=False, stop=(c == 7))
                S_.op("act", "activation", R=[por], W=["SQ"], out=SQ[:], in_=pob[:], func=AF.Square)
                S_.op("pe", "matmul", R=["onesb", "SQ"], W=[("pp", 1)], sig=True, out=pp[1][:], lhsT=onesb[:], rhs=SQ[:], start=True, stop=True)
                S_.op("act", "activation", R=[("pp", 1), "CST"], W=["LNR"], out=LNR[:], in_=pp[1][:], func=AF.Ln, scale=1.0 / 128, bias=eps_ap)
                S_.op("act", "activation", R=["LNR"], W=["Rr"], out=Rr[:], in_=LNR[:], func=AF.Exp, scale=-0.5)
                S_.op("dve", "scalar_tensor_tensor", R=[por, "Rr", "CST"], W=["Y1"],
                      out=Y1[:], in0=pob[:], scalar=CST[:, C_HW + h:C_HW + h + 1], in1=Rr[:], op0=ALU.mult, op1=ALU.mult)
                S_.op("dve", "tensor_tensor", R=["Y1", ("SG", h)], W=[("yh", h)], out=yh[:, h, :], in0=Y1[:], in1=SG[:, h, :], op=ALU.mult)

            if "h1" in g.stages:
                pass
            elif "h2" in g.stages:
                for h in range(4):
                    stage1(h)
            else:
                stage1(0)
                stage1(1)
                stage1(2)
                stage2(0)
                stage1(3)
                stage2(1)
                stage2(2)
                stage2(3)

            for j in range(4 if ("h1" not in g.stages and "h2" not in g.stages) else 0):
                for dh in range(2):
                    i = next_pp()
                    for h in range(4):
                        S_.op("pe", "matmul", R=[("yh", h), ("WOh", h)], W=[PPT[i][1]], acc=True, sig=(h == 3),
                              out=PPT[i][0][:], lhsT=yh[:, h, j * 128:(j + 1) * 128], rhs=WOh[:, h, dh * 512:(dh + 1) * 512], start=(h == 0), stop=(h == 3))
                    S_.op("dve", "tensor_tensor", R=[PPT[i][1], ("xt", b, j)], W=[("xt", b, j)],
                          out=xt[b][:, j, dh * 512:(dh + 1) * 512], in0=PPT[i][0][:], in1=xt[b][:, j, dh * 512:(dh + 1) * 512], op=ALU.add)
            S_.dma("sp", ("parto", b), R=xr(b), W=[("out", st)], out=g.part_v[st], in_=xt[b][:])
            if st + 2 < NST:
                load_x(st + 2)


def _pass_ab(g):
    nc = g.nc; S_ = g.S_; NST = g.NST; CST = g.CST; identb = g.identb; onesb = g.onesb; eps_ap = g.eps_ap
    S = g.S
    c1, c2, c3 = _split_2pi()
    qs = contextlib.ExitStack()
    with qs:
        qT = qs.enter_context(nc.sbuf_tensor("qT", [128, 4, S], BF16))
        kT = qs.enter_context(nc.sbuf_tensor("kT", [128, 4, S], BF16))
        vT = qs.enter_context(nc.sbuf_tensor("vT", [128, 4, S], BF16))

        as_ = contextlib.ExitStack()
        with as_:
            def sba(name, shape, dt):
                return as_.enter_context(nc.sbuf_tensor(name, list(shape), dt))

            def psa(name, shape, dt):
                return as_.enter_context(nc.psum_tensor(name, list(shape), dt))

            WA = sba("WA", [128, 8, 2048], BF16)
            wst = [sba(f"wsta{i}", [128, 1024], F32) for i in range(4)]
            permb = sba("permb", [128, 128], BF16)
            hnT = [sba(f"hnTa{i}", [128, 8, 512], BF16) for i in range(2)]
            posi = sba("posi", [128, 512], I32)
            posf = sba("posf", [128, 512], F32)
            ang = sba("ang", [128, 512], F32)
            ki = sba("ki", [128, 512], I32)
            kf = sba("kf", [128, 512], F32)
            r1 = sba("r1", [128, 512], F32)
            r2 = sba("r2", [128, 512], F32)
            aab = sba("aab", [128, 512], F32)
            COSb = [sba(f"COS{i}", [128, 512], F32) for i in range(2)]
            SINb = [sba(f"SIN{i}", [128, 512], F32) for i in range(2)]
            Qb = [sba(f"Qb{i}", [128, 512], BF16) for i in range(2)]
            T1 = [sba(f"T1{i}", [128, 512], F32) for i in range(2)]
            T2 = [sba(f"T2{i}", [128, 512], F32) for i in range(2)]
            sgt = [sba(f"sgt{i}", [128, 512], BF16) for i in range(2)]
            pp = [psa(f"ppa{i}", [128, 512], F32) for i in range(6)]
            pq = [psa(f"pqa{i}", [128, 512], F32) for i in range(2)]

            S_.op("dve", "tensor_copy", R=["CST"], W=["permb"], out=permb[:], in_=CST[:, C_PERM:C_PERM + 128])
            for dc in range(8):
                for hf in range(2):
                    wi = (dc * 2 + hf) % 4
                    w = wst[wi]
                    S_.dma("sp" if wi % 2 == 0 else "act", ("wsta", wi), W=[("wsta", wi)], out=w[:], in_=g.win_v[dc][:, hf * 1024:(hf + 1) * 1024])
                    S_.op("dve", "tensor_scalar", R=[("wsta", wi), "CST"], W=[("WA", dc)],
                          out=WA[:, dc, hf * 1024:(hf + 1) * 1024], in0=w[:], scalar1=CST[:, C_G + dc:C_G + dc + 1], scalar2=None, op0=ALU.mult)

            ppi = [0]
            tb = [0]

            def load_hnT(st):
                hb_ = st % 2
                S_.dma("sp", ("hnTa", hb_), R=[("hnT_d", st)], W=[("hnTa", hb_)], out=hnT[hb_][:].rearrange("p c t -> p (c t)"), in_=g.hnT_d[st])

            def tables(st):
                tsl_ = slice(st * 512, (st + 1) * 512)
                COS_, SIN_ = COSb[st % 2], SINb[st % 2]
                cn, sn = ("COS", st % 2), ("SIN", st % 2)
                S_.dma("sp", "posi", W=["posi"], out=posi[:], in_=g.pos_d[:, tsl_])
                S_.op("dve", "tensor_copy", R=["posi"], W=["posf"], out=posf[:], in_=posi[:])
                S_.op("dve", "tensor_scalar", R=["posf", "CST"], W=["ang"], out=ang[:], in0=posf[:], scalar1=CST[:, C_INVF:C_INVF + 1], scalar2=None, op0=ALU.mult)
                S_.op("dve", "tensor_scalar", R=["ang"], W=["ki"], out=ki[:], in0=ang[:], scalar1=float(1.0 / TWO_PI), scalar2=None, op0=ALU.mult)
                S_.op("dve", "tensor_copy", R=["ki"], W=["kf"], out=kf[:], in_=ki[:])
                S_.op("dve", "scalar_tensor_tensor", R=["kf", "ang"], W=["r1"], out=r1[:], in0=kf[:], scalar=-c1, in1=ang[:], op0=ALU.mult, op1=ALU.add)
                S_.op("dve", "scalar_tensor_tensor", R=["kf", "r1"], W=["r2"], out=r2[:], in0=kf[:], scalar=-c2, in1=r1[:], op0=ALU.mult, op1=ALU.add)
                S_.op("dve", "scalar_tensor_tensor", R=["kf", "r2"], W=["r1"], out=r1[:], in0=kf[:], scalar=-c3, in1=r2[:], op0=ALU.mult, op1=ALU.add)
                S_.op("dve", "tensor_scalar", R=["r1"], W=["r2"], out=r2[:], in0=r1[:], scalar1=float(np.pi), scalar2=float(-np.pi), op0=ALU.min, op1=ALU.max)
                S_.op("dve", "scalar_tensor_tensor", R=["r2"], W=["aab"], out=aab[:], in0=r2[:], scalar=-1.0, in1=r2[:], op0=ALU.mult, op1=ALU.max)
                S_.op("act", "activation", R=["aab", "CST"], W=[cn], out=COS_[:], in_=aab[:], func=AF.Sin, scale=-1.0, bias=CST[:, C_HPI:C_HPI + 1])
                S_.op("act", "activation", R=["r2", "CST"], W=[sn], out=SIN_[:], in_=r2[:], func=AF.Sin, scale=CST[:, C_SGN:C_SGN + 1])

            load_hnT(0)
            tables(0)
            for st in range(NST if "a1" not in g.stages else 0):
                hb = st % 2
                tsl = slice(st * 512, (st + 1) * 512)
                if st + 1 < NST:
                    load_hnT(st + 1)
                COS, SIN = COSb[st % 2], SINb[st % 2]
                cosn, sinn = ("COS", st % 2), ("SIN", st % 2)

                hnTr = [("hnTa", hb)]
                if "a2" in g.stages:
                    continue

                def proj(col0):
                    i = ppi[0] % 6
                    ppi[0] += 1
                    for dc in range(8):
                        S_.op("pe", "matmul", R=[("WA", dc)] + hnTr, W=[("ppa", i)], acc=True, sig=(dc == 7),
                              out=pp[i][:], lhsT=WA[:, dc, col0:col0 + 128], rhs=hnT[hb][:, dc, :], start=(dc == 0), stop=(dc == 7))
                    return i

                pendq = []

                def rot(i_, t_, dst_, dname_, c_):
                    S_.op("pe", "matmul", R=["permb", ("Qb", t_)], W=[("pqa", t_)], sig=True,
                          out=pq[t_][:], lhsT=permb[:], rhs=Qb[t_][:], start=True, stop=True)
                    S_.op("dve", "tensor_tensor", R=[("Qb", t_), cosn], W=[("T1", t_)], out=T1[t_][:], in0=Qb[t_][:], in1=COS[:], op=ALU.mult)
                    S_.op("dve", "tensor_tensor", R=[("pqa", t_), sinn], W=[("T2", t_)], out=T2[t_][:], in0=pq[t_][:], in1=SIN[:], op=ALU.mult)
                    S_.op("pool", "tensor_tensor", R=[("T1", t_), ("T2", t_)], W=[(dname_, c_, st)], out=dst_[:, c_, tsl], in0=T1[t_][:], in1=T2[t_][:], op=ALU.add)

                for which, dst in ((0, qT), (1, kT)):
                    dname = "qT" if which == 0 else "kT"
                    for c in range(4):
                        i = proj(which * 512 + c * 128)
                        t = tb[0] % 2
                        tb[0] += 1
                        S_.op("act", "activation", R=[("ppa", i)], W=[("Qb", t)], out=Qb[t][:], in_=pp[i][:], func=AF.Copy)
                        if pendq:
                            rot(*pendq.pop())
                        pendq.append((i, t, dst, dname, c))
                if st + 1 < NST:
                    tables(st + 1)
                for c in range(4):
                    i = proj(1024 + c * 128)
                    S_.op("act", "activation", R=[("ppa", i)], W=[("vT", c, st)], out=vT[:, c, tsl], in_=pp[i][:], func=AF.Copy)
                    if pendq:
                        rot(*pendq.pop())
                for c in range(4 if "a5" not in g.stages else 0):
                    i = proj(1536 + c * 128)
                    t = (st * 4 + c) % 2
                    S_.op("act", "activation", R=[("ppa", i)], W=[("sgt", t)], out=sgt[t][:], in_=pp[i][:], func=AF.Silu)
                    S_.dma("sp", ("sgo", t), R=[("sgt", t)], W=[("sg_d", c, st)], out=g.sg_d[c][:, tsl], in_=sgt[t][:])

        if "B" not in g.stages:
            return
        S_.barrier()
        bs = contextlib.ExitStack()
        with bs:
            def sbb(name, shape, dt):
                return bs.enter_context(nc.sbuf_tensor(name, list(shape), dt))

            def psb_(name, shape, dt):
                return bs.enter_context(nc.psum_tensor(name, list(shape), dt))

            NVW, NPT, PVLAG = 5, 5, 2
            acc = sbb("accXY", [128, 2, S], F32)
            maskb = sbb("maskb", [128, 2, 2, 256], BF16)
            VW = [sbb(f"VW{i}", [128, 2, 256], BF16) for i in range(NVW)]
            PT = [sbb(f"PT{i}", [128, 2, 512], BF16) for i in range(NPT)]
            scx = sbb("scx", [128, 2], F32)
            SQX = sbb("SQX", [128, 512], BF16)
            SQY = sbb("SQY", [128, 512], BF16)
            LNR = sbb("LNRb", [128, 512], F32)
            Rr = sbb("Rrb", [128, 512], F32)
            Y1 = sbb("Y1b", [128, 512], F32)
            sgl = [sbb(f"sgl{i}", [128, 512], BF16) for i in range(2)]

            ptv = psb_("ptv", [128, 512], BF16)
            psc = [psb_(f"psc{i}", [128, 2, 512], F32) for i in range(2)]
            pxy = [psb_(f"pxy{i}", [128, 2, 256], F32) for i in range(2)]
            pU = psb_("pU", [128, 512], F32)
            PTV = [(ptv[:, 0:256], "ptv"), (pU.bitcast(BF16)[:, 0:256], "pU")]

            for hh in range(2):
                for kb in range(2):
                    S_.op("dve", "tensor_copy", R=["CST"], W=["maskb"], out=maskb[:, hh, kb, :], in_=CST[:, C_AMASK:C_AMASK + 256])
            for i in range(NVW):
                S_.op("pool", "memset", W=[("VW", i)], ap=VW[i][:], constant=1.0)
            se = float(np.sqrt(EPS))
            S_.op("pool", "memset", W=["scx"], ap=scx[0:64, 0:1], constant=1.0)
            S_.op("pool", "memset", W=["scx"], ap=scx[64:128, 0:1], constant=se)
            S_.op("pool", "memset", W=["scx"], ap=scx[0:64, 1:2], constant=se)
            S_.op("pool", "memset", W=["scx"], ap=scx[64:128, 1:2], constant=1.0)

            vwi = [0]
            pti = [0]
            gi = [0]
            SC = 1.0 / 8.0

            for c in range(4):
                first_pat = True
                pend = []
                for d in PATTERNS:
                    sub_len = S // d
                    n_blk = sub_len // 128
                    for r in range(d):
                        def tok(j, nb=1, r=r, d=d):
                            s0 = r + d * 128 * j
                            return slice(s0, s0 + d * (128 * nb - 1) + 1, d) if d > 1 else slice(s0, s0 + 128 * nb)

                        grp = {}

                        def emit_pv(j0, nkb, pbuf, vws, grp=grp, tok=tok, n_blk=n_blk, first_pat=first_pat):
                            for kb in range(nkb):
                                j = j0 + kb
                                gq = j // 2
                                if j % 2 == 0:
                                    segs = [(gq, 0, 0, min(2, n_blk - j))]
                                else:
                                    segs = [(gq, 1, 0, 1)]
                                    if j + 1 < n_blk:
                                        segs.append((gq + 1, 0, 1, 1))
                                for (gg, qpos, poff_blk, nb) in segs:
                                    if gg not in grp:
                                        tot = 2 * ((1 if gg > 0 else 0) + 1 + (1 if 2 * gg + 1 < n_blk else 0))
                                        grp[gg] = [gi[0] % 2, 0, tot]
                                        gi[0] += 1
                                    for hh in range(2):
                                        bank, cnt, tot = grp[gg]
                                        qoff = qpos * 128
                                        poff = kb * 256 + poff_blk * 128
                                        S_.op("pe", "matmul", R=[("VW", vws[kb]), ("PT", pbuf)], W=[("pxy", bank)], acc=True, sig=(cnt == tot - 1),
                                              out=pxy[bank][:, hh, qoff:qoff + nb * 128], lhsT=VW[vws[kb]][:, kb, hh * 128:(hh + 1) * 128],
                                              rhs=PT[pbuf][:, hh, poff:poff + nb * 128], start=(cnt == 0), stop=(cnt == tot - 1), skip_group_check=True)
                                        grp[gg][1] += 1
                                        if grp[gg][1] == tot:
                                            nqb = min(2, n_blk - 2 * gg)
                                            dstap = acc[:, :, tok(2 * gg, nqb)]
                                            if first_pat:
                                                S_.op("dve", "tensor_copy", R=[("pxy", bank)], W=[("acc", c)], out=dstap, in_=pxy[bank][:, :, 0:nqb * 128])
                                            else:
                                                S_.op("dve", "tensor_tensor", R=[("pxy", bank), ("acc", c)], W=[("acc", c)], out=dstap, in0=pxy[bank][:, :, 0:nqb * 128], in1=dstap, op=ALU.add)

                        for j0 in range(0, n_blk, 2):
                            nkb = min(2, n_blk - j0)
                            sbuf = (j0 // 2) % 2
                            pbuf = pti[0] % NPT
                            pti[0] += 1
                            vws = []
                            nqs = []
                            for kb in range(nkb):
                                j = j0 + kb
                                nq = 2 if j + 1 < n_blk else 1
                                nqs.append(nq)
                                for hh in range(2):
                                    S_.op("pe", "matmul", R=[("kT", c, s_) for s_ in range(NST)] + [("qT", c, s_) for s_ in range(NST)], W=[("psc", sbuf)], acc=True, sig=(kb == nkb - 1 and hh == 1),
                                          out=psc[sbuf][:, hh, kb * 256:kb * 256 + nq * 128], lhsT=kT[hh * 64:(hh + 1) * 64, c, tok(j)],
                                          rhs=qT[hh * 64:(hh + 1) * 64, c, tok(j, nq)], start=True, stop=True)
                            tvh = pti[0] % 2
                            for kb in range(nkb):
                                j = j0 + kb
                                S_.op("pe", "transpose", R=[("vT", c, s_) for s_ in range(NST)] + ["identb"], W=[PTV[tvh][1]], acc=True, sig=(kb == nkb - 1),
                                      out=PTV[tvh][0][:, kb * 128:(kb + 1) * 128], in_=vT[:, c, tok(j)], identity=identb[:])
                            ncol = (nkb - 1) * 256 + nqs[-1] * 128
                            S_.op("act", "activation", R=[("psc", sbuf)], W=[("PT", pbuf)], out=PT[pbuf][:, :, 0:ncol], in_=psc[sbuf][:, :, 0:ncol], func=AF.Exp, scale=SC)
                            meng = "dve"
                            S_.op(meng, "tensor_tensor", R=[("PT", pbuf), "maskb"], W=[("PT", pbuf)], out=PT[pbuf][:, :, 0:ncol], in0=PT[pbuf][:, :, 0:ncol],
                                  in1=maskb[:].rearrange("p h k q -> p h (k q)")[:, :, 0:ncol], op=ALU.mult)
                            v = vwi[0] % NVW
                            vwi[0] += 1
                            vws = [v] * nkb
                            dst = bass.AP(VW[v], 0, [[512, 128], [256, nkb], [192, 2], [1, 64]])
                            S_.op("act", "activation", R=[PTV[tvh][1]], W=[("VW", v)], out=dst,
                                  in_=PTV[tvh][0][:, 0:nkb * 128].rearrange("p (k h e) -> p k h e", h=2, e=64), func=AF.Copy)
                            pend.append((emit_pv, (j0, nkb, pbuf, vws)))
                            if len(pend) > PVLAG:
                                f_, a_ = pend.pop(0)
                                f_(*a_)
                    first_pat = False
                while pend:
                    f_, a_ = pend.pop(0)
                    f_(*a_)

                for st in range(NST):
                    tsl = slice(st * 512, (st + 1) * 512)
                    t = st % 2
                    S_.dma("sp", ("sgl", t), R=[("sg_d", c, st)], W=[("sgl", t)], out=sgl[t][:], in_=g.sg_d[c][:, tsl])
                    S_.op("act", "activation", R=[("acc", c), "scx"], W=["SQX"], out=SQX[:], in_=acc[:, 0, tsl], func=AF.Square, scale=scx[:, 0:1])
                    S_.op("act", "activation", R=[("acc", c), "scx"], W=["SQY"], out=SQY[:], in_=acc[:, 1, tsl], func=AF.Square, scale=scx[:, 1:2])
                    S_.op("pe", "matmul", R=["onesb", "SQX"], W=["pU"], out=pU[0:64, :], lhsT=onesb[:, 0:64], rhs=SQX[:], start=True, stop=True)
                    S_.op("pe", "matmul", R=["onesb", "SQY"], W=["pU"], sig=True, out=pU[64:128, :], lhsT=onesb[:, 0:64], rhs=SQY[:], start=True, stop=True)
                    S_.op("act", "activation", R=["pU"], W=["LNRb"], out=LNR[:], in_=pU[:], func=AF.Ln, scale=1.0 / 64)
                    S_.op("act", "activation", R=["LNRb"], W=["Rrb"], out=Rr[:], in_=LNR[:], func=AF.Exp, scale=-0.5)
                    S_.op("dve", "scalar_tensor_tensor", R=[("acc", c), "Rrb", "CST"], W=["Y1b"],
                          out=Y1[0:64, :], in0=acc[0:64, 0, tsl], scalar=CST[0:64, C_AW + c:C_AW + c + 1], in1=Rr[0:64, :], op0=ALU.mult, op1=ALU.mult)
                    S_.op("dve", "scalar_tensor_tensor", R=[("acc", c), "Rrb", "CST"], W=["Y1b"],
                          out=Y1[64:128, :], in0=acc[64:128, 1, tsl], scalar=CST[64:128, C_AW + c:C_AW + c + 1], in1=Rr[64:128, :], op0=ALU.mult, op1=ALU.mult)
                    S_.op("pool", "tensor_tensor", R=["Y1b", ("sgl", t)] , W=[("qT", c, st)], out=qT[:, c, tsl], in0=Y1[:], in1=sgl[t][:], op=ALU.mult)

        if "C" in g.stages:
            S_.barrier()
            _pass_c_inner(g, qT)


def _pass_c_inner(g, qT):
    nc = g.nc; S_ = g.S_; NST = g.NST; CST = g.CST; eps_ap = g.eps_ap
    cs = contextlib.ExitStack()
    with cs:
        def sbc(name, shape, dt):
            return cs.enter_context(nc.sbuf_tensor(name, list(shape), dt))

        def psc_(name, shape, dt):
            return cs.enter_context(nc.psum_tensor(name, list(shape), dt))

        WOa = sbc("WOa", [128, 4, 1024], BF16)
        wst = [sbc(f"wstc{i}", [128, 1024], F32) for i in range(2)]
        fnw = sbc("fnw_sb", [128, D], F32)
        zt = [sbc(f"zt{i}", [128, 4, 1024], F32) for i in range(2)]
        junk = sbc("junkc", [128, 1024], BF16)
        ss = sbc("ssc", [128, 4], F32)
        lnr4 = sbc("lnr4c", [128, 4], F32)
        rstd = sbc("rstdc", [128, 4], F32)
        pc = [psc_(f"pc{i}", [128, 512], F32) for i in range(8)]

        S_.dma("sp", "fnw", W=["fnw"], out=fnw[:], in_=g.fnw_d)
        for c in range(4):
            w = wst[c % 2]
            S_.dma("sp", ("wstc", c % 2), W=[("wstc", c % 2)], out=w[:], in_=g.wout_v[c])
            S_.op("dve", "tensor_copy", R=[("wstc", c % 2)], W=[("WOa", c)], out=WOa[:, c, :], in_=w[:])

        def load_part(st):
            b = st % 2
            S_.dma("sp", ("zt", b), R=[("out", st)], W=[("zt", b, j) for j in range(4)], out=zt[b][:], in_=g.part_v[st])

        pci = [0]
        load_part(0)
        for st in range(NST):
            b = st % 2
            if st + 1 < NST:
                load_part(st + 1)
            for j in range(4):
                tok = slice(st * 512 + j * 128, st * 512 + (j + 1) * 128)
                for dh in range(2):
                    i = pci[0] % 8
                    pci[0] += 1
                    for c in range(4):
                        S_.op("pe", "matmul", R=[("qT", c, st), ("WOa", c)], W=[("pc", i)], acc=True, sig=(c == 3),
                              out=pc[i][:], lhsT=qT[:, c, tok], rhs=WOa[:, c, dh * 512:(dh + 1) * 512], start=(c == 0), stop=(c == 3))
                    S_.op("dve", "tensor_tensor", R=[("pc", i), ("zt", b, j)], W=[("zt", b, j)],
                          out=zt[b][:, j, dh * 512:(dh + 1) * 512], in0=pc[i][:], in1=zt[b][:, j, dh * 512:(dh + 1) * 512], op=ALU.add)
                S_.op("act", "activation", R=[("zt", b, j)], W=["junkc", ("ssc", j)],
                      out=junk[:], in_=zt[b][:, j, :], func=AF.Square, accum_out=ss[:, j:j + 1])
            S_.op("act", "activation", R=[("ssc", j) for j in range(4)] + ["CST"], W=["lnr4c"], out=lnr4[:], in_=ss[:], func=AF.Ln, scale=1.0 / D, bias=eps_ap)
            S_.op("act", "activation", R=["lnr4c"], W=["rstdc"], out=rstd[:], in_=lnr4[:], func=AF.Exp, scale=-0.5)
            for j in range(4):
                S_.op("dve", "scalar_tensor_tensor", R=[("zt", b, j), "rstdc", "fnw"], W=[("zt", b, j)],
                      out=zt[b][:, j, :], in0=zt[b][:, j, :], scalar=rstd[:, j:j + 1], in1=fnw[:], op0=ALU.mult, op1=ALU.mult)
            S_.dma("sp", ("outo", b), R=[("zt", b, j) for j in range(4)], W=[("out", st)], out=g.out_v[st], in_=zt[b][:])


def prep_core_inputs(inputs, b, S=4096):
    cst = _const_table()
    cst[:, C_G:C_G + 8] = np.asarray(inputs["mix_norm_w"], np.float32)[0].reshape(8, 128).T
    cst[:, C_AW:C_AW + 4] = np.asarray(inputs["attn_out_norm_w"], np.float32)[0].reshape(4, 128).T
    cst[:, C_HW:C_HW + 4] = np.asarray(inputs["hgrn_out_norm_w"], np.float32)[0].reshape(4, 128).T
    lb = np.asarray(inputs["hgrn_lb_raw"], np.float32)
    cst[:, C_LB0:C_LB0 + 4] = lb[0].reshape(4, 128).T
    cst[:, C_LB1:C_LB1 + 4] = lb[1].reshape(4, 128).T
    pos = np.asarray(inputs["positions"])[b, :S].astype(np.int32)
    return {
        "x": np.ascontiguousarray(np.asarray(inputs["x"], np.float32)[b, :S]),
        "pos": np.ascontiguousarray(np.broadcast_to(pos[None, :], (128, S))),
        "w_in": np.ascontiguousarray(np.asarray(inputs["w_in"], np.float32)[0]),
        "w_out": np.ascontiguousarray(np.asarray(inputs["w_out"], np.float32)[0]),
        "cst": cst,
        "fnw": np.ascontiguousarray(np.broadcast_to(np.asarray(inputs["final_norm_w"], np.float32)[None, :], (128, D))),
    }


_NC_CACHE = {}


def kernel(x, positions, w_in, w_out, mix_norm_w, attn_out_norm_w, hgrn_out_norm_w, hgrn_lb_raw, final_norm_w):
    inputs = dict(x=x, positions=positions, w_in=w_in, w_out=w_out, mix_norm_w=mix_norm_w,
                  attn_out_norm_w=attn_out_norm_w, hgrn_out_norm_w=hgrn_out_norm_w,
                  hgrn_lb_raw=hgrn_lb_raw, final_norm_w=final_norm_w)
    B, S, _ = np.asarray(x).shape
    if S not in _NC_CACHE:
        _NC_CACHE[S] = build(S)
    nc = _NC_CACHE[S]
    in_maps = [prep_core_inputs(inputs, b, S) for b in range(B)]
    res = run_bass_kernel_spmd(nc, in_maps, core_ids=list(range(B)))
    return np.stack([np.asarray(r["out"], np.float32) for r in res.results], axis=0)
```

```python
import contextlib
import numpy as np
import concourse.bass as bass
import concourse.mybir as mybir
from concourse.bass_utils import run_bass_kernel_spmd

F32 = mybir.dt.float32
BF16 = mybir.dt.bfloat16
I32 = mybir.dt.int32
AF = mybir.ActivationFunctionType
ALU = mybir.AluOpType

D = 1024
EPS = 1e-6
ROPE_THETA = 500000.0
PATTERNS = (1, 4, 16)
TWO_PI = 2.0 * np.pi


def _split_2pi():
    c1 = 6.28125
    r = TWO_PI - c1
    c2 = np.float32(r)
    c2 = (c2.view(np.uint32) & np.uint32(0xFFFFF000)).view(np.float32)
    c3 = np.float32(r - float(c2))
    return float(c1), float(c2), float(c3)


C_G = 0
C_AW = 8
C_HW = 12
C_LB0 = 16
C_LB1 = 20
C_INVF = 24
C_SGN = 25
C_HPI = 26
C_EPS = 27
C_ONE = 28
C_IDENT = 32
C_PERM = 160
C_AMASK = 288
C_M2 = 544
C_RST = 672
NCST = 1184


def _const_table():
    c = np.zeros((128, NCST), np.float32)
    p = np.arange(128)
    e = p % 64
    half = 8
    invf = ROPE_THETA ** (-np.arange(half, dtype=np.float64) * (2.0 / 16.0))
    c[:, C_INVF] = np.where(e < 16, invf[e % 8], 0.0)
    c[:, C_SGN] = np.where(e < 8, -1.0, np.where(e < 16, 1.0, 0.0))
    c[:, C_HPI] = np.pi / 2
    c[:, C_EPS] = EPS
    c[:, C_ONE] = 1.0
    c[:, C_IDENT:C_IDENT + 128] = np.eye(128)
    perm = np.zeros((128, 128), np.float32)
    for m in range(128):
        em = m % 64
        if em < 8:
            perm[m + 8, m] = 1.0
        elif em < 16:
            perm[m - 8, m] = 1.0
    c[:, C_PERM:C_PERM + 128] = perm
    k = np.arange(128)[:, None]
    q = np.arange(128)[None, :]
    am = np.concatenate([(q >= k), (q <= k)], axis=1).astype(np.float32)
    c[:, C_AMASK:C_AMASK + 256] = am
    s = np.arange(128)[:, None]
    t = np.arange(128)[None, :]
    c[:, C_M2:C_M2 + 128] = ((s // 64 == t // 64) & (s <= t)).astype(np.float32)
    rst = np.ones(512, np.float32)
    rst[0::64] = 0.0
    c[:, C_RST:C_RST + 512] = rst[None, :]
    return c


class Sched:
    COMPUTE = ("pe", "act", "dve", "pool")

    def __init__(self, nc, stack):
        self.nc = nc
        self.stack = stack
        self.streams = {k: [] for k in ("pe", "act", "dve", "pool", "sp")}
        self.sems = {}
        self.cnt = {}
        self.sigs = {k: [] for k in self.COMPUTE}
        self.waited = {k: {} for k in self.streams}
        self.last_w = {}
        self.readers = {}
        for k in self.COMPUTE:
            self._sem(k)

    def _sem(self, name):
        if name not in self.sems:
            self.sems[name] = self.stack.enter_context(self.nc.semaphore("s_" + "".join(ch if ch.isalnum() else "_" for ch in str(name))))
            self.cnt[name] = 0
        return self.sems[name]

    def _resolve(self, ev):
        kind = ev[0]
        if kind == "dma":
            return ev[1], ev[2]
        eng, seq = ev[1], ev[2]
        best = None
        for s, v in reversed(self.sigs[eng]):
            if s >= seq:
                best = v
            else:
                break
        if best is not None:
            return eng, best
        rec = self.streams[eng][-1]
        assert rec["seq"] >= seq
        self.cnt[eng] += 1
        rec["inc"] = True
        self.sigs[eng].append((rec["seq"], self.cnt[eng]))
        return eng, self.cnt[eng]

    def _collect(self, eng, reads, writes, acc):
        evs = []
        for r in reads:
            ev = self.last_w.get(r)
            if ev is not None:
                evs.append(ev)
        for w in writes:
            ev = self.last_w.get(w)
            if ev is not None and not (acc and ev[0] == "c" and ev[1] == eng):
                evs.append(ev)
            for ev in self.readers.get(w, ()):
                evs.append(ev)
        waits = {}
        for ev in evs:
            name, val = self._resolve(ev)
            if self.waited[eng].get(name, 0) >= val:
                continue
            waits[name] = max(waits.get(name, 0), val)
        for name, val in waits.items():
            self.waited[eng][name] = val
        return list(waits.items())

    def _commit(self, ev, reads, writes):
        for r in reads:
            self.readers.setdefault(r, []).append(ev)
        for w in writes:
            self.last_w[w] = ev
            self.readers[w] = []

    def op(self, eng, method, R=(), W=(), sig=None, acc=False, **kw):
        if sig is None:
            sig = (eng != "pe")
        reads, writes = list(R), list(W)
        fn = (lambda e, method=method, kw=kw: getattr(e, method)(**kw))
        waits = self._collect(eng, reads, writes, acc)
        seq = len(self.streams[eng])
        rec = {"fn": fn, "waits": waits, "inc": False, "seq": seq, "dma": None}
        self.streams[eng].append(rec)
        if sig:
            self.cnt[eng] += 1
            rec["inc"] = True
            self.sigs[eng].append((seq, self.cnt[eng]))
        self._commit(("c", eng, seq), reads, writes)

    def dma(self, q, key, R=(), W=(), **kw):
        reads, writes = list(R), list(W)
        fn = (lambda e, kw=kw: e.dma_start(**kw))
        name = ("dma", key)
        self._sem(name)
        waits = self._collect(q, reads, writes, False)
        self.cnt[name] += 16
        rec = {"fn": fn, "waits": waits, "inc": False, "seq": len(self.streams[q]), "dma": name}
        self.streams[q].append(rec)
        self._commit(("dma", name, self.cnt[name]), reads, writes)

    def barrier(self):
        for eng in self.COMPUTE:
            if self.streams[eng]:
                rec = self.streams[eng][-1]
                if not rec["inc"]:
                    self.cnt[eng] += 1
                    rec["inc"] = True
                    self.sigs[eng].append((rec["seq"], self.cnt[eng]))
        for eng in self.streams:
            waits = []
            for name, val in self.cnt.items():
                if val <= 0 or name == eng:
                    continue
                if self.waited[eng].get(name, 0) >= val:
                    continue
                self.waited[eng][name] = val
                waits.append((name, val))
            self.streams[eng].append({"fn": None, "waits": waits, "inc": False, "seq": len(self.streams[eng]), "dma": None})

    def final_wait(self, q):
        waits = {}
        for name, val in self.cnt.items():
            if isinstance(name, tuple) and name[0] == "dma" and val > 0:
                waits[name] = val
        rec = {"fn": None, "waits": list(waits.items()), "inc": False, "seq": len(self.streams[q]), "dma": None}
        self.streams[q].append(rec)

    def emit(self):
        nc = self.nc
        sems = self.sems

        def replay(name):
            def run(e):
                for rec in self.streams[name]:
                    for sname, val in rec["waits"]:
                        e.wait_ge(sems[sname], val)
                    if rec["fn"] is None:
                        continue
                    ins = rec["fn"](e)
                    if rec["dma"] is not None:
                        ins.then_inc(sems[rec["dma"]], 16)
                    elif rec["inc"]:
                        ins.then_inc(sems[name], 1)
            return run

        with nc.Block() as block:
            block.tensor(replay("pe"))
            block.scalar(replay("act"))
            block.vector(replay("dve"))
            block.gpsimd(replay("pool"))
            block.sync(replay("sp"))


class Ctx:
    pass


def build(S=4096, stages=("H", "A", "B", "C"), dbg=()):
    g = Ctx()
    g.S = S
    g.NST = S // 512
    g.NT = S // 128
    g.stages = stages
    nc = bass.Bass("TRN2", target_bir_lowering=False)
    g.nc = nc
    x_d = nc.dram_tensor("x", [S, D], F32, kind="ExternalInput").ap()
    g.pos_d = nc.dram_tensor("pos", [128, S], I32, kind="ExternalInput").ap()
    win_d = nc.dram_tensor("w_in", [D, 4096], F32, kind="ExternalInput").ap()
    wout_d = nc.dram_tensor("w_out", [D, D], F32, kind="ExternalInput").ap()
    cst_d = nc.dram_tensor("cst", [128, NCST], F32, kind="ExternalInput").ap()
    g.fnw_d = nc.dram_tensor("fnw", [128, D], F32, kind="ExternalInput").ap()
    out_d = nc.dram_tensor("out", [S, D], F32, kind="ExternalOutput").ap()
    part_d = out_d if "C" in stages else nc.dram_tensor("part_scr", [S, D], F32, kind="ExternalOutput").ap()
    g.hnT_d = nc.dram_tensor("hnT_scr", [g.NST, 128, 8 * 512], BF16, kind="Internal").ap()
    g.sg_d = nc.dram_tensor("sg_scr", [4, 128, S], BF16, kind="Internal").ap()
    g.dbg_d = {}
    for name, shape in dbg:
        g.dbg_d[name] = nc.dram_tensor(name, list(shape), F32, kind="ExternalOutput").ap()

    g.x_v = x_d.rearrange("(n j p) d -> n p j d", j=4, p=128)
    g.part_v = part_d.rearrange("(n j p) d -> n p j d", j=4, p=128)
    g.out_v = out_d.rearrange("(n j p) d -> n p j d", j=4, p=128)
    g.win_v = win_d.rearrange("(dc p) c -> dc p c", p=128)
    g.wout_v = wout_d.rearrange("(mc p) c -> mc p c", p=128)

    stack = contextlib.ExitStack()
    with stack:
        S_ = Sched(nc, stack)
        g.S_ = S_

        def sb(name, shape, dt):
            return stack.enter_context(nc.sbuf_tensor(name, list(shape), dt))

        g.CST = CST = sb("CST", [128, NCST], F32)
        g.identb = identb = sb("identb", [128, 128], BF16)
        g.onesb = onesb = sb("onesb", [128, 128], BF16)
        g.m2b = m2b = sb("m2b", [128, 4, 128], BF16)
        g.lbA = lbA = sb("lbA", [128, 4], F32)
        g.lbB = lbB = sb("lbB", [128, 4], F32)
        tmp4 = sb("tmp4", [128, 4], F32)
        tmp4b = sb("tmp4b", [128, 4], F32)

        S_.dma("sp", "cst", W=["CST"], out=CST[:], in_=cst_d)
        S_.op("dve", "tensor_copy", R=["CST"], W=["identb"], out=identb[:], in_=CST[:, C_IDENT:C_IDENT + 128])
        S_.op("dve", "memset", W=["onesb"], ap=onesb[:], constant=1.0)
        for r in range(4):
            S_.op("dve", "tensor_copy", R=["CST"], W=[("m2b", r)], out=m2b[:, r, :], in_=CST[:, C_M2:C_M2 + 128])
        S_.op("dve", "tensor_tensor", R=["CST"], W=["tmp4"], out=tmp4[:], in0=CST[:, C_LB0:C_LB0 + 4], in1=CST[:, C_LB1:C_LB1 + 4], op=ALU.subtract)
        S_.op("act", "activation", R=["tmp4"], W=["tmp4b"], out=tmp4b[:], in_=tmp4[:], func=AF.Tanh, scale=0.5)
        S_.op("dve", "tensor_scalar", R=["tmp4b"], W=["lbA"], out=lbA[:], in0=tmp4b[:], scalar1=-0.25, scalar2=0.25, op0=ALU.mult, op1=ALU.add)
        S_.op("dve", "tensor_scalar", R=["tmp4b"], W=["lbB"], out=lbB[:], in0=tmp4b[:], scalar1=0.25, scalar2=0.75, op0=ALU.mult, op1=ALU.add)
        g.eps_ap = CST[:, C_EPS:C_EPS + 1]

        if "H" in stages:
            _pass_h(g)
        if "A" in stages:
            S_.barrier()
            _pass_ab(g)

        S_.final_wait("sp")
        S_.emit()
    return nc


def _pass_h(g):
    nc = g.nc; S_ = g.S_; NST = g.NST; CST = g.CST; identb = g.identb; onesb = g.onesb; m2b = g.m2b
    lbA = g.lbA; lbB = g.lbB; eps_ap = g.eps_ap

    hs = contextlib.ExitStack()
    with hs:
        def sbh(name, shape, dt):
            return hs.enter_context(nc.sbuf_tensor(name, list(shape), dt))

        def psh(name, shape, dt):
            return hs.enter_context(nc.psum_tensor(name, list(shape), dt))

        WH = sbh("WH", [128, 8, 2048], BF16)
        WOh = sbh("WOh", [128, 4, 1024], BF16)
        xt = [sbh(f"xt{i}", [128, 4, 1024], F32) for i in range(2)]
        junk = sbh("junk", [128, 1024], BF16)
        ss = sbh("ss", [128, 4], F32)
        lnr4 = sbh("lnr4", [128, 4], F32)
        rstd = sbh("rstd", [128, 4], F32)
        hn = sbh("hn", [128, 4, 1024], BF16)
        hnT = sbh("hnT", [128, 8, 512], BF16)
        TH = sbh("TH", [128, 4, 512], F32)
        QS = sbh("QS", [128, 4, 512], F32)
        SG = sbh("SG", [128, 4, 512], BF16)
        Vt = sbh("Vt", [128, 4, 512], BF16)
        KK = sbh("KK", [128, 4, 512], F32)
        CUM = sbh("CUM", [128, 4, 512], F32)
        E2 = sbh("E2", [128, 4, 512], F32)
        qdec = sbh("qdec", [128, 4, 512], BF16)
        kinvb = sbh("kinvb", [128, 4, 512], BF16)
        kend = sbh("kend", [128, 4, 512], BF16)
        kendT = sbh("kendT", [128, 4, 4, 128], BF16)
        attm = sbh("attm", [128, 4, 512], BF16)
        stS = [sbh(f"stS{i}", [128, 8, 128], F32) for i in range(4)]
        stC = sbh("stC", [128, 4, 128], F32)
        stB = [sbh(f"stB{i}", [128, 8, 128], BF16) for i in range(4)]
        SQ = sbh("SQ", [128, 512], BF16)
        LNR = sbh("LNR", [128, 512], F32)
        Rr = sbh("Rr", [128, 512], F32)
        Y1 = sbh("Y1", [128, 512], F32)
        yh = sbh("yh", [128, 4, 512], BF16)

        ptr = psh("ptr", [128, 1024], BF16)
        pp = [psh(f"pp{i}", [128, 512], F32) for i in range(2)]
        pa = psh("pa", [128, 512], F32)
        pkv = [psh(f"pkv{i}", [128, 4, 128], F32) for i in range(2)]
        po = [psh(f"po{i}", [128, 512], F32) for i in range(2)]

        PPT_LIST = [(pp[0], ("pp", 0)), (pp[1], ("pp", 1)), (pa, "pa"), (po[0], ("po", 0)), (po[1], ("po", 1))]
        TRB = [(ptr[:], "ptr"),
               (pkv[0].bitcast(BF16)[:].rearrange("p a b -> p (a b)"), ("pkv", 0)),
               (pkv[1].bitcast(BF16)[:].rearrange("p a b -> p (a b)"), ("pkv", 1)),
               (pp[0].bitcast(BF16)[:], ("pp", 0))]
        TH2 = TH[:].rearrange("p h t -> p (h t)")
        KK2 = KK[:].rearrange("p h t -> p (h t)")
        CUM2 = CUM[:].rearrange("p h t -> p (h t)")
        E22 = E2[:].rearrange("p h t -> p (h t)")
        QS2 = QS[:].rearrange("p h t -> p (h t)")
        THr = [("TH", h) for h in range(4)]
        KKr = [("KK", h) for h in range(4)]
        CUMr = [("CUM", h) for h in range(4)]
        E2r = [("E2", h) for h in range(4)]
        QSr = [("QS", h) for h in range(4)]

        ppi = [0]
        PPT = PPT_LIST

        def next_pp():
            i = ppi[0] % len(PPT)
            ppi[0] += 1
            return i

        def xr(b, j=None):
            return [("xt", b, jj) for jj in (range(4) if j is None else [j])]

        def load_x(st):
            b = st % 2
            S_.dma("sp", ("xt", b), W=xr(b), out=xt[b][:], in_=g.x_v[st])

        hnTr = [("hnT", j) for j in range(4)]

        def xnorm_pre(st):
            b = st % 2
            for j in range(4):
                S_.op("dve", "scalar_tensor_tensor", R=xr(b, j), W=["junk", ("ss", j)],
                      out=junk[:], in0=xt[b][:, j, :], scalar=1.0, in1=xt[b][:, j, :], op0=ALU.mult, op1=ALU.mult, accum_out=ss[:, j:j + 1])
            S_.op("act", "activation", R=[("ss", j) for j in range(4)] + ["CST"], W=["lnr4"],
                  out=lnr4[:], in_=ss[:], func=AF.Ln, scale=1.0 / D, bias=eps_ap)
            S_.op("act", "activation", R=["lnr4"], W=["rstd"], out=rstd[:], in_=lnr4[:], func=AF.Exp, scale=-0.5)
            for j in range(4):
                S_.op("dve", "tensor_scalar", R=xr(b, j) + ["rstd"], W=[("hn", j)],
                      out=hn[:, j, :], in0=xt[b][:, j, :], scalar1=rstd[:, j:j + 1], scalar2=None, op0=ALU.mult)

        def xnorm_T(st, j):
            tbk, tbr = TRB[j]
            for dc in range(8):
                S_.op("pe", "transpose", R=[("hn", j), "identb"], W=[tbr], acc=True, sig=(dc == 7),
                      out=tbk[:, dc * 128:(dc + 1) * 128], in_=hn[:, j, dc * 128:(dc + 1) * 128], identity=identb[:])
            S_.op("act", "activation", R=[tbr], W=[("hnT", j)],
                  out=hnT[:, :, j * 128:(j + 1) * 128], in_=tbk.rearrange("p (c t) -> p c t", t=128), func=AF.Copy)

        def xnorm_spill(st):
            if "A" in g.stages or "S" in g.stages:
                S_.dma("sp", "hnTo", R=hnTr, W=[("hnT_d", st)], out=g.hnT_d[st], in_=hnT[:].rearrange("p c t -> p (c t)"))

        def proj_fm(col0):
            i = next_pp()
            for dc in range(8):
                S_.op("pe", "matmul", R=[("WH", dc)] + hnTr, W=[PPT[i][1]], acc=True, sig=(dc == 7),
                      out=PPT[i][0][:], lhsT=WH[:, dc, col0:col0 + 128], rhs=hnT[:, dc, :], start=(dc == 0), stop=(dc == 7))
            return i

        def fproj():
            for h in range(4):
                i = proj_fm(512 + h * 128)
                S_.op("act", "activation", R=[PPT[i][1]], W=[("TH", h)], out=TH[:, h, :], in_=PPT[i][0][:], func=AF.Tanh, scale=0.5)
            for h in range(4):
                S_.op("dve", "tensor_scalar", R=[("TH", h), "lbA", "lbB"], W=[("TH", h)],
                      out=TH[:, h, :], in0=TH[:, h, :], scalar1=lbA[:, h:h + 1], scalar2=lbB[:, h:h + 1], op0=ALU.mult, op1=ALU.add)
                S_.op("dve", "tensor_scalar", R=[("TH", h)], W=[("KK", h)], out=KK[:, h, :], in0=TH[:, h, :], scalar1=-1.0, scalar2=1.0, op0=ALU.mult, op1=ALU.add)


        load_x(0)
        if NST > 1:
            load_x(1)
        xnorm_pre(0)
        for j in range(4):
            xnorm_T(0, j)
        xnorm_spill(0)

        stg = [(CUM2, CUMr), (E22, E2r), (KK2, KKr), (QS2, QSr)]
        for dc in range(8):
            w, wr = stg[dc % 4]
            S_.dma("sp" if dc % 2 == 0 else "act", ("wst", dc % 4), W=wr, out=w, in_=g.win_v[dc][:, 2048:4096])
            S_.op("dve", "tensor_scalar", R=wr + ["CST"], W=[("WH", dc)],
                  out=WH[:, dc, :], in0=w, scalar1=CST[:, C_G + dc:C_G + dc + 1], scalar2=None, op0=ALU.mult)
        for h in range(4):
            w, wr = stg[h % 4]
            S_.dma("sp" if h % 2 == 0 else "act", ("wst", h % 4), W=wr, out=w[:, 0:1024], in_=g.wout_v[4 + h])
            S_.op("dve", "tensor_copy", R=wr, W=[("WOh", h)], out=WOh[:, h, :], in_=w[:, 0:1024])
        S_.op("pool", "memset", W=[("stC", h) for h in range(4)], ap=stC[:], constant=0.0)

        for st in range(NST):
            b = st % 2
            nxt = st + 1 < NST
            fproj()
            for h in range(4):
                i = proj_fm(0 + h * 128)
                S_.op("act", "activation", R=[PPT[i][1]], W=[("QS", h)], out=QS[:, h, :], in_=PPT[i][0][:], func=AF.Silu)
            for h in range(4):
                i = proj_fm(1536 + h * 128)
                S_.op("act", "activation", R=[PPT[i][1]], W=[("SG", h)], out=SG[:, h, :], in_=PPT[i][0][:], func=AF.Silu)
            if nxt:
                xnorm_pre(st + 1)
            for h in range(4):
                S_.op("act", "activation", R=[("TH", h)], W=[("TH", h)], out=TH[:, h, :], in_=TH[:, h, :], func=AF.Ln)
                S_.op("dve", "tensor_tensor_scan", R=[("TH", h), "CST"], W=[("CUM", h)],
                      out=CUM[:, h, :], data0=CST[:, C_RST:C_RST + 512], data1=TH[:, h, :], initial=0.0, op0=ALU.mult, op1=ALU.add)
            for j in range(4):
                h = j
                i = next_pp()
                for dc in range(8):
                    S_.op("pe", "matmul", R=[("WH", dc), ("hnT", j)], W=[PPT[i][1]], acc=True, sig=(dc == 7),
                          out=PPT[i][0][:], lhsT=hnT[:, dc, j * 128:(j + 1) * 128], rhs=WH[:, dc, 1024:1536], start=(dc == 0), stop=(dc == 7))
                S_.op("dve", "tensor_copy", R=[PPT[i][1]], W=[("Vt", j)], out=Vt[:, j, :], in_=PPT[i][0][:])
                S_.op("act", "activation", R=[("CUM", h)], W=[("TH", h)], out=TH[:, h, :], in_=CUM[:, h, :], func=AF.Exp)
                S_.op("act", "activation", R=[("CUM", h)], W=[("E2", h)], out=E2[:, h, :], in_=CUM[:, h, :], func=AF.Exp, scale=-1.0)
                S_.op("dve", "tensor_tensor", R=[("QS", h), ("TH", h)], W=[("qdec", h)], out=qdec[:, h, :], in0=QS[:, h, :], in1=TH[:, h, :], op=ALU.mult)
                S_.op("dve", "tensor_tensor", R=[("KK", h), ("E2", h)], W=[("E2", h)], out=E2[:, h, :], in0=KK[:, h, :], in1=E2[:, h, :], op=ALU.mult)
            for h in range(4):
                S_.op("dve", "tensor_copy", R=[("E2", h)], W=[("kinvb", h)], out=kinvb[:, h, :], in_=E2[:, h, :])
                cd_bc = bass.AP(TH, h * 512 + 63, [[2048, 128], [64, 8], [0, 64]])
                S_.op("dve", "tensor_tensor", R=[("E2", h), ("TH", h)], W=[("kend", h)],
                      out=kend[:, h, :].rearrange("p (c t) -> p c t", t=64), in0=E2[:, h, :].rearrange("p (c t) -> p c t", t=64), in1=cd_bc, op=ALU.mult)
            if nxt:
                for j in range(4):
                    xnorm_T(st + 1, j)
                xnorm_spill(st + 1)

            for rnd in range(2):
                for hh in range(2):
                    h = rnd * 2 + hh
                    for j in range(4):
                        col = (hh * 4 + j) * 128
                        S_.op("pe", "transpose", R=[("kend", h), "identb"], W=[TRB[rnd][1]], acc=True, sig=(hh == 1 and j == 3),
                              out=TRB[rnd][0][:, col:col + 128], in_=kend[:, h, j * 128:(j + 1) * 128], identity=identb[:])
                S_.op("act", "activation", R=[TRB[rnd][1]], W=[("kendT", rnd * 2), ("kendT", rnd * 2 + 1)],
                      out=kendT[:, rnd * 2:rnd * 2 + 2, :, :].rearrange("p h j e -> p (h j e)"), in_=TRB[rnd][0], func=AF.Copy)

            def stage1(h):
                sbuf = h % 4
                for j in range(4):
                    S_.op("pe", "matmul", R=[("kinvb", h), ("qdec", h)], W=["pa"], acc=True, sig=(j == 3),
                          out=pa[:, j * 128:(j + 1) * 128], lhsT=kinvb[:, h, j * 128:(j + 1) * 128], rhs=qdec[:, h, j * 128:(j + 1) * 128], start=(j == 0), stop=(j == 3))
                S_.op("dve", "tensor_tensor", R=["pa"] + [("m2b", r) for r in range(4)], W=[("attm", h)],
                      out=attm[:, h, :], in0=pa[:], in1=m2b[:].rearrange("p c t -> p (c t)"), op=ALU.mult)
                S_.op("act", "activation", R=[("stC", h)], W=[("stB", sbuf, 0)], out=stB[sbuf][:, 0, :], in_=stC[:, h, :], func=AF.Copy)
                for c in range(8):
                    j = c // 2
                    r0 = (c % 2) * 64
                    pk = pkv[c % 2]
                    S_.op("pe", "matmul", R=[("kendT", h), ("Vt", j)], W=[("pkv", c % 2)], acc=True, sig=(c >= 6),
                          out=pk[:, c // 2, :], lhsT=kendT[r0:r0 + 64, h, j, :], rhs=Vt[r0:r0 + 64, j, h * 128:(h + 1) * 128], start=True, stop=True)
                for c in range(8):
                    src = stC[:, h, :] if c == 0 else stS[sbuf][:, c, :]
                    srcr = ("stC", h) if c == 0 else ("stS", sbuf, c)
                    dst = stC[:, h, :] if c == 7 else stS[sbuf][:, c + 1, :]
                    dstr = ("stC", h) if c == 7 else ("stS", sbuf, c + 1)
                    S_.op("dve", "scalar_tensor_tensor", R=[("pkv", c % 2), srcr, ("TH", h)], W=[dstr],
                          out=dst, in0=src, scalar=TH[:, h, c * 64 + 63:c * 64 + 64], in1=pkv[c % 2][:, c // 2, :], op0=ALU.mult, op1=ALU.add)
                S_.op("act", "activation", R=[("stS", sbuf, c) for c in range(1, 8)], W=[("stB", sbuf, 1)],
                      out=stB[sbuf][:, 1:8, :], in_=stS[sbuf][:, 1:8, :], func=AF.Copy)

            def stage2(h):
                sbuf = h % 4
                pob = po[h % 2]
                por = ("po", h % 2)
                for c in range(8):
                    j = c // 2
                    if c % 2 == 0:
                        S_.op("pe", "matmul", R=[("Vt", j), ("attm", h)], W=[por], acc=True,
                              out=pob[:, j * 128:(j + 1) * 128], lhsT=Vt[:, j, h * 128:(h + 1) * 128], rhs=attm[:, h, j * 128:(j + 1) * 128], start=(c == 0), stop=False)
                    S_.op("pe", "matmul", R=[("stB", sbuf, 0), ("stB", sbuf, 1), ("qdec", h)], W=[por], acc=True, sig=(c == 7),
                          out=pob[:, c * 64:(c + 1) * 64], lhsT=stB[sbuf][:, c, :], rhs=qdec[:, h, c * 64:(c + 1) * 64], start=False, stop=(c == 7))
                S_.op("act", "activation", R=[por], W=["SQ"], out=SQ[:], in_=pob[:], func=AF.Square)
                S_.op("pe", "matmul", R=["onesb", "SQ"], W=[("pp", 1)], sig=True, out=pp[1][:], lhsT=onesb[:], rhs=SQ[:], start=True, stop=True)
                S_.op("act", "activation", R=[("pp", 1), "CST"], W=["LNR"], out=LNR[:], in_=pp[1][:], func=AF.Ln, scale=1.0 / 128, bias=eps_ap)
                S_.op("act", "activation", R=["LNR"], W=["Rr"], out=Rr[:], in_=LNR[:], func=AF.Exp, scale=-0.5)
                S_.op("dve", "scalar_tensor_tensor", R=[por, "Rr", "CST"], W=["Y1"],
                      out=Y1[:], in0=pob[:], scalar=CST[:, C_HW + h:C_HW + h + 1], in1=Rr[:], op0=ALU.mult, op1=ALU.mult)
                S_.op("dve", "tensor_tensor", R=["Y1", ("SG", h)], W=[("yh", h)], out=yh[:, h, :], in0=Y1[:], in1=SG[:, h, :], op=ALU.mult)

            if "h1" in g.stages:
                pass
            elif "h2" in g.stages:
                for h in range(4):
                    stage1(h)
            else:
                stage1(0)
                stage1(1)
                stage1(2)
                stage1(3)
                stage2(0)
                stage2(1)
                stage2(2)
                stage2(3)

            for j in range(4 if ("h1" not in g.stages and "h2" not in g.stages) else 0):
                for dh in range(2):
                    i = next_pp()
                    for h in range(4):
                        S_.op("pe", "matmul", R=[("yh", h), ("WOh", h)], W=[PPT[i][1]], acc=True, sig=(h == 3),
                              out=PPT[i][0][:], lhsT=yh[:, h, j * 128:(j + 1) * 128], rhs=WOh[:, h, dh * 512:(dh + 1) * 512], start=(h == 0), stop=(h == 3))
                    S_.op("dve", "tensor_tensor", R=[PPT[i][1], ("xt", b, j)], W=[("xt", b, j)],
                          out=xt[b][:, j, dh * 512:(dh + 1) * 512], in0=PPT[i][0][:], in1=xt[b][:, j, dh * 512:(dh + 1) * 512], op=ALU.add)
            S_.dma("sp", ("parto", b), R=xr(b), W=[("out", st)], out=g.part_v[st], in_=xt[b][:])
            if st + 2 < NST:
                load_x(st + 2)


def _pass_ab(g):
    nc = g.nc; S_ = g.S_; NST = g.NST; CST = g.CST; identb = g.identb; onesb = g.onesb; eps_ap = g.eps_ap
    S = g.S
    c1, c2, c3 = _split_2pi()
    qs = contextlib.ExitStack()
    with qs:
        qT = qs.enter_context(nc.sbuf_tensor("qT", [128, 4, S], BF16))
        kT = qs.enter_context(nc.sbuf_tensor("kT", [128, 4, S], BF16))
        vT = qs.enter_context(nc.sbuf_tensor("vT", [128, 4, S], BF16))

        as_ = contextlib.ExitStack()
        with as_:
            def sba(name, shape, dt):
                return as_.enter_context(nc.sbuf_tensor(name, list(shape), dt))

            def psa(name, shape, dt):
                return as_.enter_context(nc.psum_tensor(name, list(shape), dt))

            WA = sba("WA", [128, 8, 2048], BF16)
            wst = [sba(f"wsta{i}", [128, 1024], F32) for i in range(4)]
            permb = sba("permb", [128, 128], BF16)
            hnT = [sba(f"hnTa{i}", [128, 8, 512], BF16) for i in range(2)]
            posi = sba("posi", [128, 512], I32)
            posf = sba("posf", [128, 512], F32)
            ang = sba("ang", [128, 512], F32)
            ki = sba("ki", [128, 512], I32)
            kf = sba("kf", [128, 512], F32)
            r1 = sba("r1", [128, 512], F32)
            r2 = sba("r2", [128, 512], F32)
            aab = sba("aab", [128, 512], F32)
            COSb = [sba(f"COS{i}", [128, 512], F32) for i in range(2)]
            SINb = [sba(f"SIN{i}", [128, 512], F32) for i in range(2)]
            Qb = [sba(f"Qb{i}", [128, 512], BF16) for i in range(2)]
            T1 = [sba(f"T1{i}", [128, 512], F32) for i in range(2)]
            T2 = [sba(f"T2{i}", [128, 512], F32) for i in range(2)]
            sgt = [sba(f"sgt{i}", [128, 512], BF16) for i in range(2)]
            pp = [psa(f"ppa{i}", [128, 512], F32) for i in range(6)]
            pq = [psa(f"pqa{i}", [128, 512], F32) for i in range(2)]

            S_.op("dve", "tensor_copy", R=["CST"], W=["permb"], out=permb[:], in_=CST[:, C_PERM:C_PERM + 128])
            for dc in range(8):
                for hf in range(2):
                    wi = (dc * 2 + hf) % 4
                    w = wst[wi]
                    S_.dma("sp" if wi % 2 == 0 else "act", ("wsta", wi), W=[("wsta", wi)], out=w[:], in_=g.win_v[dc][:, hf * 1024:(hf + 1) * 1024])
                    S_.op("dve", "tensor_scalar", R=[("wsta", wi), "CST"], W=[("WA", dc)],
                          out=WA[:, dc, hf * 1024:(hf + 1) * 1024], in0=w[:], scalar1=CST[:, C_G + dc:C_G + dc + 1], scalar2=None, op0=ALU.mult)

            ppi = [0]
            tb = [0]

            def load_hnT(st):
                hb_ = st % 2
                S_.dma("sp", ("hnTa", hb_), R=[("hnT_d", st)], W=[("hnTa", hb_)], out=hnT[hb_][:].rearrange("p c t -> p (c t)"), in_=g.hnT_d[st])

            def tables(st):
                tsl_ = slice(st * 512, (st + 1) * 512)
                COS_, SIN_ = COSb[st % 2], SINb[st % 2]
                cn, sn = ("COS", st % 2), ("SIN", st % 2)
                S_.dma("sp", "posi", W=["posi"], out=posi[:], in_=g.pos_d[:, tsl_])
                S_.op("dve", "tensor_copy", R=["posi"], W=["posf"], out=posf[:], in_=posi[:])
                S_.op("dve", "tensor_scalar", R=["posf", "CST"], W=["ang"], out=ang[:], in0=posf[:], scalar1=CST[:, C_INVF:C_INVF + 1], scalar2=None, op0=ALU.mult)
                S_.op("dve", "tensor_scalar", R=["ang"], W=["ki"], out=ki[:], in0=ang[:], scalar1=float(1.0 / TWO_PI), scalar2=None, op0=ALU.mult)
                S_.op("dve", "tensor_copy", R=["ki"], W=["kf"], out=kf[:], in_=ki[:])
                S_.op("dve", "scalar_tensor_tensor", R=["kf", "ang"], W=["r1"], out=r1[:], in0=kf[:], scalar=-c1, in1=ang[:], op0=ALU.mult, op1=ALU.add)
                S_.op("dve", "scalar_tensor_tensor", R=["kf", "r1"], W=["r2"], out=r2[:], in0=kf[:], scalar=-c2, in1=r1[:], op0=ALU.mult, op1=ALU.add)
                S_.op("dve", "scalar_tensor_tensor", R=["kf", "r2"], W=["r1"], out=r1[:], in0=kf[:], scalar=-c3, in1=r2[:], op0=ALU.mult, op1=ALU.add)
                S_.op("dve", "tensor_scalar", R=["r1"], W=["r2"], out=r2[:], in0=r1[:], scalar1=float(np.pi), scalar2=float(-np.pi), op0=ALU.min, op1=ALU.max)
                S_.op("dve", "scalar_tensor_tensor", R=["r2"], W=["aab"], out=aab[:], in0=r2[:], scalar=-1.0, in1=r2[:], op0=ALU.mult, op1=ALU.max)
                S_.op("act", "activation", R=["aab", "CST"], W=[cn], out=COS_[:], in_=aab[:], func=AF.Sin, scale=-1.0, bias=CST[:, C_HPI:C_HPI + 1])
                S_.op("act", "activation", R=["r2", "CST"], W=[sn], out=SIN_[:], in_=r2[:], func=AF.Sin, scale=CST[:, C_SGN:C_SGN + 1])

            load_hnT(0)
            tables(0)
            for st in range(NST if "a1" not in g.stages else 0):
                hb = st % 2
                tsl = slice(st * 512, (st + 1) * 512)
                if st + 1 < NST:
                    load_hnT(st + 1)
                COS, SIN = COSb[st % 2], SINb[st % 2]
                cosn, sinn = ("COS", st % 2), ("SIN", st % 2)

                hnTr = [("hnTa", hb)]
                if "a2" in g.stages:
                    continue

                def proj(col0):
                    i = ppi[0] % 6
                    ppi[0] += 1
                    for dc in range(8):
                        S_.op("pe", "matmul", R=[("WA", dc)] + hnTr, W=[("ppa", i)], acc=True, sig=(dc == 7),
                              out=pp[i][:], lhsT=WA[:, dc, col0:col0 + 128], rhs=hnT[hb][:, dc, :], start=(dc == 0), stop=(dc == 7))
                    return i

                pendq = []

                def rot(i_, t_, dst_, dname_, c_):
                    S_.op("pe", "matmul", R=["permb", ("Qb", t_)], W=[("pqa", t_)], sig=True,
                          out=pq[t_][:], lhsT=permb[:], rhs=Qb[t_][:], start=True, stop=True)
                    S_.op("dve", "tensor_tensor", R=[("Qb", t_), cosn], W=[("T1", t_)], out=T1[t_][:], in0=Qb[t_][:], in1=COS[:], op=ALU.mult)
                    S_.op("dve", "tensor_tensor", R=[("pqa", t_), sinn], W=[("T2", t_)], out=T2[t_][:], in0=pq[t_][:], in1=SIN[:], op=ALU.mult)
                    S_.op("pool", "tensor_tensor", R=[("T1", t_), ("T2", t_)], W=[(dname_, c_, st)], out=dst_[:, c_, tsl], in0=T1[t_][:], in1=T2[t_][:], op=ALU.add)

                for which, dst in ((0, qT), (1, kT)):
                    dname = "qT" if which == 0 else "kT"
                    for c in range(4):
                        i = proj(which * 512 + c * 128)
                        t = tb[0] % 2
                        tb[0] += 1
                        S_.op("act", "activation", R=[("ppa", i)], W=[("Qb", t)], out=Qb[t][:], in_=pp[i][:], func=AF.Copy)
                        if pendq:
                            rot(*pendq.pop())
                        pendq.append((i, t, dst, dname, c))
                if st + 1 < NST:
                    tables(st + 1)
                for c in range(4):
                    i = proj(1024 + c * 128)
                    S_.op("act", "activation", R=[("ppa", i)], W=[("vT", c, st)], out=vT[:, c, tsl], in_=pp[i][:], func=AF.Copy)
                    if pendq:
                        rot(*pendq.pop())
                for c in range(4 if "a5" not in g.stages else 0):
                    i = proj(1536 + c * 128)
                    t = (st * 4 + c) % 2
                    S_.op("act", "activation", R=[("ppa", i)], W=[("sgt", t)], out=sgt[t][:], in_=pp[i][:], func=AF.Silu)
                    S_.dma("sp", ("sgo", t), R=[("sgt", t)], W=[("sg_d", c, st)], out=g.sg_d[c][:, tsl], in_=sgt[t][:])

        if "B" not in g.stages:
            return
        S_.barrier()
        bs = contextlib.ExitStack()
        with bs:
            def sbb(name, shape, dt):
                return bs.enter_context(nc.sbuf_tensor(name, list(shape), dt))

            def psb_(name, shape, dt):
                return bs.enter_context(nc.psum_tensor(name, list(shape), dt))

            NVW, NPT, PVLAG = 5, 5, 2
            acc = sbb("accXY", [128, 2, S], F32)
            maskb = sbb("maskb", [128, 2, 2, 256], BF16)
            VW = [sbb(f"VW{i}", [128, 2, 256], BF16) for i in range(NVW)]
            PT = [sbb(f"PT{i}", [128, 2, 512], BF16) for i in range(NPT)]
            scx = sbb("scx", [128, 2], F32)
            SQX = sbb("SQX", [128, 512], BF16)
            SQY = sbb("SQY", [128, 512], BF16)
            LNR = sbb("LNRb", [128, 512], F32)
            Rr = sbb("Rrb", [128, 512], F32)
            Y1 = sbb("Y1b", [128, 512], F32)
            sgl = [sbb(f"sgl{i}", [128, 512], BF16) for i in range(2)]

            ptv = psb_("ptv", [128, 512], BF16)
            psc = [psb_(f"psc{i}", [128, 2, 512], F32) for i in range(2)]
            pxy = [psb_(f"pxy{i}", [128, 2, 256], F32) for i in range(2)]
            pU = psb_("pU", [128, 512], F32)
            PTV = [(ptv[:, 0:256], "ptv"), (pU.bitcast(BF16)[:, 0:256], "pU")]

            for hh in range(2):
                for kb in range(2):
                    S_.op("dve", "tensor_copy", R=["CST"], W=["maskb"], out=maskb[:, hh, kb, :], in_=CST[:, C_AMASK:C_AMASK + 256])
            for i in range(NVW):
                S_.op("pool", "memset", W=[("VW", i)], ap=VW[i][:], constant=1.0)
            se = float(np.sqrt(EPS))
            S_.op("pool", "memset", W=["scx"], ap=scx[0:64, 0:1], constant=1.0)
            S_.op("pool", "memset", W=["scx"], ap=scx[64:128, 0:1], constant=se)
            S_.op("pool", "memset", W=["scx"], ap=scx[0:64, 1:2], constant=se)
            S_.op("pool", "memset", W=["scx"], ap=scx[64:128, 1:2], constant=1.0)

            vwi = [0]
            pti = [0]
            gi = [0]
            SC = 1.0 / 8.0

            for c in range(4):
                first_pat = True
                pend = []
                for d in PATTERNS:
                    sub_len = S // d
                    n_blk = sub_len // 128
                    for r in range(d):
                        def tok(j, nb=1, r=r, d=d):
                            s0 = r + d * 128 * j
                            return slice(s0, s0 + d * (128 * nb - 1) + 1, d) if d > 1 else slice(s0, s0 + 128 * nb)

                        grp = {}

                        def emit_pv(j0, nkb, pbuf, vws, grp=grp, tok=tok, n_blk=n_blk, first_pat=first_pat):
                            for kb in range(nkb):
                                j = j0 + kb
                                gq = j // 2
                                if j % 2 == 0:
                                    segs = [(gq, 0, 0, min(2, n_blk - j))]
                                else:
                                    segs = [(gq, 1, 0, 1)]
                                    if j + 1 < n_blk:
                                        segs.append((gq + 1, 0, 1, 1))
                                for (gg, qpos, poff_blk, nb) in segs:
                                    if gg not in grp:
                                        tot = 2 * ((1 if gg > 0 else 0) + 1 + (1 if 2 * gg + 1 < n_blk else 0))
                                        grp[gg] = [gi[0] % 2, 0, tot]
                                        gi[0] += 1
                                    for hh in range(2):
                                        bank, cnt, tot = grp[gg]
                                        qoff = qpos * 128
                                        poff = kb * 256 + poff_blk * 128
                                        S_.op("pe", "matmul", R=[("VW", vws[kb]), ("PT", pbuf)], W=[("pxy", bank)], acc=True, sig=(cnt == tot - 1),
                                              out=pxy[bank][:, hh, qoff:qoff + nb * 128], lhsT=VW[vws[kb]][:, kb, hh * 128:(hh + 1) * 128],
                                              rhs=PT[pbuf][:, hh, poff:poff + nb * 128], start=(cnt == 0), stop=(cnt == tot - 1), skip_group_check=True)
                                        grp[gg][1] += 1
                                        if grp[gg][1] == tot:
                                            nqb = min(2, n_blk - 2 * gg)
                                            dstap = acc[:, :, tok(2 * gg, nqb)]
                                            if first_pat:
                                                S_.op("dve", "tensor_copy", R=[("pxy", bank)], W=[("acc", c)], out=dstap, in_=pxy[bank][:, :, 0:nqb * 128])
                                            else:
                                                S_.op("dve", "tensor_tensor", R=[("pxy", bank), ("acc", c)], W=[("acc", c)], out=dstap, in0=pxy[bank][:, :, 0:nqb * 128], in1=dstap, op=ALU.add)

                        for j0 in range(0, n_blk, 2):
                            nkb = min(2, n_blk - j0)
                            sbuf = (j0 // 2) % 2
                            pbuf = pti[0] % NPT
                            pti[0] += 1
                            vws = []
                            nqs = []
                            for kb in range(nkb):
                                j = j0 + kb
                                nq = 2 if j + 1 < n_blk else 1
                                nqs.append(nq)
                                for hh in range(2):
                                    S_.op("pe", "matmul", R=[("kT", c, s_) for s_ in range(NST)] + [("qT", c, s_) for s_ in range(NST)], W=[("psc", sbuf)], acc=True, sig=(kb == nkb - 1 and hh == 1),
                                          out=psc[sbuf][:, hh, kb * 256:kb * 256 + nq * 128], lhsT=kT[hh * 64:(hh + 1) * 64, c, tok(j)],
                                          rhs=qT[hh * 64:(hh + 1) * 64, c, tok(j, nq)], start=True, stop=True)
                            tvh = pti[0] % 2
                            for kb in range(nkb):
                                j = j0 + kb
                                S_.op("pe", "transpose", R=[("vT", c, s_) for s_ in range(NST)] + ["identb"], W=[PTV[tvh][1]], acc=True, sig=(kb == nkb - 1),
                                      out=PTV[tvh][0][:, kb * 128:(kb + 1) * 128], in_=vT[:, c, tok(j)], identity=identb[:])
                            ncol = (nkb - 1) * 256 + nqs[-1] * 128
                            S_.op("act", "activation", R=[("psc", sbuf)], W=[("PT", pbuf)], out=PT[pbuf][:, :, 0:ncol], in_=psc[sbuf][:, :, 0:ncol], func=AF.Exp, scale=SC)
                            meng = "dve"
                            S_.op(meng, "tensor_tensor", R=[("PT", pbuf), "maskb"], W=[("PT", pbuf)], out=PT[pbuf][:, :, 0:ncol], in0=PT[pbuf][:, :, 0:ncol],
                                  in1=maskb[:].rearrange("p h k q -> p h (k q)")[:, :, 0:ncol], op=ALU.mult)
                            v = vwi[0] % NVW
                            vwi[0] += 1
                            vws = [v] * nkb
                            dst = bass.AP(VW[v], 0, [[512, 128], [256, nkb], [192, 2], [1, 64]])
                            S_.op("act", "activation", R=[PTV[tvh][1]], W=[("VW", v)], out=dst,
                                  in_=PTV[tvh][0][:, 0:nkb * 128].rearrange("p (k h e) -> p k h e", h=2, e=64), func=AF.Copy)
                            pend.append((emit_pv, (j0, nkb, pbuf, vws)))
                            if len(pend) > PVLAG:
                                f_, a_ = pend.pop(0)
                                f_(*a_)
                    first_pat = False
                while pend:
                    f_, a_ = pend.pop(0)
                    f_(*a_)

                for st in range(NST):
                    tsl = slice(st * 512, (st + 1) * 512)
                    t = st % 2
                    S_.dma("sp", ("sgl", t), R=[("sg_d", c, st)], W=[("sgl", t)], out=sgl[t][:], in_=g.sg_d[c][:, tsl])
                    S_.op("act", "activation", R=[("acc", c), "scx"], W=["SQX"], out=SQX[:], in_=acc[:, 0, tsl], func=AF.Square, scale=scx[:, 0:1])
                    S_.op("act", "activation", R=[("acc", c), "scx"], W=["SQY"], out=SQY[:], in_=acc[:, 1, tsl], func=AF.Square, scale=scx[:, 1:2])
                    S_.op("pe", "matmul", R=["onesb", "SQX"], W=["pU"], out=pU[0:64, :], lhsT=onesb[:, 0:64], rhs=SQX[:], start=True, stop=True)
                    S_.op("pe", "matmul", R=["onesb", "SQY"], W=["pU"], sig=True, out=pU[64:128, :], lhsT=onesb[:, 0:64], rhs=SQY[:], start=True, stop=True)
                    S_.op("act", "activation", R=["pU"], W=["LNRb"], out=LNR[:], in_=pU[:], func=AF.Ln, scale=1.0 / 64)
                    S_.op("act", "activation", R=["LNRb"], W=["Rrb"], out=Rr[:], in_=LNR[:], func=AF.Exp, scale=-0.5)
                    S_.op("dve", "scalar_tensor_tensor", R=[("acc", c), "Rrb", "CST"], W=["Y1b"],
                          out=Y1[0:64, :], in0=acc[0:64, 0, tsl], scalar=CST[0:64, C_AW + c:C_AW + c + 1], in1=Rr[0:64, :], op0=ALU.mult, op1=ALU.mult)
                    S_.op("dve", "scalar_tensor_tensor", R=[("acc", c), "Rrb", "CST"], W=["Y1b"],
                          out=Y1[64:128, :], in0=acc[64:128, 1, tsl], scalar=CST[64:128, C_AW + c:C_AW + c + 1], in1=Rr[64:128, :], op0=ALU.mult, op1=ALU.mult)
                    S_.op("pool", "tensor_tensor", R=["Y1b", ("sgl", t)] , W=[("qT", c, st)], out=qT[:, c, tsl], in0=Y1[:], in1=sgl[t][:], op=ALU.mult)

        if "C" in g.stages:
            S_.barrier()
            _pass_c_inner(g, qT)


def _pass_c_inner(g, qT):
    nc = g.nc; S_ = g.S_; NST = g.NST; CST = g.CST; eps_ap = g.eps_ap
    cs = contextlib.ExitStack()
    with cs:
        def sbc(name, shape, dt):
            return cs.enter_context(nc.sbuf_tensor(name, list(shape), dt))

        def psc_(name, shape, dt):
            return cs.enter_context(nc.psum_tensor(name, list(shape), dt))

        WOa = sbc("WOa", [128, 4, 1024], BF16)
        wst = [sbc(f"wstc{i}", [128, 1024], F32) for i in range(2)]
        fnw = sbc("fnw_sb", [128, D], F32)
        zt = [sbc(f"zt{i}", [128, 4, 1024], F32) for i in range(2)]
        junk = sbc("junkc", [128, 1024], BF16)
        ss = sbc("ssc", [128, 4], F32)
        lnr4 = sbc("lnr4c", [128, 4], F32)
        rstd = sbc("rstdc", [128, 4], F32)
        pc = [psc_(f"pc{i}", [128, 512], F32) for i in range(8)]

        S_.dma("sp", "fnw", W=["fnw"], out=fnw[:], in_=g.fnw_d)
        for c in range(4):
            w = wst[c % 2]
            S_.dma("sp", ("wstc", c % 2), W=[("wstc", c % 2)], out=w[:], in_=g.wout_v[c])
            S_.op("dve", "tensor_copy", R=[("wstc", c % 2)], W=[("WOa", c)], out=WOa[:, c, :], in_=w[:])

        def load_part(st):
            b = st % 2
            S_.dma("sp", ("zt", b), R=[("out", st)], W=[("zt", b, j) for j in range(4)], out=zt[b][:], in_=g.part_v[st])

        pci = [0]
        load_part(0)
        for st in range(NST):
            b = st % 2
            if st + 1 < NST:
                load_part(st + 1)
            for j in range(4):
                tok = slice(st * 512 + j * 128, st * 512 + (j + 1) * 128)
                for dh in range(2):
                    i = pci[0] % 8
                    pci[0] += 1
                    for c in range(4):
                        S_.op("pe", "matmul", R=[("qT", c, st), ("WOa", c)], W=[("pc", i)], acc=True, sig=(c == 3),
                              out=pc[i][:], lhsT=qT[:, c, tok], rhs=WOa[:, c, dh * 512:(dh + 1) * 512], start=(c == 0), stop=(c == 3))
                    S_.op("dve", "tensor_tensor", R=[("pc", i), ("zt", b, j)], W=[("zt", b, j)],
                          out=zt[b][:, j, dh * 512:(dh + 1) * 512], in0=pc[i][:], in1=zt[b][:, j, dh * 512:(dh + 1) * 512], op=ALU.add)
                S_.op("act", "activation", R=[("zt", b, j)], W=["junkc", ("ssc", j)],
                      out=junk[:], in_=zt[b][:, j, :], func=AF.Square, accum_out=ss[:, j:j + 1])
            S_.op("act", "activation", R=[("ssc", j) for j in range(4)] + ["CST"], W=["lnr4c"], out=lnr4[:], in_=ss[:], func=AF.Ln, scale=1.0 / D, bias=eps_ap)
            S_.op("act", "activation", R=["lnr4c"], W=["rstdc"], out=rstd[:], in_=lnr4[:], func=AF.Exp, scale=-0.5)
            for j in range(4):
                S_.op("dve", "scalar_tensor_tensor", R=[("zt", b, j), "rstdc", "fnw"], W=[("zt", b, j)],
                      out=zt[b][:, j, :], in0=zt[b][:, j, :], scalar=rstd[:, j:j + 1], in1=fnw[:], op0=ALU.mult, op1=ALU.mult)
            S_.dma("sp", ("outo", b), R=[("zt", b, j) for j in range(4)], W=[("out", st)], out=g.out_v[st], in_=zt[b][:])


def prep_core_inputs(inputs, b, S=4096):
    cst = _const_table()
    cst[:, C_G:C_G + 8] = np.asarray(inputs["mix_norm_w"], np.float32)[0].reshape(8, 128).T
    cst[:, C_AW:C_AW + 4] = np.asarray(inputs["attn_out_norm_w"], np.float32)[0].reshape(4, 128).T
    cst[:, C_HW:C_HW + 4] = np.asarray(inputs["hgrn_out_norm_w"], np.float32)[0].reshape(4, 128).T
    lb = np.asarray(inputs["hgrn_lb_raw"], np.float32)
    cst[:, C_LB0:C_LB0 + 4] = lb[0].reshape(4, 128).T
    cst[:, C_LB1:C_LB1 + 4] = lb[1].reshape(4, 128).T
    pos = np.asarray(inputs["positions"])[b, :S].astype(np.int32)
    return {
        "x": np.ascontiguousarray(np.asarray(inputs["x"], np.float32)[b, :S]),
        "pos": np.ascontiguousarray(np.broadcast_to(pos[None, :], (128, S))),
        "w_in": np.ascontiguousarray(np.asarray(inputs["w_in"], np.float32)[0]),
        "w_out": np.ascontiguousarray(np.asarray(inputs["w_out"], np.float32)[0]),
        "cst": cst,
        "fnw": np.ascontiguousarray(np.broadcast_to(np.asarray(inputs["final_norm_w"], np.float32)[None, :], (128, D))),
    }


_NC_CACHE = {}


def kernel(x, positions, w_in, w_out, mix_norm_w, attn_out_norm_w, hgrn_out_norm_w, hgrn_lb_raw, final_norm_w):
    inputs = dict(x=x, positions=positions, w_in=w_in, w_out=w_out, mix_norm_w=mix_norm_w,
                  attn_out_norm_w=attn_out_norm_w, hgrn_out_norm_w=hgrn_out_norm_w,
                  hgrn_lb_raw=hgrn_lb_raw, final_norm_w=final_norm_w)
    B, S, _ = np.asarray(x).shape
    if S not in _NC_CACHE:
        _NC_CACHE[S] = build(S)
    nc = _NC_CACHE[S]
    in_maps = [prep_core_inputs(inputs, b, S) for b in range(B)]
    res = run_bass_kernel_spmd(nc, in_maps, core_ids=list(range(B)))
    return np.stack([np.asarray(r["out"], np.float32) for r in res.results], axis=0)
```

```python
import contextlib
import numpy as np
import concourse.bass as bass
import concourse.mybir as mybir
from concourse.bass_utils import run_bass_kernel_spmd

F32 = mybir.dt.float32
BF16 = mybir.dt.bfloat16
I32 = mybir.dt.int32
AF = mybir.ActivationFunctionType
ALU = mybir.AluOpType

D = 1024
EPS = 1e-6
ROPE_THETA = 500000.0
PATTERNS = (1, 4, 16)
TWO_PI = 2.0 * np.pi


def _split_2pi():
    c1 = 6.28125
    r = TWO_PI - c1
    c2 = np.float32(r)
    c2 = (c2.view(np.uint32) & np.uint32(0xFFFFF000)).view(np.float32)
    c3 = np.float32(r - float(c2))
    return float(c1), float(c2), float(c3)


C_G = 0
C_AW = 8
C_HW = 12
C_LB0 = 16
C_LB1 = 20
C_INVF = 24
C_SGN = 25
C_HPI = 26
C_EPS = 27
C_ONE = 28
C_IDENT = 32
C_PERM = 160
C_AMASK = 288
C_M2 = 544
C_RST = 672
NCST = 1184


def _const_table():
    c = np.zeros((128, NCST), np.float32)
    p = np.arange(128)
    e = p % 64
    half = 8
    invf = ROPE_THETA ** (-np.arange(half, dtype=np.float64) * (2.0 / 16.0))
    c[:, C_INVF] = np.where(e < 16, invf[e % 8], 0.0)
    c[:, C_SGN] = np.where(e < 8, -1.0, np.where(e < 16, 1.0, 0.0))
    c[:, C_HPI] = np.pi / 2
    c[:, C_EPS] = EPS
    c[:, C_ONE] = 1.0
    c[:, C_IDENT:C_IDENT + 128] = np.eye(128)
    perm = np.zeros((128, 128), np.float32)
    for m in range(128):
        em = m % 64
        if em < 8:
            perm[m + 8, m] = 1.0
        elif em < 16:
            perm[m - 8, m] = 1.0
    c[:, C_PERM:C_PERM + 128] = perm
    k = np.arange(128)[:, None]
    q = np.arange(128)[None, :]
    am = np.concatenate([(q >= k), (q <= k)], axis=1).astype(np.float32)
    c[:, C_AMASK:C_AMASK + 256] = am
    s = np.arange(128)[:, None]
    t = np.arange(128)[None, :]
    c[:, C_M2:C_M2 + 128] = ((s // 64 == t // 64) & (s <= t)).astype(np.float32)
    rst = np.ones(512, np.float32)
    rst[0::64] = 0.0
    c[:, C_RST:C_RST + 512] = rst[None, :]
    return c


class Sched:
    COMPUTE = ("pe", "act", "dve", "pool")

    def __init__(self, nc, stack):
        self.nc = nc
        self.stack = stack
        self.streams = {k: [] for k in ("pe", "act", "dve", "pool", "sp")}
        self.sems = {}
        self.cnt = {}
        self.sigs = {k: [] for k in self.COMPUTE}
        self.waited = {k: {} for k in self.streams}
        self.last_w = {}
        self.readers = {}
        for k in self.COMPUTE:
            self._sem(k)

    def _sem(self, name):
        if name not in self.sems:
            self.sems[name] = self.stack.enter_context(self.nc.semaphore("s_" + "".join(ch if ch.isalnum() else "_" for ch in str(name))))
            self.cnt[name] = 0
        return self.sems[name]

    def _resolve(self, ev):
        kind = ev[0]
        if kind == "dma":
            return ev[1], ev[2]
        eng, seq = ev[1], ev[2]
        best = None
        for s, v in reversed(self.sigs[eng]):
            if s >= seq:
                best = v
            else:
                break
        if best is not None:
            return eng, best
        rec = self.streams[eng][-1]
        assert rec["seq"] >= seq
        self.cnt[eng] += 1
        rec["inc"] = True
        self.sigs[eng].append((rec["seq"], self.cnt[eng]))
        return eng, self.cnt[eng]

    def _collect(self, eng, reads, writes, acc):
        evs = []
        for r in reads:
            ev = self.last_w.get(r)
            if ev is not None:
                evs.append(ev)
        for w in writes:
            ev = self.last_w.get(w)
            if ev is not None and not (acc and ev[0] == "c" and ev[1] == eng):
                evs.append(ev)
            for ev in self.readers.get(w, ()):
                evs.append(ev)
        waits = {}
        for ev in evs:
            name, val = self._resolve(ev)
            if self.waited[eng].get(name, 0) >= val:
                continue
            waits[name] = max(waits.get(name, 0), val)
        for name, val in waits.items():
            self.waited[eng][name] = val
        return list(waits.items())

    def _commit(self, ev, reads, writes):
        for r in reads:
            self.readers.setdefault(r, []).append(ev)
        for w in writes:
            self.last_w[w] = ev
            self.readers[w] = []

    def op(self, eng, method, R=(), W=(), sig=None, acc=False, **kw):
        if sig is None:
            sig = (eng != "pe")
        reads, writes = list(R), list(W)
        fn = (lambda e, method=method, kw=kw: getattr(e, method)(**kw))
        waits = self._collect(eng, reads, writes, acc)
        seq = len(self.streams[eng])
        rec = {"fn": fn, "waits": waits, "inc": False, "seq": seq, "dma": None}
        self.streams[eng].append(rec)
        if sig:
            self.cnt[eng] += 1
            rec["inc"] = True
            self.sigs[eng].append((seq, self.cnt[eng]))
        self._commit(("c", eng, seq), reads, writes)

    def dma(self, q, key, R=(), W=(), **kw):
        reads, writes = list(R), list(W)
        fn = (lambda e, kw=kw: e.dma_start(**kw))
        name = ("dma", key)
        self._sem(name)
        waits = self._collect(q, reads, writes, False)
        self.cnt[name] += 16
        rec = {"fn": fn, "waits": waits, "inc": False, "seq": len(self.streams[q]), "dma": name}
        self.streams[q].append(rec)
        self._commit(("dma", name, self.cnt[name]), reads, writes)

    def barrier(self):
        for eng in self.COMPUTE:
            if self.streams[eng]:
                rec = self.streams[eng][-1]
                if not rec["inc"]:
                    self.cnt[eng] += 1
                    rec["inc"] = True
                    self.sigs[eng].append((rec["seq"], self.cnt[eng]))
        for eng in self.streams:
            waits = []
            for name, val in self.cnt.items():
                if val <= 0 or name == eng:
                    continue
                if self.waited[eng].get(name, 0) >= val:
                    continue
                self.waited[eng][name] = val
                waits.append((name, val))
            self.streams[eng].append({"fn": None, "waits": waits, "inc": False, "seq": len(self.streams[eng]), "dma": None})

    def final_wait(self, q):
        waits = {}
        for name, val in self.cnt.items():
            if isinstance(name, tuple) and name[0] == "dma" and val > 0:
                waits[name] = val
        rec = {"fn": None, "waits": list(waits.items()), "inc": False, "seq": len(self.streams[q]), "dma": None}
        self.streams[q].append(rec)

    def emit(self):
        nc = self.nc
        sems = self.sems

        def replay(name):
            def run(e):
                for rec in self.streams[name]:
                    for sname, val in rec["waits"]:
                        e.wait_ge(sems[sname], val)
                    if rec["fn"] is None:
                        continue
                    ins = rec["fn"](e)
                    if rec["dma"] is not None:
                        ins.then_inc(sems[rec["dma"]], 16)
                    elif rec["inc"]:
                        ins.then_inc(sems[name], 1)
            return run

        with nc.Block() as block:
            block.tensor(replay("pe"))
            block.scalar(replay("act"))
            block.vector(replay("dve"))
            block.gpsimd(replay("pool"))
            block.sync(replay("sp"))


class Ctx:
    pass


def build(S=4096, stages=("H", "A", "B", "C"), dbg=()):
    g = Ctx()
    g.S = S
    g.NST = S // 512
    g.NT = S // 128
    g.stages = stages
    nc = bass.Bass("TRN2", target_bir_lowering=False)
    g.nc = nc
    x_d = nc.dram_tensor("x", [S, D], F32, kind="ExternalInput").ap()
    g.pos_d = nc.dram_tensor("pos", [128, S], I32, kind="ExternalInput").ap()
    win_d = nc.dram_tensor("w_in", [D, 4096], F32, kind="ExternalInput").ap()
    wout_d = nc.dram_tensor("w_out", [D, D], F32, kind="ExternalInput").ap()
    cst_d = nc.dram_tensor("cst", [128, NCST], F32, kind="ExternalInput").ap()
    g.fnw_d = nc.dram_tensor("fnw", [128, D], F32, kind="ExternalInput").ap()
    out_d = nc.dram_tensor("out", [S, D], F32, kind="ExternalOutput").ap()
    part_d = out_d if "C" in stages else nc.dram_tensor("part_scr", [S, D], F32, kind="ExternalOutput").ap()
    g.hnT_d = nc.dram_tensor("hnT_scr", [g.NST, 128, 8 * 512], BF16, kind="Internal").ap()
    g.sg_d = nc.dram_tensor("sg_scr", [4, 128, S], BF16, kind="Internal").ap()
    g.dbg_d = {}
    for name, shape in dbg:
        g.dbg_d[name] = nc.dram_tensor(name, list(shape), F32, kind="ExternalOutput").ap()

    g.x_v = x_d.rearrange("(n j p) d -> n p j d", j=4, p=128)
    g.part_v = part_d.rearrange("(n j p) d -> n p j d", j=4, p=128)
    g.out_v = out_d.rearrange("(n j p) d -> n p j d", j=4, p=128)
    g.win_v = win_d.rearrange("(dc p) c -> dc p c", p=128)
    g.wout_v = wout_d.rearrange("(mc p) c -> mc p c", p=128)

    stack = contextlib.ExitStack()
    with stack:
        S_ = Sched(nc, stack)
        g.S_ = S_

        def sb(name, shape, dt):
            return stack.enter_context(nc.sbuf_tensor(name, list(shape), dt))

        g.CST = CST = sb("CST", [128, NCST], F32)
        g.identb = identb = sb("identb", [128, 128], BF16)
        g.onesb = onesb = sb("onesb", [128, 128], BF16)
        g.m2b = m2b = sb("m2b", [128, 4, 128], BF16)
        g.lbA = lbA = sb("lbA", [128, 4], F32)
        g.lbB = lbB = sb("lbB", [128, 4], F32)
        tmp4 = sb("tmp4", [128, 4], F32)
        tmp4b = sb("tmp4b", [128, 4], F32)

        S_.dma("sp", "cst", W=["CST"], out=CST[:], in_=cst_d)
        S_.op("dve", "tensor_copy", R=["CST"], W=["identb"], out=identb[:], in_=CST[:, C_IDENT:C_IDENT + 128])
        S_.op("dve", "memset", W=["onesb"], ap=onesb[:], constant=1.0)
        for r in range(4):
            S_.op("dve", "tensor_copy", R=["CST"], W=[("m2b", r)], out=m2b[:, r, :], in_=CST[:, C_M2:C_M2 + 128])
        S_.op("dve", "tensor_tensor", R=["CST"], W=["tmp4"], out=tmp4[:], in0=CST[:, C_LB0:C_LB0 + 4], in1=CST[:, C_LB1:C_LB1 + 4], op=ALU.subtract)
        S_.op("act", "activation", R=["tmp4"], W=["tmp4b"], out=tmp4b[:], in_=tmp4[:], func=AF.Tanh, scale=0.5)
        S_.op("dve", "tensor_scalar", R=["tmp4b"], W=["lbA"], out=lbA[:], in0=tmp4b[:], scalar1=-0.25, scalar2=0.25, op0=ALU.mult, op1=ALU.add)
        S_.op("dve", "tensor_scalar", R=["tmp4b"], W=["lbB"], out=lbB[:], in0=tmp4b[:], scalar1=0.25, scalar2=0.75, op0=ALU.mult, op1=ALU.add)
        g.eps_ap = CST[:, C_EPS:C_EPS + 1]

        if "H" in stages:
            _pass_h(g)
        if "A" in stages:
            S_.barrier()
            _pass_ab(g)

        S_.final_wait("sp")
        S_.emit()
    return nc


def _pass_h(g):
    nc = g.nc; S_ = g.S_; NST = g.NST; CST = g.CST; identb = g.identb; onesb = g.onesb; m2b = g.m2b
    lbA = g.lbA; lbB = g.lbB; eps_ap = g.eps_ap

    hs = contextlib.ExitStack()
    with hs:
        def sbh(name, shape, dt):
            return hs.enter_context(nc.sbuf_tensor(name, list(shape), dt))

        def psh(name, shape, dt):
            return hs.enter_context(nc.psum_tensor(name, list(shape), dt))

        WH = sbh("WH", [128, 8, 2048], BF16)
        WOh = sbh("WOh", [128, 4, 1024], BF16)
        xt = [sbh(f"xt{i}", [128, 4, 1024], F32) for i in range(2)]
        junk = sbh("junk", [128, 1024], BF16)
        ss = sbh("ss", [128, 4], F32)
        lnr4 = sbh("lnr4", [128, 4], F32)
        rstd = sbh("rstd", [128, 4], F32)
        hn = sbh("hn", [128, 4, 1024], BF16)
        hnT = sbh("hnT", [128, 8, 512], BF16)
        TH = sbh("TH", [128, 4, 512], F32)
        QS = sbh("QS", [128, 4, 512], F32)
        SG = sbh("SG", [128, 4, 512], BF16)
        Vt = sbh("Vt", [128, 4, 512], BF16)
        KK = sbh("KK", [128, 4, 512], F32)
        CUM = sbh("CUM", [128, 4, 512], F32)
        E2 = sbh("E2", [128, 4, 512], F32)
        qdec = sbh("qdec", [128, 4, 512], BF16)
        kinvb = sbh("kinvb", [128, 4, 512], BF16)
        kend = sbh("kend", [128, 4, 512], BF16)
        kendT = sbh("kendT", [128, 4, 4, 128], BF16)
        attm = sbh("attm", [128, 4, 512], BF16)
        stS = [sbh(f"stS{i}", [128, 8, 128], F32) for i in range(4)]
        stC = sbh("stC", [128, 4, 128], F32)
        stB = [sbh(f"stB{i}", [128, 8, 128], BF16) for i in range(4)]
        SQ = sbh("SQ", [128, 512], BF16)
        LNR = sbh("LNR", [128, 512], F32)
        Rr = sbh("Rr", [128, 512], F32)
        Y1 = sbh("Y1", [128, 512], F32)
        yh = sbh("yh", [128, 4, 512], BF16)

        ptr = psh("ptr", [128, 1024], BF16)
        pp = [psh(f"pp{i}", [128, 512], F32) for i in range(2)]
        pa = psh("pa", [128, 512], F32)
        pkv = [psh(f"pkv{i}", [128, 4, 128], F32) for i in range(2)]
        po = [psh(f"po{i}", [128, 512], F32) for i in range(2)]

        PPT_LIST = [(pp[0], ("pp", 0)), (pp[1], ("pp", 1)), (pa, "pa"), (po[0], ("po", 0)), (po[1], ("po", 1))]
        TRB = [(ptr[:], "ptr"),
               (pkv[0].bitcast(BF16)[:].rearrange("p a b -> p (a b)"), ("pkv", 0)),
               (pkv[1].bitcast(BF16)[:].rearrange("p a b -> p (a b)"), ("pkv", 1)),
               (pp[0].bitcast(BF16)[:], ("pp", 0))]
        TH2 = TH[:].rearrange("p h t -> p (h t)")
        KK2 = KK[:].rearrange("p h t -> p (h t)")
        CUM2 = CUM[:].rearrange("p h t -> p (h t)")
        E22 = E2[:].rearrange("p h t -> p (h t)")
        QS2 = QS[:].rearrange("p h t -> p (h t)")
        THr = [("TH", h) for h in range(4)]
        KKr = [("KK", h) for h in range(4)]
        CUMr = [("CUM", h) for h in range(4)]
        E2r = [("E2", h) for h in range(4)]
        QSr = [("QS", h) for h in range(4)]

        ppi = [0]
        PPT = PPT_LIST

        def next_pp():
            i = ppi[0] % len(PPT)
            ppi[0] += 1
            return i

        def xr(b, j=None):
            return [("xt", b, jj) for jj in (range(4) if j is None else [j])]

        def load_x(st):
            b = st % 2
            S_.dma("sp", ("xt", b), W=xr(b), out=xt[b][:], in_=g.x_v[st])

        hnTr = [("hnT", j) for j in range(4)]

        def xnorm_pre(st):
            b = st % 2
            for j in range(4):
                S_.op("dve", "scalar_tensor_tensor", R=xr(b, j), W=["junk", ("ss", j)],
                      out=junk[:], in0=xt[b][:, j, :], scalar=1.0, in1=xt[b][:, j, :], op0=ALU.mult, op1=ALU.mult, accum_out=ss[:, j:j + 1])
            S_.op("act", "activation", R=[("ss", j) for j in range(4)] + ["CST"], W=["lnr4"],
                  out=lnr4[:], in_=ss[:], func=AF.Ln, scale=1.0 / D, bias=eps_ap)
            S_.op("act", "activation", R=["lnr4"], W=["rstd"], out=rstd[:], in_=lnr4[:], func=AF.Exp, scale=-0.5)
            for j in range(4):
                S_.op("dve", "tensor_scalar", R=xr(b, j) + ["rstd"], W=[("hn", j)],
                      out=hn[:, j, :], in0=xt[b][:, j, :], scalar1=rstd[:, j:j + 1], scalar2=None, op0=ALU.mult)

        def xnorm_T(st, j):
            tbk, tbr = TRB[j]
            for dc in range(8):
                S_.op("pe", "transpose", R=[("hn", j), "identb"], W=[tbr], acc=True, sig=(dc == 7),
                      out=tbk[:, dc * 128:(dc + 1) * 128], in_=hn[:, j, dc * 128:(dc + 1) * 128], identity=identb[:])
            S_.op("act", "activation", R=[tbr], W=[("hnT", j)],
                  out=hnT[:, :, j * 128:(j + 1) * 128], in_=tbk.rearrange("p (c t) -> p c t", t=128), func=AF.Copy)

        def xnorm_spill(st):
            if "A" in g.stages or "S" in g.stages:
                S_.dma("sp", "hnTo", R=hnTr, W=[("hnT_d", st)], out=g.hnT_d[st], in_=hnT[:].rearrange("p c t -> p (c t)"))

        def proj_fm(col0):
            i = next_pp()
            for dc in range(8):
                S_.op("pe", "matmul", R=[("WH", dc)] + hnTr, W=[PPT[i][1]], acc=True, sig=(dc == 7),
                      out=PPT[i][0][:], lhsT=WH[:, dc, col0:col0 + 128], rhs=hnT[:, dc, :], start=(dc == 0), stop=(dc == 7))
            return i

        def fproj():
            for h in range(4):
                i = proj_fm(512 + h * 128)
                S_.op("act", "activation", R=[PPT[i][1]], W=[("TH", h)], out=TH[:, h, :], in_=PPT[i][0][:], func=AF.Tanh, scale=0.5)
            for h in range(4):
                S_.op("dve", "tensor_scalar", R=[("TH", h), "lbA", "lbB"], W=[("TH", h)],
                      out=TH[:, h, :], in0=TH[:, h, :], scalar1=lbA[:, h:h + 1], scalar2=lbB[:, h:h + 1], op0=ALU.mult, op1=ALU.add)
                S_.op("dve", "tensor_scalar", R=[("TH", h)], W=[("KK", h)], out=KK[:, h, :], in0=TH[:, h, :], scalar1=-1.0, scalar2=1.0, op0=ALU.mult, op1=ALU.add)


        load_x(0)
        if NST > 1:
            load_x(1)
        xnorm_pre(0)
        for j in range(4):
            xnorm_T(0, j)
        xnorm_spill(0)

        stg = [(CUM2, CUMr), (E22, E2r), (KK2, KKr), (QS2, QSr)]
        for dc in range(8):
            w, wr = stg[dc % 4]
            S_.dma("sp" if dc % 2 == 0 else "act", ("wst", dc % 4), W=wr, out=w, in_=g.win_v[dc][:, 2048:4096])
            S_.op("dve", "tensor_scalar", R=wr + ["CST"], W=[("WH", dc)],
                  out=WH[:, dc, :], in0=w, scalar1=CST[:, C_G + dc:C_G + dc + 1], scalar2=None, op0=ALU.mult)
        for h in range(4):
            w, wr = stg[h % 4]
            S_.dma("sp" if h % 2 == 0 else "act", ("wst", h % 4), W=wr, out=w[:, 0:1024], in_=g.wout_v[4 + h])
            S_.op("dve", "tensor_copy", R=wr, W=[("WOh", h)], out=WOh[:, h, :], in_=w[:, 0:1024])
        S_.op("pool", "memset", W=[("stC", h) for h in range(4)], ap=stC[:], constant=0.0)

        for st in range(NST):
            b = st % 2
            nxt = st + 1 < NST
            fproj()
            for h in range(4):
                i = proj_fm(0 + h * 128)
                S_.op("act", "activation", R=[PPT[i][1]], W=[("QS", h)], out=QS[:, h, :], in_=PPT[i][0][:], func=AF.Silu)
            for h in range(4):
                i = proj_fm(1536 + h * 128)
                S_.op("act", "activation", R=[PPT[i][1]], W=[("SG", h)], out=SG[:, h, :], in_=PPT[i][0][:], func=AF.Silu)
            if nxt:
                xnorm_pre(st + 1)
            for h in range(4):
                S_.op("act", "activation", R=[("TH", h)], W=[("TH", h)], out=TH[:, h, :], in_=TH[:, h, :], func=AF.Ln)
                S_.op("dve", "tensor_tensor_scan", R=[("TH", h), "CST"], W=[("CUM", h)],
                      out=CUM[:, h, :], data0=CST[:, C_RST:C_RST + 512], data1=TH[:, h, :], initial=0.0, op0=ALU.mult, op1=ALU.add)
            for j in range(4):
                h = j
                i = next_pp()
                for dc in range(8):
                    S_.op("pe", "matmul", R=[("WH", dc), ("hnT", j)], W=[PPT[i][1]], acc=True, sig=(dc == 7),
                          out=PPT[i][0][:], lhsT=hnT[:, dc, j * 128:(j + 1) * 128], rhs=WH[:, dc, 1024:1536], start=(dc == 0), stop=(dc == 7))
                S_.op("dve", "tensor_copy", R=[PPT[i][1]], W=[("Vt", j)], out=Vt[:, j, :], in_=PPT[i][0][:])
                S_.op("act", "activation", R=[("CUM", h)], W=[("TH", h)], out=TH[:, h, :], in_=CUM[:, h, :], func=AF.Exp)
                S_.op("act", "activation", R=[("CUM", h)], W=[("E2", h)], out=E2[:, h, :], in_=CUM[:, h, :], func=AF.Exp, scale=-1.0)
                S_.op("dve", "tensor_tensor", R=[("QS", h), ("TH", h)], W=[("qdec", h)], out=qdec[:, h, :], in0=QS[:, h, :], in1=TH[:, h, :], op=ALU.mult)
                S_.op("dve", "tensor_tensor", R=[("KK", h), ("E2", h)], W=[("E2", h)], out=E2[:, h, :], in0=KK[:, h, :], in1=E2[:, h, :], op=ALU.mult)
            for h in range(4):
                S_.op("dve", "tensor_copy", R=[("E2", h)], W=[("kinvb", h)], out=kinvb[:, h, :], in_=E2[:, h, :])
                cd_bc = bass.AP(TH, h * 512 + 63, [[2048, 128], [64, 8], [0, 64]])
                S_.op("dve", "tensor_tensor", R=[("E2", h), ("TH", h)], W=[("kend", h)],
                      out=kend[:, h, :].rearrange("p (c t) -> p c t", t=64), in0=E2[:, h, :].rearrange("p (c t) -> p c t", t=64), in1=cd_bc, op=ALU.mult)
            if nxt:
                for j in range(4):
                    xnorm_T(st + 1, j)
                xnorm_spill(st + 1)

            for rnd in range(2):
                for hh in range(2):
                    h = rnd * 2 + hh
                    for j in range(4):
                        col = (hh * 4 + j) * 128
                        S_.op("pe", "transpose", R=[("kend", h), "identb"], W=[TRB[rnd][1]], acc=True, sig=(hh == 1 and j == 3),
                              out=TRB[rnd][0][:, col:col + 128], in_=kend[:, h, j * 128:(j + 1) * 128], identity=identb[:])
                S_.op("act", "activation", R=[TRB[rnd][1]], W=[("kendT", rnd * 2), ("kendT", rnd * 2 + 1)],
                      out=kendT[:, rnd * 2:rnd * 2 + 2, :, :].rearrange("p h j e -> p (h j e)"), in_=TRB[rnd][0], func=AF.Copy)

            def stage1(h):
                sbuf = h % 4
                for j in range(4):
                    S_.op("pe", "matmul", R=[("kinvb", h), ("qdec", h)], W=["pa"], acc=True, sig=(j == 3),
                          out=pa[:, j * 128:(j + 1) * 128], lhsT=kinvb[:, h, j * 128:(j + 1) * 128], rhs=qdec[:, h, j * 128:(j + 1) * 128], start=(j == 0), stop=(j == 3))
                S_.op("dve", "tensor_tensor", R=["pa"] + [("m2b", r) for r in range(4)], W=[("attm", h)],
                      out=attm[:, h, :], in0=pa[:], in1=m2b[:].rearrange("p c t -> p (c t)"), op=ALU.mult)
                S_.op("act", "activation", R=[("stC", h)], W=[("stB", sbuf, 0)], out=stB[sbuf][:, 0, :], in_=stC[:, h, :], func=AF.Copy)
                for c in range(8):
                    j = c // 2
                    r0 = (c % 2) * 64
                    pk = pkv[c % 2]
                    S_.op("pe", "matmul", R=[("kendT", h), ("Vt", j)], W=[("pkv", c % 2)], acc=True, sig=(c >= 6),
                          out=pk[:, c // 2, :], lhsT=kendT[r0:r0 + 64, h, j, :], rhs=Vt[r0:r0 + 64, j, h * 128:(h + 1) * 128], start=True, stop=True)
                for c in range(8):
                    src = stC[:, h, :] if c == 0 else stS[sbuf][:, c, :]
                    srcr = ("stC", h) if c == 0 else ("stS", sbuf, c)
                    dst = stC[:, h, :] if c == 7 else stS[sbuf][:, c + 1, :]
                    dstr = ("stC", h) if c == 7 else ("stS", sbuf, c + 1)
                    S_.op("dve", "scalar_tensor_tensor", R=[("pkv", c % 2), srcr, ("TH", h)], W=[dstr],
                          out=dst, in0=src, scalar=TH[:, h, c * 64 + 63:c * 64 + 64], in1=pkv[c % 2][:, c // 2, :], op0=ALU.mult, op1=ALU.add)
                S_.op("act", "activation", R=[("stS", sbuf, c) for c in range(1, 8)], W=[("stB", sbuf, 1)],
                      out=stB[sbuf][:, 1:8, :], in_=stS[sbuf][:, 1:8, :], func=AF.Copy)

            def stage2(h):
                sbuf = h % 4
                pob = po[h % 2]
                por = ("po", h % 2)
                for c in range(8):
                    j = c // 2
                    if c % 2 == 0:
                        S_.op("pe", "matmul", R=[("Vt", j), ("attm", h)], W=[por], acc=True,
                              out=pob[:, j * 128:(j + 1) * 128], lhsT=Vt[:, j, h * 128:(h + 1) * 128], rhs=attm[:, h, j * 128:(j + 1) * 128], start=(c == 0), stop=False)
                    S_.op("pe", "matmul", R=[("stB", sbuf, 0), ("stB", sbuf, 1), ("qdec", h)], W=[por], acc=True, sig=(c == 7),
                          out=pob[:, c * 64:(c + 1) * 64], lhsT=stB[sbuf][:, c, :], rhs=qdec[:, h, c * 64:(c + 1) * 64], start=False, stop=(c == 7))
                S_.op("act", "activation", R=[por], W=["SQ"], out=SQ[:], in_=pob[:], func=AF.Square)
                S_.op("pe", "matmul", R=["onesb", "SQ"], W=[("pp", 1)], sig=True, out=pp[1][:], lhsT=onesb[:], rhs=SQ[:], start=True, stop=True)
                S_.op("act", "activation", R=[("pp", 1), "CST"], W=["LNR"], out=LNR[:], in_=pp[1][:], func=AF.Ln, scale=1.0 / 128, bias=eps_ap)
                S_.op("act", "activation", R=["LNR"], W=["Rr"], out=Rr[:], in_=LNR[:], func=AF.Exp, scale=-0.5)
                S_.op("dve", "scalar_tensor_tensor", R=[por, "Rr", "CST"], W=["Y1"],
                      out=Y1[:], in0=pob[:], scalar=CST[:, C_HW + h:C_HW + h + 1], in1=Rr[:], op0=ALU.mult, op1=ALU.mult)
                S_.op("dve", "tensor_tensor", R=["Y1", ("SG", h)], W=[("yh", h)], out=yh[:, h, :], in0=Y1[:], in1=SG[:, h, :], op=ALU.mult)

            if "h1" in g.stages:
                pass
            elif "h2" in g.stages:
                for h in range(4):
                    stage1(h)
            else:
                stage1(0)
                stage1(1)
                stage1(2)
                stage1(3)
                stage2(0)
                stage2(1)
                stage2(2)
                stage2(3)

            for j in range(4 if ("h1" not in g.stages and "h2" not in g.stages) else 0):
                for dh in range(2):
                    i = next_pp()
                    for h in range(4):
                        S_.op("pe", "matmul", R=[("yh", h), ("WOh", h)], W=[PPT[i][1]], acc=True, sig=(h == 3),
                              out=PPT[i][0][:], lhsT=yh[:, h, j * 128:(j + 1) * 128], rhs=WOh[:, h, dh * 512:(dh + 1) * 512], start=(h == 0), stop=(h == 3))
                    S_.op("dve", "tensor_tensor", R=[PPT[i][1], ("xt", b, j)], W=[("xt", b, j)],
                          out=xt[b][:, j, dh * 512:(dh + 1) * 512], in0=PPT[i][0][:], in1=xt[b][:, j, dh * 512:(dh + 1) * 512], op=ALU.add)
            S_.dma("sp", ("parto", b), R=xr(b), W=[("out", st)], out=g.part_v[st], in_=xt[b][:])
            if st + 2 < NST:
                load_x(st + 2)


def _pass_ab(g):
    nc = g.nc; S_ = g.S_; NST = g.NST; CST = g.CST; identb = g.identb; onesb = g.onesb; eps_ap = g.eps_ap
    S = g.S
    c1, c2, c3 = _split_2pi()
    qs = contextlib.ExitStack()
    with qs:
        qT = qs.enter_context(nc.sbuf_tensor("qT", [128, 4, S], BF16))
        kT = qs.enter_context(nc.sbuf_tensor("kT", [128, 4, S], BF16))
        vT = qs.enter_context(nc.sbuf_tensor("vT", [128, 4, S], BF16))

        as_ = contextlib.ExitStack()
        with as_:
            def sba(name, shape, dt):
                return as_.enter_context(nc.sbuf_tensor(name, list(shape), dt))

            def psa(name, shape, dt):
                return as_.enter_context(nc.psum_tensor(name, list(shape), dt))

            WA = sba("WA", [128, 8, 2048], BF16)
            wst = [sba(f"wsta{i}", [128, 1024], F32) for i in range(4)]
            permb = sba("permb", [128, 128], BF16)
            hnT = [sba(f"hnTa{i}", [128, 8, 512], BF16) for i in range(2)]
            posi = sba("posi", [128, 512], I32)
            posf = sba("posf", [128, 512], F32)
            ang = sba("ang", [128, 512], F32)
            ki = sba("ki", [128, 512], I32)
            kf = sba("kf", [128, 512], F32)
            r1 = sba("r1", [128, 512], F32)
            r2 = sba("r2", [128, 512], F32)
            aab = sba("aab", [128, 512], F32)
            COSb = [sba(f"COS{i}", [128, 512], F32) for i in range(2)]
            SINb = [sba(f"SIN{i}", [128, 512], F32) for i in range(2)]
            Qb = [sba(f"Qb{i}", [128, 512], BF16) for i in range(2)]
            T1 = [sba(f"T1{i}", [128, 512], F32) for i in range(2)]
            T2 = [sba(f"T2{i}", [128, 512], F32) for i in range(2)]
            sgt = [sba(f"sgt{i}", [128, 512], BF16) for i in range(2)]
            pp = [psa(f"ppa{i}", [128, 512], F32) for i in range(6)]
            pq = [psa(f"pqa{i}", [128, 512], F32) for i in range(2)]

            S_.op("dve", "tensor_copy", R=["CST"], W=["permb"], out=permb[:], in_=CST[:, C_PERM:C_PERM + 128])
            for dc in range(8):
                for hf in range(2):
                    wi = (dc * 2 + hf) % 4
                    w = wst[wi]
                    S_.dma("sp" if wi % 2 == 0 else "act", ("wsta", wi), W=[("wsta", wi)], out=w[:], in_=g.win_v[dc][:, hf * 1024:(hf + 1) * 1024])
                    S_.op("dve", "tensor_scalar", R=[("wsta", wi), "CST"], W=[("WA", dc)],
                          out=WA[:, dc, hf * 1024:(hf + 1) * 1024], in0=w[:], scalar1=CST[:, C_G + dc:C_G + dc + 1], scalar2=None, op0=ALU.mult)

            ppi = [0]
            tb = [0]

            def load_hnT(st):
                hb_ = st % 2
                S_.dma("sp", ("hnTa", hb_), R=[("hnT_d", st)], W=[("hnTa", hb_)], out=hnT[hb_][:].rearrange("p c t -> p (c t)"), in_=g.hnT_d[st])

            def tables(st):
                tsl_ = slice(st * 512, (st + 1) * 512)
                COS_, SIN_ = COSb[st % 2], SINb[st % 2]
                cn, sn = ("COS", st % 2), ("SIN", st % 2)
                S_.dma("sp", "posi", W=["posi"], out=posi[:], in_=g.pos_d[:, tsl_])
                S_.op("dve", "tensor_copy", R=["posi"], W=["posf"], out=posf[:], in_=posi[:])
                S_.op("dve", "tensor_scalar", R=["posf", "CST"], W=["ang"], out=ang[:], in0=posf[:], scalar1=CST[:, C_INVF:C_INVF + 1], scalar2=None, op0=ALU.mult)
                S_.op("dve", "tensor_scalar", R=["ang"], W=["ki"], out=ki[:], in0=ang[:], scalar1=float(1.0 / TWO_PI), scalar2=None, op0=ALU.mult)
                S_.op("dve", "tensor_copy", R=["ki"], W=["kf"], out=kf[:], in_=ki[:])
                S_.op("dve", "scalar_tensor_tensor", R=["kf", "ang"], W=["r1"], out=r1[:], in0=kf[:], scalar=-c1, in1=ang[:], op0=ALU.mult, op1=ALU.add)
                S_.op("dve", "scalar_tensor_tensor", R=["kf", "r1"], W=["r2"], out=r2[:], in0=kf[:], scalar=-c2, in1=r1[:], op0=ALU.mult, op1=ALU.add)
                S_.op("dve", "scalar_tensor_tensor", R=["kf", "r2"], W=["r1"], out=r1[:], in0=kf[:], scalar=-c3, in1=r2[:], op0=ALU.mult, op1=ALU.add)
                S_.op("dve", "tensor_scalar", R=["r1"], W=["r2"], out=r2[:], in0=r1[:], scalar1=float(np.pi), scalar2=float(-np.pi), op0=ALU.min, op1=ALU.max)
                S_.op("dve", "scalar_tensor_tensor", R=["r2"], W=["aab"], out=aab[:], in0=r2[:], scalar=-1.0, in1=r2[:], op0=ALU.mult, op1=ALU.max)
                S_.op("act", "activation", R=["aab", "CST"], W=[cn], out=COS_[:], in_=aab[:], func=AF.Sin, scale=-1.0, bias=CST[:, C_HPI:C_HPI + 1])
                S_.op("act", "activation", R=["r2", "CST"], W=[sn], out=SIN_[:], in_=r2[:], func=AF.Sin, scale=CST[:, C_SGN:C_SGN + 1])

            load_hnT(0)
            tables(0)
            for st in range(NST if "a1" not in g.stages else 0):
                hb = st % 2
                tsl = slice(st * 512, (st + 1) * 512)
                if st + 1 < NST:
                    load_hnT(st + 1)
                COS, SIN = COSb[st % 2], SINb[st % 2]
                cosn, sinn = ("COS", st % 2), ("SIN", st % 2)

                hnTr = [("hnTa", hb)]
                if "a2" in g.stages:
                    continue

                def proj(col0):
                    i = ppi[0] % 6
                    ppi[0] += 1
                    for dc in range(8):
                        S_.op("pe", "matmul", R=[("WA", dc)] + hnTr, W=[("ppa", i)], acc=True, sig=(dc == 7),
                              out=pp[i][:], lhsT=WA[:, dc, col0:col0 + 128], rhs=hnT[hb][:, dc, :], start=(dc == 0), stop=(dc == 7))
                    return i

                pendq = []

                def rot(i_, t_, dst_, dname_, c_):
                    S_.op("pe", "matmul", R=["permb", ("Qb", t_)], W=[("pqa", t_)], sig=True,
                          out=pq[t_][:], lhsT=permb[:], rhs=Qb[t_][:], start=True, stop=True)
                    S_.op("dve", "tensor_tensor", R=[("Qb", t_), cosn], W=[("T1", t_)], out=T1[t_][:], in0=Qb[t_][:], in1=COS[:], op=ALU.mult)
                    S_.op("dve", "tensor_tensor", R=[("pqa", t_), sinn], W=[("T2", t_)], out=T2[t_][:], in0=pq[t_][:], in1=SIN[:], op=ALU.mult)
                    S_.op("pool", "tensor_tensor", R=[("T1", t_), ("T2", t_)], W=[(dname_, c_, st)], out=dst_[:, c_, tsl], in0=T1[t_][:], in1=T2[t_][:], op=ALU.add)

                for which, dst in ((0, qT), (1, kT)):
                    dname = "qT" if which == 0 else "kT"
                    for c in range(4):
                        i = proj(which * 512 + c * 128)
                        t = tb[0] % 2
                        tb[0] += 1
                        S_.op("act", "activation", R=[("ppa", i)], W=[("Qb", t)], out=Qb[t][:], in_=pp[i][:], func=AF.Copy)
                        if pendq:
                            rot(*pendq.pop())
                        pendq.append((i, t, dst, dname, c))
                if st + 1 < NST:
                    tables(st + 1)
                for c in range(4):
                    i = proj(1024 + c * 128)
                    S_.op("act", "activation", R=[("ppa", i)], W=[("vT", c, st)], out=vT[:, c, tsl], in_=pp[i][:], func=AF.Copy)
                    if pendq:
                        rot(*pendq.pop())
                for c in range(4 if "a5" not in g.stages else 0):
                    i = proj(1536 + c * 128)
                    t = (st * 4 + c) % 2
                    S_.op("act", "activation", R=[("ppa", i)], W=[("sgt", t)], out=sgt[t][:], in_=pp[i][:], func=AF.Silu)
                    S_.dma("sp", ("sgo", t), R=[("sgt", t)], W=[("sg_d", c, st)], out=g.sg_d[c][:, tsl], in_=sgt[t][:])

        if "B" not in g.stages:
            return
        S_.barrier()
        bs = contextlib.ExitStack()
        with bs:
            def sbb(name, shape, dt):
                return bs.enter_context(nc.sbuf_tensor(name, list(shape), dt))

            def psb_(name, shape, dt):
                return bs.enter_context(nc.psum_tensor(name, list(shape), dt))

            NVW, NPT, PVLAG = 5, 5, 2
            acc = sbb("accXY", [128, 2, S], F32)
            maskb = sbb("maskb", [128, 2, 2, 256], BF16)
            VW = [sbb(f"VW{i}", [128, 2, 256], BF16) for i in range(NVW)]
            PT = [sbb(f"PT{i}", [128, 2, 512], BF16) for i in range(NPT)]
            scx = sbb("scx", [128, 2], F32)
            SQXb = [sbb(f"SQX{i}", [128, 512], BF16) for i in range(2)]
            SQYb = [sbb(f"SQY{i}", [128, 512], BF16) for i in range(2)]
            LNR = sbb("LNRb", [128, 512], F32)
            Rr = sbb("Rrb", [128, 512], F32)
            Y1 = sbb("Y1b", [128, 512], F32)
            sgl = [sbb(f"sgl{i}", [128, 512], BF16) for i in range(2)]

            ptv = psb_("ptv", [128, 512], BF16)
            psc = [psb_(f"psc{i}", [128, 2, 512], F32) for i in range(2)]
            pxy = [psb_(f"pxy{i}", [128, 2, 256], F32) for i in range(2)]
            pU = psb_("pU", [128, 512], F32)
            PTV = [(ptv[:, 0:256], "ptv"), (pU.bitcast(BF16)[:, 0:256], "pU")]

            for hh in range(2):
                for kb in range(2):
                    S_.op("dve", "tensor_copy", R=["CST"], W=["maskb"], out=maskb[:, hh, kb, :], in_=CST[:, C_AMASK:C_AMASK + 256])
            for i in range(NVW):
                S_.op("pool", "memset", W=[("VW", i)], ap=VW[i][:], constant=1.0)
            se = float(np.sqrt(EPS))
            S_.op("pool", "memset", W=["scx"], ap=scx[0:64, 0:1], constant=1.0)
            S_.op("pool", "memset", W=["scx"], ap=scx[64:128, 0:1], constant=se)
            S_.op("pool", "memset", W=["scx"], ap=scx[0:64, 1:2], constant=se)
            S_.op("pool", "memset", W=["scx"], ap=scx[64:128, 1:2], constant=1.0)

            vwi = [0]
            pti = [0]
            gi = [0]
            SC = 1.0 / 8.0

            for c in range(4):
                first_pat = True
                pend = []
                for d in PATTERNS:
                    sub_len = S // d
                    n_blk = sub_len // 128
                    for r in range(d):
                        def tok(j, nb=1, r=r, d=d):
                            s0 = r + d * 128 * j
                            return slice(s0, s0 + d * (128 * nb - 1) + 1, d) if d > 1 else slice(s0, s0 + 128 * nb)

                        grp = {}

                        def emit_pv(j0, nkb, pbuf, vws, grp=grp, tok=tok, n_blk=n_blk, first_pat=first_pat):
                            for kb in range(nkb):
                                j = j0 + kb
                                gq = j // 2
                                if j % 2 == 0:
                                    segs = [(gq, 0, 0, min(2, n_blk - j))]
                                else:
                                    segs = [(gq, 1, 0, 1)]
                                    if j + 1 < n_blk:
                                        segs.append((gq + 1, 0, 1, 1))
                                for (gg, qpos, poff_blk, nb) in segs:
                                    if gg not in grp:
                                        tot = 2 * ((1 if gg > 0 else 0) + 1 + (1 if 2 * gg + 1 < n_blk else 0))
                                        grp[gg] = [gi[0] % 2, 0, tot]
                                        gi[0] += 1
                                    for hh in range(2):
                                        bank, cnt, tot = grp[gg]
                                        qoff = qpos * 128
                                        poff = kb * 256 + poff_blk * 128
                                        S_.op("pe", "matmul", R=[("VW", vws[kb]), ("PT", pbuf)], W=[("pxy", bank)], acc=True, sig=(cnt == tot - 1),
                                              out=pxy[bank][:, hh, qoff:qoff + nb * 128], lhsT=VW[vws[kb]][:, kb, hh * 128:(hh + 1) * 128],
                                              rhs=PT[pbuf][:, hh, poff:poff + nb * 128], start=(cnt == 0), stop=(cnt == tot - 1), skip_group_check=True)
                                        grp[gg][1] += 1
                                        if grp[gg][1] == tot:
                                            nqb = min(2, n_blk - 2 * gg)
                                            dstap = acc[:, :, tok(2 * gg, nqb)]
                                            if first_pat:
                                                S_.op("dve", "tensor_copy", R=[("pxy", bank)], W=[("acc", c)], out=dstap, in_=pxy[bank][:, :, 0:nqb * 128])
                                            else:
                                                S_.op("dve", "tensor_tensor", R=[("pxy", bank), ("acc", c)], W=[("acc", c)], out=dstap, in0=pxy[bank][:, :, 0:nqb * 128], in1=dstap, op=ALU.add)

                        for j0 in range(0, n_blk, 2):
                            nkb = min(2, n_blk - j0)
                            sbuf = (j0 // 2) % 2
                            pbuf = pti[0] % NPT
                            pti[0] += 1
                            vws = []
                            nqs = []
                            for kb in range(nkb):
                                j = j0 + kb
                                nq = 2 if j + 1 < n_blk else 1
                                nqs.append(nq)
                                for hh in range(2):
                                    S_.op("pe", "matmul", R=[("kT", c, s_) for s_ in range(NST)] + [("qT", c, s_) for s_ in range(NST)], W=[("psc", sbuf)], acc=True, sig=(kb == nkb - 1 and hh == 1),
                                          out=psc[sbuf][:, hh, kb * 256:kb * 256 + nq * 128], lhsT=kT[hh * 64:(hh + 1) * 64, c, tok(j)],
                                          rhs=qT[hh * 64:(hh + 1) * 64, c, tok(j, nq)], start=True, stop=True)
                            tvh = pti[0] % 2
                            for kb in range(nkb):
                                j = j0 + kb
                                S_.op("pe", "transpose", R=[("vT", c, s_) for s_ in range(NST)] + ["identb"], W=[PTV[tvh][1]], acc=True, sig=(kb == nkb - 1),
                                      out=PTV[tvh][0][:, kb * 128:(kb + 1) * 128], in_=vT[:, c, tok(j)], identity=identb[:])
                            ncol = (nkb - 1) * 256 + nqs[-1] * 128
                            S_.op("act", "activation", R=[("psc", sbuf)], W=[("PT", pbuf)], out=PT[pbuf][:, :, 0:ncol], in_=psc[sbuf][:, :, 0:ncol], func=AF.Exp, scale=SC)
                            meng = "dve"
                            S_.op(meng, "tensor_tensor", R=[("PT", pbuf), "maskb"], W=[("PT", pbuf)], out=PT[pbuf][:, :, 0:ncol], in0=PT[pbuf][:, :, 0:ncol],
                                  in1=maskb[:].rearrange("p h k q -> p h (k q)")[:, :, 0:ncol], op=ALU.mult)
                            v = vwi[0] % NVW
                            vwi[0] += 1
                            vws = [v] * nkb
                            dst = bass.AP(VW[v], 0, [[512, 128], [256, nkb], [192, 2], [1, 64]])
                            S_.op("act", "activation", R=[PTV[tvh][1]], W=[("VW", v)], out=dst,
                                  in_=PTV[tvh][0][:, 0:nkb * 128].rearrange("p (k h e) -> p k h e", h=2, e=64), func=AF.Copy)
                            pend.append((emit_pv, (j0, nkb, pbuf, vws)))
                            if len(pend) > PVLAG:
                                f_, a_ = pend.pop(0)
                                f_(*a_)
                    first_pat = False
                while pend:
                    f_, a_ = pend.pop(0)
                    f_(*a_)

                def norm_sq(st):
                    tsl_ = slice(st * 512, (st + 1) * 512)
                    t_ = st % 2
                    S_.op("act", "activation", R=[("acc", c), "scx"], W=[("SQX", t_)], out=SQXb[t_][:], in_=acc[:, 0, tsl_], func=AF.Square, scale=scx[:, 0:1])
                    S_.op("act", "activation", R=[("acc", c), "scx"], W=[("SQY", t_)], out=SQYb[t_][:], in_=acc[:, 1, tsl_], func=AF.Square, scale=scx[:, 1:2])

                norm_sq(0)
                for st in range(NST):
                    tsl = slice(st * 512, (st + 1) * 512)
                    t = st % 2
                    S_.dma("sp", ("sgl", t), R=[("sg_d", c, st)], W=[("sgl", t)], out=sgl[t][:], in_=g.sg_d[c][:, tsl])
                    if st + 1 < NST:
                        norm_sq(st + 1)
                    S_.op("pe", "matmul", R=["onesb", ("SQX", t)], W=["pU"], out=pU[0:64, :], lhsT=onesb[:, 0:64], rhs=SQXb[t][:], start=True, stop=True)
                    S_.op("pe", "matmul", R=["onesb", ("SQY", t)], W=["pU"], sig=True, out=pU[64:128, :], lhsT=onesb[:, 0:64], rhs=SQYb[t][:], start=True, stop=True)
                    S_.op("act", "activation", R=["pU"], W=["LNRb"], out=LNR[:], in_=pU[:], func=AF.Ln, scale=1.0 / 64)
                    S_.op("act", "activation", R=["LNRb"], W=["Rrb"], out=Rr[:], in_=LNR[:], func=AF.Exp, scale=-0.5)
                    S_.op("dve", "scalar_tensor_tensor", R=[("acc", c), "Rrb", "CST"], W=["Y1b"],
                          out=Y1[0:64, :], in0=acc[0:64, 0, tsl], scalar=CST[0:64, C_AW + c:C_AW + c + 1], in1=Rr[0:64, :], op0=ALU.mult, op1=ALU.mult)
                    S_.op("dve", "scalar_tensor_tensor", R=[("acc", c), "Rrb", "CST"], W=["Y1b"],
                          out=Y1[64:128, :], in0=acc[64:128, 1, tsl], scalar=CST[64:128, C_AW + c:C_AW + c + 1], in1=Rr[64:128, :], op0=ALU.mult, op1=ALU.mult)
                    S_.op("pool", "tensor_tensor", R=["Y1b", ("sgl", t)], W=[("qT", c, st)], out=qT[:, c, tsl], in0=Y1[:], in1=sgl[t][:], op=ALU.mult)

        if "C" in g.stages:
            S_.barrier()
            _pass_c_inner(g, qT)


def _pass_c_inner(g, qT):
    nc = g.nc; S_ = g.S_; NST = g.NST; CST = g.CST; eps_ap = g.eps_ap
    cs = contextlib.ExitStack()
    with cs:
        def sbc(name, shape, dt):
            return cs.enter_context(nc.sbuf_tensor(name, list(shape), dt))

        def psc_(name, shape, dt):
            return cs.enter_context(nc.psum_tensor(name, list(shape), dt))

        WOa = sbc("WOa", [128, 4, 1024], BF16)
        wst = [sbc(f"wstc{i}", [128, 1024], F32) for i in range(2)]
        fnw = sbc("fnw_sb", [128, D], F32)
        zt = [sbc(f"zt{i}", [128, 4, 1024], F32) for i in range(2)]
        junk = sbc("junkc", [128, 1024], BF16)
        ss = sbc("ssc", [128, 4], F32)
        lnr4 = sbc("lnr4c", [128, 4], F32)
        rstd = sbc("rstdc", [128, 4], F32)
        pc = [psc_(f"pc{i}", [128, 512], F32) for i in range(8)]

        S_.dma("sp", "fnw", W=["fnw"], out=fnw[:], in_=g.fnw_d)
        for c in range(4):
            w = wst[c % 2]
            S_.dma("sp", ("wstc", c % 2), W=[("wstc", c % 2)], out=w[:], in_=g.wout_v[c])
            S_.op("dve", "tensor_copy", R=[("wstc", c % 2)], W=[("WOa", c)], out=WOa[:, c, :], in_=w[:])

        def load_part(st):
            b = st % 2
            S_.dma("sp", ("zt", b), R=[("out", st)], W=[("zt", b, j) for j in range(4)], out=zt[b][:], in_=g.part_v[st])

        pci = [0]
        load_part(0)
        for st in range(NST):
            b = st % 2
            if st + 1 < NST:
                load_part(st + 1)
            for j in range(4):
                tok = slice(st * 512 + j * 128, st * 512 + (j + 1) * 128)
                for dh in range(2):
                    i = pci[0] % 8
                    pci[0] += 1
                    for c in range(4):
                        S_.op("pe", "matmul", R=[("qT", c, st), ("WOa", c)], W=[("pc", i)], acc=True, sig=(c == 3),
                              out=pc[i][:], lhsT=qT[:, c, tok], rhs=WOa[:, c, dh * 512:(dh + 1) * 512], start=(c == 0), stop=(c == 3))
                    S_.op("dve", "tensor_tensor", R=[("pc", i), ("zt", b, j)], W=[("zt", b, j)],
                          out=zt[b][:, j, dh * 512:(dh + 1) * 512], in0=pc[i][:], in1=zt[b][:, j, dh * 512:(dh + 1) * 512], op=ALU.add)
                S_.op("act", "activation", R=[("zt", b, j)], W=["junkc", ("ssc", j)],
                      out=junk[:], in_=zt[b][:, j, :], func=AF.Square, accum_out=ss[:, j:j + 1])
            S_.op("act", "activation", R=[("ssc", j) for j in range(4)] + ["CST"], W=["lnr4c"], out=lnr4[:], in_=ss[:], func=AF.Ln, scale=1.0 / D, bias=eps_ap)
            S_.op("act", "activation", R=["lnr4c"], W=["rstdc"], out=rstd[:], in_=lnr4[:], func=AF.Exp, scale=-0.5)
            for j in range(4):
                S_.op("dve", "scalar_tensor_tensor", R=[("zt", b, j), "rstdc", "fnw"], W=[("zt", b, j)],
                      out=zt[b][:, j, :], in0=zt[b][:, j, :], scalar=rstd[:, j:j + 1], in1=fnw[:], op0=ALU.mult, op1=ALU.mult)
            S_.dma("sp", ("outo", b), R=[("zt", b, j) for j in range(4)], W=[("out", st)], out=g.out_v[st], in_=zt[b][:])


def prep_core_inputs(inputs, b, S=4096):
    cst = _const_table()
    cst[:, C_G:C_G + 8] = np.asarray(inputs["mix_norm_w"], np.float32)[0].reshape(8, 128).T
    cst[:, C_AW:C_AW + 4] = np.asarray(inputs["attn_out_norm_w"], np.float32)[0].reshape(4, 128).T
    cst[:, C_HW:C_HW + 4] = np.asarray(inputs["hgrn_out_norm_w"], np.float32)[0].reshape(4, 128).T
    lb = np.asarray(inputs["hgrn_lb_raw"], np.float32)
    cst[:, C_LB0:C_LB0 + 4] = lb[0].reshape(4, 128).T
    cst[:, C_LB1:C_LB1 + 4] = lb[1].reshape(4, 128).T
    pos = np.asarray(inputs["positions"])[b, :S].astype(np.int32)
    return {
        "x": np.ascontiguousarray(np.asarray(inputs["x"], np.float32)[b, :S]),
        "pos": np.ascontiguousarray(np.broadcast_to(pos[None, :], (128, S))),
        "w_in": np.ascontiguousarray(np.asarray(inputs["w_in"], np.float32)[0]),
        "w_out": np.ascontiguousarray(np.asarray(inputs["w_out"], np.float32)[0]),
        "cst": cst,
        "fnw": np.ascontiguousarray(np.broadcast_to(np.asarray(inputs["final_norm_w"], np.float32)[None, :], (128, D))),
    }


_NC_CACHE = {}


def kernel(x, positions, w_in, w_out, mix_norm_w, attn_out_norm_w, hgrn_out_norm_w, hgrn_lb_raw, final_norm_w):
    inputs = dict(x=x, positions=positions, w_in=w_in, w_out=w_out, mix_norm_w=mix_norm_w,
                  attn_out_norm_w=attn_out_norm_w, hgrn_out_norm_w=hgrn_out_norm_w,
                  hgrn_lb_raw=hgrn_lb_raw, final_norm_w=final_norm_w)
    B, S, _ = np.asarray(x).shape
    if S not in _NC_CACHE:
        _NC_CACHE[S] = build(S)
    nc = _NC_CACHE[S]
    in_maps = [prep_core_inputs(inputs, b, S) for b in range(B)]
    res = run_bass_kernel_spmd(nc, in_maps, core_ids=list(range(B)))
    return np.stack([np.asarray(r["out"], np.float32) for r in res.results], axis=0)
```

```python
import contextlib
import numpy as np
import concourse.bass as bass
import concourse.mybir as mybir
from concourse.bass_utils import run_bass_kernel_spmd

F32 = mybir.dt.float32
BF16 = mybir.dt.bfloat16
I32 = mybir.dt.int32
AF = mybir.ActivationFunctionType
ALU = mybir.AluOpType

D = 1024
EPS = 1e-6
ROPE_THETA = 500000.0
PATTERNS = (1, 4, 16)
TWO_PI = 2.0 * np.pi


def _split_2pi():
    c1 = 6.28125
    r = TWO_PI - c1
    c2 = np.float32(r)
    c2 = (c2.view(np.uint32) & np.uint32(0xFFFFF000)).view(np.float32)
    c3 = np.float32(r - float(c2))
    return float(c1), float(c2), float(c3)


C_G = 0
C_AW = 8
C_HW = 12
C_LB0 = 16
C_LB1 = 20
C_INVF = 24
C_SGN = 25
C_HPI = 26
C_EPS = 27
C_ONE = 28
C_IDENT = 32
C_PERM = 160
C_AMASK = 288
C_M2 = 544
C_RST = 672
NCST = 1184


def _const_table():
    c = np.zeros((128, NCST), np.float32)
    p = np.arange(128)
    e = p % 64
    half = 8
    invf = ROPE_THETA ** (-np.arange(half, dtype=np.float64) * (2.0 / 16.0))
    c[:, C_INVF] = np.where(e < 16, invf[e % 8], 0.0)
    c[:, C_SGN] = np.where(e < 8, -1.0, np.where(e < 16, 1.0, 0.0))
    c[:, C_HPI] = np.pi / 2
    c[:, C_EPS] = EPS
    c[:, C_ONE] = 1.0
    c[:, C_IDENT:C_IDENT + 128] = np.eye(128)
    perm = np.zeros((128, 128), np.float32)
    for m in range(128):
        em = m % 64
        if em < 8:
            perm[m + 8, m] = 1.0
        elif em < 16:
            perm[m - 8, m] = 1.0
    c[:, C_PERM:C_PERM + 128] = perm
    k = np.arange(128)[:, None]
    q = np.arange(128)[None, :]
    am = np.concatenate([(q >= k), (q <= k)], axis=1).astype(np.float32)
    c[:, C_AMASK:C_AMASK + 256] = am
    s = np.arange(128)[:, None]
    t = np.arange(128)[None, :]
    c[:, C_M2:C_M2 + 128] = ((s // 64 == t // 64) & (s <= t)).astype(np.float32)
    rst = np.ones(512, np.float32)
    rst[0::64] = 0.0
    c[:, C_RST:C_RST + 512] = rst[None, :]
    return c


class Sched:
    COMPUTE = ("pe", "act", "dve", "pool")

    def __init__(self, nc, stack):
        self.nc = nc
        self.stack = stack
        self.streams = {k: [] for k in ("pe", "act", "dve", "pool", "sp")}
        self.sems = {}
        self.cnt = {}
        self.sigs = {k: [] for k in self.COMPUTE}
        self.waited = {k: {} for k in self.streams}
        self.last_w = {}
        self.readers = {}
        for k in self.COMPUTE:
            self._sem(k)

    def _sem(self, name):
        if name not in self.sems:
            self.sems[name] = self.stack.enter_context(self.nc.semaphore("s_" + "".join(ch if ch.isalnum() else "_" for ch in str(name))))
            self.cnt[name] = 0
        return self.sems[name]

    def _resolve(self, ev):
        kind = ev[0]
        if kind == "dma":
            return ev[1], ev[2]
        eng, seq = ev[1], ev[2]
        best = None
        for s, v in reversed(self.sigs[eng]):
            if s >= seq:
                best = v
            else:
                break
        if best is not None:
            return eng, best
        rec = self.streams[eng][-1]
        assert rec["seq"] >= seq
        self.cnt[eng] += 1
        rec["inc"] = True
        self.sigs[eng].append((rec["seq"], self.cnt[eng]))
        return eng, self.cnt[eng]

    def _collect(self, eng, reads, writes, acc):
        evs = []
        for r in reads:
            ev = self.last_w.get(r)
            if ev is not None:
                evs.append(ev)
        for w in writes:
            ev = self.last_w.get(w)
            if ev is not None and not (acc and ev[0] == "c" and ev[1] == eng):
                evs.append(ev)
            for ev in self.readers.get(w, ()):
                evs.append(ev)
        waits = {}
        for ev in evs:
            name, val = self._resolve(ev)
            if self.waited[eng].get(name, 0) >= val:
                continue
            waits[name] = max(waits.get(name, 0), val)
        for name, val in waits.items():
            self.waited[eng][name] = val
        return list(waits.items())

    def _commit(self, ev, reads, writes):
        for r in reads:
            self.readers.setdefault(r, []).append(ev)
        for w in writes:
            self.last_w[w] = ev
            self.readers[w] = []

    def op(self, eng, method, R=(), W=(), sig=None, acc=False, **kw):
        if sig is None:
            sig = (eng != "pe")
        reads, writes = list(R), list(W)
        fn = (lambda e, method=method, kw=kw: getattr(e, method)(**kw))
        waits = self._collect(eng, reads, writes, acc)
        seq = len(self.streams[eng])
        rec = {"fn": fn, "waits": waits, "inc": False, "seq": seq, "dma": None}
        self.streams[eng].append(rec)
        if sig:
            self.cnt[eng] += 1
            rec["inc"] = True
            self.sigs[eng].append((seq, self.cnt[eng]))
        self._commit(("c", eng, seq), reads, writes)

    def dma(self, q, key, R=(), W=(), **kw):
        reads, writes = list(R), list(W)
        fn = (lambda e, kw=kw: e.dma_start(**kw))
        name = ("dma", key)
        self._sem(name)
        waits = self._collect(q, reads, writes, False)
        self.cnt[name] += 16
        rec = {"fn": fn, "waits": waits, "inc": False, "seq": len(self.streams[q]), "dma": name}
        self.streams[q].append(rec)
        self._commit(("dma", name, self.cnt[name]), reads, writes)

    def barrier(self):
        for eng in self.COMPUTE:
            if self.streams[eng]:
                rec = self.streams[eng][-1]
                if not rec["inc"]:
                    self.cnt[eng] += 1
                    rec["inc"] = True
                    self.sigs[eng].append((rec["seq"], self.cnt[eng]))
        for eng in self.streams:
            waits = []
            for name, val in self.cnt.items():
                if val <= 0 or name == eng:
                    continue
                if self.waited[eng].get(name, 0) >= val:
                    continue
                self.waited[eng][name] = val
                waits.append((name, val))
            self.streams[eng].append({"fn": None, "waits": waits, "inc": False, "seq": len(self.streams[eng]), "dma": None})

    def final_wait(self, q):
        waits = {}
        for name, val in self.cnt.items():
            if isinstance(name, tuple) and name[0] == "dma" and val > 0:
                waits[name] = val
        rec = {"fn": None, "waits": list(waits.items()), "inc": False, "seq": len(self.streams[q]), "dma": None}
        self.streams[q].append(rec)

    def emit(self):
        nc = self.nc
        sems = self.sems

        def replay(name):
            def run(e):
                for rec in self.streams[name]:
                    for sname, val in rec["waits"]:
                        e.wait_ge(sems[sname], val)
                    if rec["fn"] is None:
                        continue
                    ins = rec["fn"](e)
                    if rec["dma"] is not None:
                        ins.then_inc(sems[rec["dma"]], 16)
                    elif rec["inc"]:
                        ins.then_inc(sems[name], 1)
            return run

        with nc.Block() as block:
            block.tensor(replay("pe"))
            block.scalar(replay("act"))
            block.vector(replay("dve"))
            block.gpsimd(replay("pool"))
            block.sync(replay("sp"))


class Ctx:
    pass


def build(S=4096, stages=("H", "A", "B", "C"), dbg=()):
    g = Ctx()
    g.S = S
    g.NST = S // 512
    g.NT = S // 128
    g.stages = stages
    nc = bass.Bass("TRN2", target_bir_lowering=False)
    g.nc = nc
    x_d = nc.dram_tensor("x", [S, D], F32, kind="ExternalInput").ap()
    g.pos_d = nc.dram_tensor("pos", [128, S], I32, kind="ExternalInput").ap()
    win_d = nc.dram_tensor("w_in", [D, 4096], F32, kind="ExternalInput").ap()
    wout_d = nc.dram_tensor("w_out", [D, D], F32, kind="ExternalInput").ap()
    cst_d = nc.dram_tensor("cst", [128, NCST], F32, kind="ExternalInput").ap()
    g.fnw_d = nc.dram_tensor("fnw", [128, D], F32, kind="ExternalInput").ap()
    out_d = nc.dram_tensor("out", [S, D], F32, kind="ExternalOutput").ap()
    part_d = out_d if "C" in stages else nc.dram_tensor("part_scr", [S, D], F32, kind="ExternalOutput").ap()
    g.hnT_d = nc.dram_tensor("hnT_scr", [g.NST, 128, 8 * 512], BF16, kind="Internal").ap()
    g.sg_d = nc.dram_tensor("sg_scr", [4, 128, S], BF16, kind="Internal").ap()
    g.dbg_d = {}
    for name, shape in dbg:
        g.dbg_d[name] = nc.dram_tensor(name, list(shape), F32, kind="ExternalOutput").ap()

    g.x_v = x_d.rearrange("(n j p) d -> n p j d", j=4, p=128)
    g.part_v = part_d.rearrange("(n j p) d -> n p j d", j=4, p=128)
    g.out_v = out_d.rearrange("(n j p) d -> n p j d", j=4, p=128)
    g.win_v = win_d.rearrange("(dc p) c -> dc p c", p=128)
    g.wout_v = wout_d.rearrange("(mc p) c -> mc p c", p=128)

    stack = contextlib.ExitStack()
    with stack:
        S_ = Sched(nc, stack)
        g.S_ = S_

        def sb(name, shape, dt):
            return stack.enter_context(nc.sbuf_tensor(name, list(shape), dt))

        g.CST = CST = sb("CST", [128, NCST], F32)
        g.identb = identb = sb("identb", [128, 128], BF16)
        g.onesb = onesb = sb("onesb", [128, 128], BF16)
        g.m2b = m2b = sb("m2b", [128, 4, 128], BF16)
        g.lbA = lbA = sb("lbA", [128, 4], F32)
        g.lbB = lbB = sb("lbB", [128, 4], F32)
        tmp4 = sb("tmp4", [128, 4], F32)
        tmp4b = sb("tmp4b", [128, 4], F32)

        S_.dma("sp", "cst", W=["CST"], out=CST[:], in_=cst_d)
        S_.op("dve", "tensor_copy", R=["CST"], W=["identb"], out=identb[:], in_=CST[:, C_IDENT:C_IDENT + 128])
        S_.op("dve", "memset", W=["onesb"], ap=onesb[:], constant=1.0)
        for r in range(4):
            S_.op("dve", "tensor_copy", R=["CST"], W=[("m2b", r)], out=m2b[:, r, :], in_=CST[:, C_M2:C_M2 + 128])
        S_.op("dve", "tensor_tensor", R=["CST"], W=["tmp4"], out=tmp4[:], in0=CST[:, C_LB0:C_LB0 + 4], in1=CST[:, C_LB1:C_LB1 + 4], op=ALU.subtract)
        S_.op("act", "activation", R=["tmp4"], W=["tmp4b"], out=tmp4b[:], in_=tmp4[:], func=AF.Tanh, scale=0.5)
        S_.op("dve", "tensor_scalar", R=["tmp4b"], W=["lbA"], out=lbA[:], in0=tmp4b[:], scalar1=-0.25, scalar2=0.25, op0=ALU.mult, op1=ALU.add)
        S_.op("dve", "tensor_scalar", R=["tmp4b"], W=["lbB"], out=lbB[:], in0=tmp4b[:], scalar1=0.25, scalar2=0.75, op0=ALU.mult, op1=ALU.add)
        g.eps_ap = CST[:, C_EPS:C_EPS + 1]

        if "H" in stages:
            _pass_h(g)
        if "A" in stages:
            S_.barrier()
            _pass_ab(g)

        S_.final_wait("sp")
        S_.emit()
    return nc


def _pass_h(g):
    nc = g.nc; S_ = g.S_; NST = g.NST; CST = g.CST; identb = g.identb; onesb = g.onesb; m2b = g.m2b
    lbA = g.lbA; lbB = g.lbB; eps_ap = g.eps_ap

    hs = contextlib.ExitStack()
    with hs:
        def sbh(name, shape, dt):
            return hs.enter_context(nc.sbuf_tensor(name, list(shape), dt))

        def psh(name, shape, dt):
            return hs.enter_context(nc.psum_tensor(name, list(shape), dt))

        WH = sbh("WH", [128, 8, 2048], BF16)
        WOh = sbh("WOh", [128, 4, 1024], BF16)
        xt = [sbh(f"xt{i}", [128, 4, 1024], F32) for i in range(2)]
        junk = sbh("junk", [128, 1024], BF16)
        ss = sbh("ss", [128, 4], F32)
        lnr4 = sbh("lnr4", [128, 4], F32)
        rstd = sbh("rstd", [128, 4], F32)
        hn = sbh("hn", [128, 4, 1024], BF16)
        hnT = sbh("hnT", [128, 8, 512], BF16)
        TH = sbh("TH", [128, 4, 512], F32)
        QS = sbh("QS", [128, 4, 512], F32)
        SG = sbh("SG", [128, 4, 512], BF16)
        Vt = sbh("Vt", [128, 4, 512], BF16)
        KK = sbh("KK", [128, 4, 512], F32)
        CUM = sbh("CUM", [128, 4, 512], F32)
        E2 = sbh("E2", [128, 4, 512], F32)
        qdec = sbh("qdec", [128, 4, 512], BF16)
        kinvb = sbh("kinvb", [128, 4, 512], BF16)
        kend = sbh("kend", [128, 4, 512], BF16)
        kendT = sbh("kendT", [128, 4, 4, 128], BF16)
        attm = sbh("attm", [128, 4, 512], BF16)
        stS = [sbh(f"stS{i}", [128, 8, 128], F32) for i in range(4)]
        stC = sbh("stC", [128, 4, 128], F32)
        stB = [sbh(f"stB{i}", [128, 8, 128], BF16) for i in range(4)]
        SQ = sbh("SQ", [128, 512], BF16)
        LNR = sbh("LNR", [128, 512], F32)
        Rr = sbh("Rr", [128, 512], F32)
        Y1 = sbh("Y1", [128, 512], F32)
        yh = sbh("yh", [128, 4, 512], BF16)

        ptr = psh("ptr", [128, 1024], BF16)
        pp = [psh(f"pp{i}", [128, 512], F32) for i in range(2)]
        pa = psh("pa", [128, 512], F32)
        pkv = [psh(f"pkv{i}", [128, 4, 128], F32) for i in range(2)]
        po = [psh(f"po{i}", [128, 512], F32) for i in range(2)]

        PPT_LIST = [(pp[0], ("pp", 0)), (pp[1], ("pp", 1)), (pa, "pa"), (po[0], ("po", 0)), (po[1], ("po", 1))]
        TRB = [(ptr[:], "ptr"),
               (pkv[0].bitcast(BF16)[:].rearrange("p a b -> p (a b)"), ("pkv", 0)),
               (pkv[1].bitcast(BF16)[:].rearrange("p a b -> p (a b)"), ("pkv", 1)),
               (pp[0].bitcast(BF16)[:], ("pp", 0))]
        TH2 = TH[:].rearrange("p h t -> p (h t)")
        KK2 = KK[:].rearrange("p h t -> p (h t)")
        CUM2 = CUM[:].rearrange("p h t -> p (h t)")
        E22 = E2[:].rearrange("p h t -> p (h t)")
        QS2 = QS[:].rearrange("p h t -> p (h t)")
        THr = [("TH", h) for h in range(4)]
        KKr = [("KK", h) for h in range(4)]
        CUMr = [("CUM", h) for h in range(4)]
        E2r = [("E2", h) for h in range(4)]
        QSr = [("QS", h) for h in range(4)]

        ppi = [0]
        PPT = PPT_LIST

        def next_pp():
            i = ppi[0] % len(PPT)
            ppi[0] += 1
            return i

        def xr(b, j=None):
            return [("xt", b, jj) for jj in (range(4) if j is None else [j])]

        def load_x(st):
            b = st % 2
            S_.dma("sp", ("xt", b), W=xr(b), out=xt[b][:], in_=g.x_v[st])

        hnTr = [("hnT", j) for j in range(4)]

        def xnorm_pre(st):
            b = st % 2
            for j in range(4):
                S_.op("dve", "scalar_tensor_tensor", R=xr(b, j), W=["junk", ("ss", j)],
                      out=junk[:], in0=xt[b][:, j, :], scalar=1.0, in1=xt[b][:, j, :], op0=ALU.mult, op1=ALU.mult, accum_out=ss[:, j:j + 1])
            S_.op("act", "activation", R=[("ss", j) for j in range(4)] + ["CST"], W=["lnr4"],
                  out=lnr4[:], in_=ss[:], func=AF.Ln, scale=1.0 / D, bias=eps_ap)
            S_.op("act", "activation", R=["lnr4"], W=["rstd"], out=rstd[:], in_=lnr4[:], func=AF.Exp, scale=-0.5)
            for j in range(4):
                S_.op("dve", "tensor_scalar", R=xr(b, j) + ["rstd"], W=[("hn", j)],
                      out=hn[:, j, :], in0=xt[b][:, j, :], scalar1=rstd[:, j:j + 1], scalar2=None, op0=ALU.mult)

        def xnorm_T(st, j):
            tbk, tbr = TRB[j]
            for dc in range(8):
                S_.op("pe", "transpose", R=[("hn", j), "identb"], W=[tbr], acc=True, sig=(dc == 7),
                      out=tbk[:, dc * 128:(dc + 1) * 128], in_=hn[:, j, dc * 128:(dc + 1) * 128], identity=identb[:])
            S_.op("act", "activation", R=[tbr], W=[("hnT", j)],
                  out=hnT[:, :, j * 128:(j + 1) * 128], in_=tbk.rearrange("p (c t) -> p c t", t=128), func=AF.Copy)

        def xnorm_spill(st):
            if "A" in g.stages or "S" in g.stages:
                S_.dma("sp", "hnTo", R=hnTr, W=[("hnT_d", st)], out=g.hnT_d[st], in_=hnT[:].rearrange("p c t -> p (c t)"))

        def proj_fm(col0):
            i = next_pp()
            for dc in range(8):
                S_.op("pe", "matmul", R=[("WH", dc)] + hnTr, W=[PPT[i][1]], acc=True, sig=(dc == 7),
                      out=PPT[i][0][:], lhsT=WH[:, dc, col0:col0 + 128], rhs=hnT[:, dc, :], start=(dc == 0), stop=(dc == 7))
            return i

        def fproj():
            for h in range(4):
                i = proj_fm(512 + h * 128)
                S_.op("act", "activation", R=[PPT[i][1]], W=[("TH", h)], out=TH[:, h, :], in_=PPT[i][0][:], func=AF.Tanh, scale=0.5)
            for h in range(4):
                S_.op("dve", "tensor_scalar", R=[("TH", h), "lbA", "lbB"], W=[("TH", h)],
                      out=TH[:, h, :], in0=TH[:, h, :], scalar1=lbA[:, h:h + 1], scalar2=lbB[:, h:h + 1], op0=ALU.mult, op1=ALU.add)
                S_.op("dve", "tensor_scalar", R=[("TH", h)], W=[("KK", h)], out=KK[:, h, :], in0=TH[:, h, :], scalar1=-1.0, scalar2=1.0, op0=ALU.mult, op1=ALU.add)


        load_x(0)
        if NST > 1:
            load_x(1)
        xnorm_pre(0)
        for j in range(4):
            xnorm_T(0, j)
        xnorm_spill(0)

        stg = [(CUM2, CUMr), (E22, E2r), (KK2, KKr), (QS2, QSr)]
        for dc in range(8):
            w, wr = stg[dc % 4]
            S_.dma("sp" if dc % 2 == 0 else "act", ("wst", dc % 4), W=wr, out=w, in_=g.win_v[dc][:, 2048:4096])
            S_.op("dve", "tensor_scalar", R=wr + ["CST"], W=[("WH", dc)],
                  out=WH[:, dc, :], in0=w, scalar1=CST[:, C_G + dc:C_G + dc + 1], scalar2=None, op0=ALU.mult)
        for h in range(4):
            w, wr = stg[h % 4]
            S_.dma("sp" if h % 2 == 0 else "act", ("wst", h % 4), W=wr, out=w[:, 0:1024], in_=g.wout_v[4 + h])
            S_.op("dve", "tensor_copy", R=wr, W=[("WOh", h)], out=WOh[:, h, :], in_=w[:, 0:1024])
        S_.op("pool", "memset", W=[("stC", h) for h in range(4)], ap=stC[:], constant=0.0)

        for st in range(NST):
            b = st % 2
            nxt = st + 1 < NST
            fproj()
            for h in range(4):
                i = proj_fm(0 + h * 128)
                S_.op("act", "activation", R=[PPT[i][1]], W=[("QS", h)], out=QS[:, h, :], in_=PPT[i][0][:], func=AF.Silu)
            for h in range(4):
                i = proj_fm(1536 + h * 128)
                S_.op("act", "activation", R=[PPT[i][1]], W=[("SG", h)], out=SG[:, h, :], in_=PPT[i][0][:], func=AF.Silu)
            if nxt:
                xnorm_pre(st + 1)
            for h in range(4):
                S_.op("act", "activation", R=[("TH", h)], W=[("TH", h)], out=TH[:, h, :], in_=TH[:, h, :], func=AF.Ln)
                S_.op("dve", "tensor_tensor_scan", R=[("TH", h), "CST"], W=[("CUM", h)],
                      out=CUM[:, h, :], data0=CST[:, C_RST:C_RST + 512], data1=TH[:, h, :], initial=0.0, op0=ALU.mult, op1=ALU.add)
            for j in range(4):
                h = j
                i = next_pp()
                for dc in range(8):
                    S_.op("pe", "matmul", R=[("WH", dc), ("hnT", j)], W=[PPT[i][1]], acc=True, sig=(dc == 7),
                          out=PPT[i][0][:], lhsT=hnT[:, dc, j * 128:(j + 1) * 128], rhs=WH[:, dc, 1024:1536], start=(dc == 0), stop=(dc == 7))
                S_.op("dve", "tensor_copy", R=[PPT[i][1]], W=[("Vt", j)], out=Vt[:, j, :], in_=PPT[i][0][:])
                S_.op("act", "activation", R=[("CUM", h)], W=[("TH", h)], out=TH[:, h, :], in_=CUM[:, h, :], func=AF.Exp)
                S_.op("act", "activation", R=[("CUM", h)], W=[("E2", h)], out=E2[:, h, :], in_=CUM[:, h, :], func=AF.Exp, scale=-1.0)
                S_.op("dve", "tensor_tensor", R=[("QS", h), ("TH", h)], W=[("qdec", h)], out=qdec[:, h, :], in0=QS[:, h, :], in1=TH[:, h, :], op=ALU.mult)
                S_.op("dve", "tensor_tensor", R=[("KK", h), ("E2", h)], W=[("E2", h)], out=E2[:, h, :], in0=KK[:, h, :], in1=E2[:, h, :], op=ALU.mult)
            for h in range(4):
                S_.op("dve", "tensor_copy", R=[("E2", h)], W=[("kinvb", h)], out=kinvb[:, h, :], in_=E2[:, h, :])
                cd_bc = bass.AP(TH, h * 512 + 63, [[2048, 128], [64, 8], [0, 64]])
                S_.op("dve", "tensor_tensor", R=[("E2", h), ("TH", h)], W=[("kend", h)],
                      out=kend[:, h, :].rearrange("p (c t) -> p c t", t=64), in0=E2[:, h, :].rearrange("p (c t) -> p c t", t=64), in1=cd_bc, op=ALU.mult)
            if nxt:
                for j in range(4):
                    xnorm_T(st + 1, j)
                xnorm_spill(st + 1)

            for rnd in range(2):
                for hh in range(2):
                    h = rnd * 2 + hh
                    for j in range(4):
                        col = (hh * 4 + j) * 128
                        S_.op("pe", "transpose", R=[("kend", h), "identb"], W=[TRB[rnd][1]], acc=True, sig=(hh == 1 and j == 3),
                              out=TRB[rnd][0][:, col:col + 128], in_=kend[:, h, j * 128:(j + 1) * 128], identity=identb[:])
                S_.op("act", "activation", R=[TRB[rnd][1]], W=[("kendT", rnd * 2), ("kendT", rnd * 2 + 1)],
                      out=kendT[:, rnd * 2:rnd * 2 + 2, :, :].rearrange("p h j e -> p (h j e)"), in_=TRB[rnd][0], func=AF.Copy)

            def stage1(h):
                sbuf = h % 4
                for j in range(4):
                    S_.op("pe", "matmul", R=[("kinvb", h), ("qdec", h)], W=["pa"], acc=True, sig=(j == 3),
                          out=pa[:, j * 128:(j + 1) * 128], lhsT=kinvb[:, h, j * 128:(j + 1) * 128], rhs=qdec[:, h, j * 128:(j + 1) * 128], start=(j == 0), stop=(j == 3))
                S_.op("dve", "tensor_tensor", R=["pa"] + [("m2b", r) for r in range(4)], W=[("attm", h)],
                      out=attm[:, h, :], in0=pa[:], in1=m2b[:].rearrange("p c t -> p (c t)"), op=ALU.mult)
                S_.op("act", "activation", R=[("stC", h)], W=[("stB", sbuf, 0)], out=stB[sbuf][:, 0, :], in_=stC[:, h, :], func=AF.Copy)
                for c in range(8):
                    j = c // 2
                    r0 = (c % 2) * 64
                    pk = pkv[c % 2]
                    S_.op("pe", "matmul", R=[("kendT", h), ("Vt", j)], W=[("pkv", c % 2)], acc=True, sig=(c >= 6),
                          out=pk[:, c // 2, :], lhsT=kendT[r0:r0 + 64, h, j, :], rhs=Vt[r0:r0 + 64, j, h * 128:(h + 1) * 128], start=True, stop=True)
                for c in range(8):
                    src = stC[:, h, :] if c == 0 else stS[sbuf][:, c, :]
                    srcr = ("stC", h) if c == 0 else ("stS", sbuf, c)
                    dst = stC[:, h, :] if c == 7 else stS[sbuf][:, c + 1, :]
                    dstr = ("stC", h) if c == 7 else ("stS", sbuf, c + 1)
                    S_.op("dve", "scalar_tensor_tensor", R=[("pkv", c % 2), srcr, ("TH", h)], W=[dstr],
                          out=dst, in0=src, scalar=TH[:, h, c * 64 + 63:c * 64 + 64], in1=pkv[c % 2][:, c // 2, :], op0=ALU.mult, op1=ALU.add)
                S_.op("act", "activation", R=[("stS", sbuf, c) for c in range(1, 8)], W=[("stB", sbuf, 1)],
                      out=stB[sbuf][:, 1:8, :], in_=stS[sbuf][:, 1:8, :], func=AF.Copy)

            def stage2(h):
                sbuf = h % 4
                pob = po[h % 2]
                por = ("po", h % 2)
                for c in range(8):
                    j = c // 2
                    if c % 2 == 0:
                        S_.op("pe", "matmul", R=[("Vt", j), ("attm", h)], W=[por], acc=True,
                              out=pob[:, j * 128:(j + 1) * 128], lhsT=Vt[:, j, h * 128:(h + 1) * 128], rhs=attm[:, h, j * 128:(j + 1) * 128], start=(c == 0), stop=False)
                    S_.op("pe", "matmul", R=[("stB", sbuf, 0), ("stB", sbuf, 1), ("qdec", h)], W=[por], acc=True, sig=(c == 7),
                          out=pob[:, c * 64:(c + 1) * 64], lhsT=stB[sbuf][:, c, :], rhs=qdec[:, h, c * 64:(c + 1) * 64], start=False, stop=(c == 7))
                S_.op("act", "activation", R=[por], W=["SQ"], out=SQ[:], in_=pob[:], func=AF.Square)
                S_.op("pe", "matmul", R=["onesb", "SQ"], W=[("pp", 1)], sig=True, out=pp[1][:], lhsT=onesb[:], rhs=SQ[:], start=True, stop=True)
                S_.op("act", "activation", R=[("pp", 1), "CST"], W=["LNR"], out=LNR[:], in_=pp[1][:], func=AF.Ln, scale=1.0 / 128, bias=eps_ap)
                S_.op("act", "activation", R=["LNR"], W=["Rr"], out=Rr[:], in_=LNR[:], func=AF.Exp, scale=-0.5)
                S_.op("dve", "scalar_tensor_tensor", R=[por, "Rr", "CST"], W=["Y1"],
                      out=Y1[:], in0=pob[:], scalar=CST[:, C_HW + h:C_HW + h + 1], in1=Rr[:], op0=ALU.mult, op1=ALU.mult)
                S_.op("dve", "tensor_tensor", R=["Y1", ("SG", h)], W=[("yh", h)], out=yh[:, h, :], in0=Y1[:], in1=SG[:, h, :], op=ALU.mult)

            if "h1" in g.stages:
                pass
            elif "h2" in g.stages:
                for h in range(4):
                    stage1(h)
            else:
                stage1(0)
                stage1(1)
                stage1(2)
                stage1(3)
                stage2(0)
                stage2(1)
                stage2(2)
                stage2(3)

            for j in range(4 if ("h1" not in g.stages and "h2" not in g.stages) else 0):
                for dh in range(2):
                    i = next_pp()
                    for h in range(4):
                        S_.op("pe", "matmul", R=[("yh", h), ("WOh", h)], W=[PPT[i][1]], acc=True, sig=(h == 3),
                              out=PPT[i][0][:], lhsT=yh[:, h, j * 128:(j + 1) * 128], rhs=WOh[:, h, dh * 512:(dh + 1) * 512], start=(h == 0), stop=(h == 3))
                    S_.op("dve", "tensor_tensor", R=[PPT[i][1], ("xt", b, j)], W=[("xt", b, j)],
                          out=xt[b][:, j, dh * 512:(dh + 1) * 512], in0=PPT[i][0][:], in1=xt[b][:, j, dh * 512:(dh + 1) * 512], op=ALU.add)
            S_.dma("sp", ("parto", b), R=xr(b), W=[("out", st)], out=g.part_v[st], in_=xt[b][:])
            if st + 2 < NST:
                load_x(st + 2)


def _pass_ab(g):
    nc = g.nc; S_ = g.S_; NST = g.NST; CST = g.CST; identb = g.identb; onesb = g.onesb; eps_ap = g.eps_ap
    S = g.S
    c1, c2, c3 = _split_2pi()
    qs = contextlib.ExitStack()
    with qs:
        qT = qs.enter_context(nc.sbuf_tensor("qT", [128, 4, S], BF16))
        kT = qs.enter_context(nc.sbuf_tensor("kT", [128, 4, S], BF16))
        vT = qs.enter_context(nc.sbuf_tensor("vT", [128, 4, S], BF16))

        as_ = contextlib.ExitStack()
        with as_:
            def sba(name, shape, dt):
                return as_.enter_context(nc.sbuf_tensor(name, list(shape), dt))

            def psa(name, shape, dt):
                return as_.enter_context(nc.psum_tensor(name, list(shape), dt))

            WA = sba("WA", [128, 8, 2048], BF16)
            wst = [sba(f"wsta{i}", [128, 1024], F32) for i in range(4)]
            permb = sba("permb", [128, 128], BF16)
            hnT = [sba(f"hnTa{i}", [128, 8, 512], BF16) for i in range(2)]
            posi = sba("posi", [128, 512], I32)
            posf = sba("posf", [128, 512], F32)
            ang = sba("ang", [128, 512], F32)
            ki = sba("ki", [128, 512], I32)
            kf = sba("kf", [128, 512], F32)
            r1 = sba("r1", [128, 512], F32)
            r2 = sba("r2", [128, 512], F32)
            aab = sba("aab", [128, 512], F32)
            COSb = [sba(f"COS{i}", [128, 512], F32) for i in range(2)]
            SINb = [sba(f"SIN{i}", [128, 512], F32) for i in range(2)]
            Qb = [sba(f"Qb{i}", [128, 512], BF16) for i in range(2)]
            T1 = [sba(f"T1{i}", [128, 512], F32) for i in range(2)]
            T2 = [sba(f"T2{i}", [128, 512], F32) for i in range(2)]
            sgt = [sba(f"sgt{i}", [128, 512], BF16) for i in range(2)]
            pp = [psa(f"ppa{i}", [128, 512], F32) for i in range(6)]
            pq = [psa(f"pqa{i}", [128, 512], F32) for i in range(2)]

            S_.op("dve", "tensor_copy", R=["CST"], W=["permb"], out=permb[:], in_=CST[:, C_PERM:C_PERM + 128])
            for dc in range(8):
                for hf in range(2):
                    wi = (dc * 2 + hf) % 4
                    w = wst[wi]
                    S_.dma("sp" if wi % 2 == 0 else "act", ("wsta", wi), W=[("wsta", wi)], out=w[:], in_=g.win_v[dc][:, hf * 1024:(hf + 1) * 1024])
                    S_.op("dve", "tensor_scalar", R=[("wsta", wi), "CST"], W=[("WA", dc)],
                          out=WA[:, dc, hf * 1024:(hf + 1) * 1024], in0=w[:], scalar1=CST[:, C_G + dc:C_G + dc + 1], scalar2=None, op0=ALU.mult)

            ppi = [0]
            tb = [0]

            def load_hnT(st):
                hb_ = st % 2
                S_.dma("sp", ("hnTa", hb_), R=[("hnT_d", st)], W=[("hnTa", hb_)], out=hnT[hb_][:].rearrange("p c t -> p (c t)"), in_=g.hnT_d[st])

            def tables(st):
                tsl_ = slice(st * 512, (st + 1) * 512)
                COS_, SIN_ = COSb[st % 2], SINb[st % 2]
                cn, sn = ("COS", st % 2), ("SIN", st % 2)
                S_.dma("sp", "posi", W=["posi"], out=posi[:], in_=g.pos_d[:, tsl_])
                S_.op("dve", "tensor_copy", R=["posi"], W=["posf"], out=posf[:], in_=posi[:])
                S_.op("dve", "tensor_scalar", R=["posf", "CST"], W=["ang"], out=ang[:], in0=posf[:], scalar1=CST[:, C_INVF:C_INVF + 1], scalar2=None, op0=ALU.mult)
                S_.op("dve", "tensor_scalar", R=["ang"], W=["ki"], out=ki[:], in0=ang[:], scalar1=float(1.0 / TWO_PI), scalar2=None, op0=ALU.mult)
                S_.op("dve", "tensor_copy", R=["ki"], W=["kf"], out=kf[:], in_=ki[:])
                S_.op("dve", "scalar_tensor_tensor", R=["kf", "ang"], W=["r1"], out=r1[:], in0=kf[:], scalar=-c1, in1=ang[:], op0=ALU.mult, op1=ALU.add)
                S_.op("dve", "scalar_tensor_tensor", R=["kf", "r1"], W=["r2"], out=r2[:], in0=kf[:], scalar=-c2, in1=r1[:], op0=ALU.mult, op1=ALU.add)
                S_.op("dve", "scalar_tensor_tensor", R=["kf", "r2"], W=["r1"], out=r1[:], in0=kf[:], scalar=-c3, in1=r2[:], op0=ALU.mult, op1=ALU.add)
                S_.op("dve", "tensor_scalar", R=["r1"], W=["r2"], out=r2[:], in0=r1[:], scalar1=float(np.pi), scalar2=float(-np.pi), op0=ALU.min, op1=ALU.max)
                S_.op("dve", "scalar_tensor_tensor", R=["r2"], W=["aab"], out=aab[:], in0=r2[:], scalar=-1.0, in1=r2[:], op0=ALU.mult, op1=ALU.max)
                S_.op("act", "activation", R=["aab", "CST"], W=[cn], out=COS_[:], in_=aab[:], func=AF.Sin, scale=-1.0, bias=CST[:, C_HPI:C_HPI + 1])
                S_.op("act", "activation", R=["r2", "CST"], W=[sn], out=SIN_[:], in_=r2[:], func=AF.Sin, scale=CST[:, C_SGN:C_SGN + 1])

            load_hnT(0)
            tables(0)
            for st in range(NST if "a1" not in g.stages else 0):
                hb = st % 2
                tsl = slice(st * 512, (st + 1) * 512)
                if st + 1 < NST:
                    load_hnT(st + 1)
                COS, SIN = COSb[st % 2], SINb[st % 2]
                cosn, sinn = ("COS", st % 2), ("SIN", st % 2)

                hnTr = [("hnTa", hb)]
                if "a2" in g.stages:
                    continue

                def proj(col0):
                    i = ppi[0] % 6
                    ppi[0] += 1
                    for dc in range(8):
                        S_.op("pe", "matmul", R=[("WA", dc)] + hnTr, W=[("ppa", i)], acc=True, sig=(dc == 7),
                              out=pp[i][:], lhsT=WA[:, dc, col0:col0 + 128], rhs=hnT[hb][:, dc, :], start=(dc == 0), stop=(dc == 7))
                    return i

                pendq = []

                def rot(i_, t_, dst_, dname_, c_):
                    S_.op("pe", "matmul", R=["permb", ("Qb", t_)], W=[("pqa", t_)], sig=True,
                          out=pq[t_][:], lhsT=permb[:], rhs=Qb[t_][:], start=True, stop=True)
                    S_.op("dve", "tensor_tensor", R=[("Qb", t_), cosn], W=[("T1", t_)], out=T1[t_][:], in0=Qb[t_][:], in1=COS[:], op=ALU.mult)
                    S_.op("dve", "tensor_tensor", R=[("pqa", t_), sinn], W=[("T2", t_)], out=T2[t_][:], in0=pq[t_][:], in1=SIN[:], op=ALU.mult)
                    S_.op("pool", "tensor_tensor", R=[("T1", t_), ("T2", t_)], W=[(dname_, c_, st)], out=dst_[:, c_, tsl], in0=T1[t_][:], in1=T2[t_][:], op=ALU.add)

                for which, dst in ((0, qT), (1, kT)):
                    dname = "qT" if which == 0 else "kT"
                    for c in range(4):
                        i = proj(which * 512 + c * 128)
                        t = tb[0] % 2
                        tb[0] += 1
                        S_.op("act", "activation", R=[("ppa", i)], W=[("Qb", t)], out=Qb[t][:], in_=pp[i][:], func=AF.Copy)
                        if pendq:
                            rot(*pendq.pop())
                        pendq.append((i, t, dst, dname, c))
                if st + 1 < NST:
                    tables(st + 1)
                for c in range(4):
                    i = proj(1024 + c * 128)
                    S_.op("act", "activation", R=[("ppa", i)], W=[("vT", c, st)], out=vT[:, c, tsl], in_=pp[i][:], func=AF.Copy)
                    if pendq:
                        rot(*pendq.pop())
                for c in range(4 if "a5" not in g.stages else 0):
                    i = proj(1536 + c * 128)
                    t = (st * 4 + c) % 2
                    S_.op("act", "activation", R=[("ppa", i)], W=[("sgt", t)], out=sgt[t][:], in_=pp[i][:], func=AF.Silu)
                    S_.dma("sp", ("sgo", t), R=[("sgt", t)], W=[("sg_d", c, st)], out=g.sg_d[c][:, tsl], in_=sgt[t][:])

        if "B" not in g.stages:
            return
        S_.barrier()
        bs = contextlib.ExitStack()
        with bs:
            def sbb(name, shape, dt):
                return bs.enter_context(nc.sbuf_tensor(name, list(shape), dt))

            def psb_(name, shape, dt):
                return bs.enter_context(nc.psum_tensor(name, list(shape), dt))

            NVW, NPT, PVLAG = 5, 5, 2
            acc = sbb("accXY", [128, 2, S], F32)
            maskb = sbb("maskb", [128, 2, 2, 256], BF16)
            VW = [sbb(f"VW{i}", [128, 2, 256], BF16) for i in range(NVW)]
            PT = [sbb(f"PT{i}", [128, 2, 512], BF16) for i in range(NPT)]
            scx = sbb("scx", [128, 2], F32)
            SQXb = [sbb(f"SQX{i}", [128, 512], BF16) for i in range(2)]
            SQYb = [sbb(f"SQY{i}", [128, 512], BF16) for i in range(2)]
            LNR = sbb("LNRb", [128, 512], F32)
            Rr = sbb("Rrb", [128, 512], F32)
            Y1 = sbb("Y1b", [128, 512], F32)
            sgl = [sbb(f"sgl{i}", [128, 512], BF16) for i in range(2)]

            ptv = psb_("ptv", [128, 512], BF16)
            psc = [psb_(f"psc{i}", [128, 2, 512], F32) for i in range(2)]
            pxy = [psb_(f"pxy{i}", [128, 2, 256], F32) for i in range(2)]
            pU = psb_("pU", [128, 512], F32)
            PTV = [(ptv[:, 0:256], "ptv"), (pU.bitcast(BF16)[:, 0:256], "pU")]

            for hh in range(2):
                for kb in range(2):
                    S_.op("dve", "tensor_copy", R=["CST"], W=["maskb"], out=maskb[:, hh, kb, :], in_=CST[:, C_AMASK:C_AMASK + 256])
            for i in range(NVW):
                S_.op("pool", "memset", W=[("VW", i)], ap=VW[i][:], constant=1.0)
            se = float(np.sqrt(EPS))
            S_.op("pool", "memset", W=["scx"], ap=scx[0:64, 0:1], constant=1.0)
            S_.op("pool", "memset", W=["scx"], ap=scx[64:128, 0:1], constant=se)
            S_.op("pool", "memset", W=["scx"], ap=scx[0:64, 1:2], constant=se)
            S_.op("pool", "memset", W=["scx"], ap=scx[64:128, 1:2], constant=1.0)

            vwi = [0]
            pti = [0]
            gi = [0]
            SC = 1.0 / 8.0

            for c in range(4):
                first_pat = True
                pend = []
                for d in PATTERNS:
                    sub_len = S // d
                    n_blk = sub_len // 128
                    for r in range(d):
                        def tok(j, nb=1, r=r, d=d):
                            s0 = r + d * 128 * j
                            return slice(s0, s0 + d * (128 * nb - 1) + 1, d) if d > 1 else slice(s0, s0 + 128 * nb)

                        grp = {}

                        def emit_pv(j0, nkb, pbuf, vws, grp=grp, tok=tok, n_blk=n_blk, first_pat=first_pat):
                            for kb in range(nkb):
                                j = j0 + kb
                                gq = j // 2
                                if j % 2 == 0:
                                    segs = [(gq, 0, 0, min(2, n_blk - j))]
                                else:
                                    segs = [(gq, 1, 0, 1)]
                                    if j + 1 < n_blk:
                                        segs.append((gq + 1, 0, 1, 1))
                                for (gg, qpos, poff_blk, nb) in segs:
                                    if gg not in grp:
                                        tot = 2 * ((1 if gg > 0 else 0) + 1 + (1 if 2 * gg + 1 < n_blk else 0))
                                        grp[gg] = [gi[0] % 2, 0, tot]
                                        gi[0] += 1
                                    for hh in range(2):
                                        bank, cnt, tot = grp[gg]
                                        qoff = qpos * 128
                                        poff = kb * 256 + poff_blk * 128
                                        S_.op("pe", "matmul", R=[("VW", vws[kb]), ("PT", pbuf)], W=[("pxy", bank)], acc=True, sig=(cnt == tot - 1),
                                              out=pxy[bank][:, hh, qoff:qoff + nb * 128], lhsT=VW[vws[kb]][:, kb, hh * 128:(hh + 1) * 128],
                                              rhs=PT[pbuf][:, hh, poff:poff + nb * 128], start=(cnt == 0), stop=(cnt == tot - 1), skip_group_check=True)
                                        grp[gg][1] += 1
                                        if grp[gg][1] == tot:
                                            nqb = min(2, n_blk - 2 * gg)
                                            dstap = acc[:, :, tok(2 * gg, nqb)]
                                            if first_pat:
                                                S_.op("dve", "tensor_copy", R=[("pxy", bank)], W=[("acc", c)], out=dstap, in_=pxy[bank][:, :, 0:nqb * 128])
                                            else:
                                                S_.op("dve", "tensor_tensor", R=[("pxy", bank), ("acc", c)], W=[("acc", c)], out=dstap, in0=pxy[bank][:, :, 0:nqb * 128], in1=dstap, op=ALU.add)

                        for j0 in range(0, n_blk, 2):
                            nkb = min(2, n_blk - j0)
                            sbuf = (j0 // 2) % 2
                            pbuf = pti[0] % NPT
                            pti[0] += 1
                            vws = []
                            nqs = []
                            for kb in range(nkb):
                                j = j0 + kb
                                nq = 2 if j + 1 < n_blk else 1
                                nqs.append(nq)
                                for hh in range(2):
                                    S_.op("pe", "matmul", R=[("kT", c, s_) for s_ in range(NST)] + [("qT", c, s_) for s_ in range(NST)], W=[("psc", sbuf)], acc=True, sig=(kb == nkb - 1 and hh == 1),
                                          out=psc[sbuf][:, hh, kb * 256:kb * 256 + nq * 128], lhsT=kT[hh * 64:(hh + 1) * 64, c, tok(j)],
                                          rhs=qT[hh * 64:(hh + 1) * 64, c, tok(j, nq)], start=True, stop=True)
                            tvh = pti[0] % 2
                            for kb in range(nkb):
                                j = j0 + kb
                                S_.op("pe", "transpose", R=[("vT", c, s_) for s_ in range(NST)] + ["identb"], W=[PTV[tvh][1]], acc=True, sig=(kb == nkb - 1),
                                      out=PTV[tvh][0][:, kb * 128:(kb + 1) * 128], in_=vT[:, c, tok(j)], identity=identb[:])
                            ncol = (nkb - 1) * 256 + nqs[-1] * 128
                            S_.op("act", "activation", R=[("psc", sbuf)], W=[("PT", pbuf)], out=PT[pbuf][:, :, 0:ncol], in_=psc[sbuf][:, :, 0:ncol], func=AF.Exp, scale=SC)
                            meng = "dve"
                            S_.op(meng, "tensor_tensor", R=[("PT", pbuf), "maskb"], W=[("PT", pbuf)], out=PT[pbuf][:, :, 0:ncol], in0=PT[pbuf][:, :, 0:ncol],
                                  in1=maskb[:].rearrange("p h k q -> p h (k q)")[:, :, 0:ncol], op=ALU.mult)
                            v = vwi[0] % NVW
                            vwi[0] += 1
                            vws = [v] * nkb
                            dst = bass.AP(VW[v], 0, [[512, 128], [256, nkb], [192, 2], [1, 64]])
                            S_.op("act", "activation", R=[PTV[tvh][1]], W=[("VW", v)], out=dst,
                                  in_=PTV[tvh][0][:, 0:nkb * 128].rearrange("p (k h e) -> p k h e", h=2, e=64), func=AF.Copy)
                            pend.append((emit_pv, (j0, nkb, pbuf, vws)))
                            if len(pend) > PVLAG:
                                f_, a_ = pend.pop(0)
                                f_(*a_)
                    first_pat = False
                while pend:
                    f_, a_ = pend.pop(0)
                    f_(*a_)

                def norm_sq(st):
                    tsl_ = slice(st * 512, (st + 1) * 512)
                    t_ = st % 2
                    S_.op("act", "activation", R=[("acc", c), "scx"], W=[("SQX", t_)], out=SQXb[t_][:], in_=acc[:, 0, tsl_], func=AF.Square, scale=scx[:, 0:1])
                    S_.op("act", "activation", R=[("acc", c), "scx"], W=[("SQY", t_)], out=SQYb[t_][:], in_=acc[:, 1, tsl_], func=AF.Square, scale=scx[:, 1:2])

                norm_sq(0)
                for st in range(NST):
                    tsl = slice(st * 512, (st + 1) * 512)
                    t = st % 2
                    S_.dma("sp", ("sgl", t), R=[("sg_d", c, st)], W=[("sgl", t)], out=sgl[t][:], in_=g.sg_d[c][:, tsl])
                    if st + 1 < NST:
                        norm_sq(st + 1)
                    S_.op("pe", "matmul", R=["onesb", ("SQX", t)], W=["pU"], out=pU[0:64, :], lhsT=onesb[:, 0:64], rhs=SQXb[t][:], start=True, stop=True)
                    S_.op("pe", "matmul", R=["onesb", ("SQY", t)], W=["pU"], sig=True, out=pU[64:128, :], lhsT=onesb[:, 0:64], rhs=SQYb[t][:], start=True, stop=True)
                    S_.op("act", "activation", R=["pU"], W=["LNRb"], out=LNR[:], in_=pU[:], func=AF.Ln, scale=1.0 / 64)
                    S_.op("act", "activation", R=["LNRb"], W=["Rrb"], out=Rr[:], in_=LNR[:], func=AF.Exp, scale=-0.5)
                    S_.op("dve", "scalar_tensor_tensor", R=[("acc", c), "Rrb", "CST"], W=["Y1b"],
                          out=Y1[0:64, :], in0=acc[0:64, 0, tsl], scalar=CST[0:64, C_AW + c:C_AW + c + 1], in1=Rr[0:64, :], op0=ALU.mult, op1=ALU.mult)
                    S_.op("dve", "scalar_tensor_tensor", R=[("acc", c), "Rrb", "CST"], W=["Y1b"],
                          out=Y1[64:128, :], in0=acc[64:128, 1, tsl], scalar=CST[64:128, C_AW + c:C_AW + c + 1], in1=Rr[64:128, :], op0=ALU.mult, op1=ALU.mult)
                    S_.op("pool", "tensor_tensor", R=["Y1b", ("sgl", t)], W=[("qT", c, st)], out=qT[:, c, tsl], in0=Y1[:], in1=sgl[t][:], op=ALU.mult)

        if "C" in g.stages:
            S_.barrier()
            _pass_c_inner(g, qT)


def _pass_c_inner(g, qT):
    nc = g.nc; S_ = g.S_; NST = g.NST; CST = g.CST; eps_ap = g.eps_ap
    cs = contextlib.ExitStack()
    with cs:
        def sbc(name, shape, dt):
            return cs.enter_context(nc.sbuf_tensor(name, list(shape), dt))

        def psc_(name, shape, dt):
            return cs.enter_context(nc.psum_tensor(name, list(shape), dt))

        WOa = sbc("WOa", [128, 4, 1024], BF16)
        wst = [sbc(f"wstc{i}", [128, 1024], F32) for i in range(2)]
        fnw = sbc("fnw_sb", [128, D], F32)
        zt = [sbc(f"zt{i}", [128, 4, 1024], F32) for i in range(2)]
        junk = sbc("junkc", [128, 1024], BF16)
        ss = sbc("ssc", [128, 4], F32)
        lnr4 = sbc("lnr4c", [128, 4], F32)
        rstd = sbc("rstdc", [128, 4], F32)
        pc = [psc_(f"pc{i}", [128, 512], F32) for i in range(8)]

        def load_part(st):
            b = st % 2
            S_.dma("sp", ("zt", b), R=[("out", st)], W=[("zt", b, j) for j in range(4)], out=zt[b][:], in_=g.part_v[st])

        for c in range(4):
            w = wst[c % 2]
            S_.dma("act", ("wstc", c % 2), W=[("wstc", c % 2)], out=w[:], in_=g.wout_v[c])
            S_.op("dve", "tensor_copy", R=[("wstc", c % 2)], W=[("WOa", c)], out=WOa[:, c, :], in_=w[:])
        S_.dma("sp", "fnw", W=["fnw"], out=fnw[:], in_=g.fnw_d)

        pci = [0]
        load_part(0)
        for st in range(NST):
            b = st % 2
            if st + 1 < NST:
                load_part(st + 1)
            for j in range(4):
                tok = slice(st * 512 + j * 128, st * 512 + (j + 1) * 128)
                for dh in range(2):
                    i = pci[0] % 8
                    pci[0] += 1
                    for c in range(4):
                        S_.op("pe", "matmul", R=[("qT", c, st), ("WOa", c)], W=[("pc", i)], acc=True, sig=(c == 3),
                              out=pc[i][:], lhsT=qT[:, c, tok], rhs=WOa[:, c, dh * 512:(dh + 1) * 512], start=(c == 0), stop=(c == 3))
                    S_.op("dve", "tensor_tensor", R=[("pc", i), ("zt", b, j)], W=[("zt", b, j)],
                          out=zt[b][:, j, dh * 512:(dh + 1) * 512], in0=pc[i][:], in1=zt[b][:, j, dh * 512:(dh + 1) * 512], op=ALU.add)
                S_.op("act", "activation", R=[("zt", b, j)], W=["junkc", ("ssc", j)],
                      out=junk[:], in_=zt[b][:, j, :], func=AF.Square, accum_out=ss[:, j:j + 1])
            S_.op("act", "activation", R=[("ssc", j) for j in range(4)] + ["CST"], W=["lnr4c"], out=lnr4[:], in_=ss[:], func=AF.Ln, scale=1.0 / D, bias=eps_ap)
            S_.op("act", "activation", R=["lnr4c"], W=["rstdc"], out=rstd[:], in_=lnr4[:], func=AF.Exp, scale=-0.5)
            for j in range(4):
                S_.op("dve", "scalar_tensor_tensor", R=[("zt", b, j), "rstdc", "fnw"], W=[("zt", b, j)],
                      out=zt[b][:, j, :], in0=zt[b][:, j, :], scalar=rstd[:, j:j + 1], in1=fnw[:], op0=ALU.mult, op1=ALU.mult)
            S_.dma("sp", ("outo", b), R=[("zt", b, j) for j in range(4)], W=[("out", st)], out=g.out_v[st], in_=zt[b][:])


def prep_core_inputs(inputs, b, S=4096):
    cst = _const_table()
    cst[:, C_G:C_G + 8] = np.asarray(inputs["mix_norm_w"], np.float32)[0].reshape(8, 128).T
    cst[:, C_AW:C_AW + 4] = np.asarray(inputs["attn_out_norm_w"], np.float32)[0].reshape(4, 128).T
    cst[:, C_HW:C_HW + 4] = np.asarray(inputs["hgrn_out_norm_w"], np.float32)[0].reshape(4, 128).T
    lb = np.asarray(inputs["hgrn_lb_raw"], np.float32)
    cst[:, C_LB0:C_LB0 + 4] = lb[0].reshape(4, 128).T
    cst[:, C_LB1:C_LB1 + 4] = lb[1].reshape(4, 128).T
    pos = np.asarray(inputs["positions"])[b, :S].astype(np.int32)
    return {
        "x": np.ascontiguousarray(np.asarray(inputs["x"], np.float32)[b, :S]),
        "pos": np.ascontiguousarray(np.broadcast_to(pos[None, :], (128, S))),
        "w_in": np.ascontiguousarray(np.asarray(inputs["w_in"], np.float32)[0]),
        "w_out": np.ascontiguousarray(np.asarray(inputs["w_out"], np.float32)[0]),
        "cst": cst,
        "fnw": np.ascontiguousarray(np.broadcast_to(np.asarray(inputs["final_norm_w"], np.float32)[None, :], (128, D))),
    }


_NC_CACHE = {}


def kernel(x, positions, w_in, w_out, mix_norm_w, attn_out_norm_w, hgrn_out_norm_w, hgrn_lb_raw, final_norm_w):
    inputs = dict(x=x, positions=positions, w_in=w_in, w_out=w_out, mix_norm_w=mix_norm_w,
                  attn_out_norm_w=attn_out_norm_w, hgrn_out_norm_w=hgrn_out_norm_w,
                  hgrn_lb_raw=hgrn_lb_raw, final_norm_w=final_norm_w)
    B, S, _ = np.asarray(x).shape
    if S not in _NC_CACHE:
        _NC_CACHE[S] = build(S)
    nc = _NC_CACHE[S]
    in_maps = [prep_core_inputs(inputs, b, S) for b in range(B)]
    res = run_bass_kernel_spmd(nc, in_maps, core_ids=list(range(B)))
    return np.stack([np.asarray(r["out"], np.float32) for r in res.results], axis=0)
```

```python
import contextlib
import numpy as np
import concourse.bass as bass
import concourse.mybir as mybir
from concourse.bass_utils import run_bass_kernel_spmd

F32 = mybir.dt.float32
BF16 = mybir.dt.bfloat16
I32 = mybir.dt.int32
AF = mybir.ActivationFunctionType
ALU = mybir.AluOpType

D = 1024
EPS = 1e-6
ROPE_THETA = 500000.0
PATTERNS = (1, 4, 16)
TWO_PI = 2.0 * np.pi


def _split_2pi():
    c1 = 6.28125
    r = TWO_PI - c1
    c2 = np.float32(r)
    c2 = (c2.view(np.uint32) & np.uint32(0xFFFFF000)).view(np.float32)
    c3 = np.float32(r - float(c2))
    return float(c1), float(c2), float(c3)


C_G = 0
C_AW = 8
C_HW = 12
C_LB0 = 16
C_LB1 = 20
C_INVF = 24
C_SGN = 25
C_HPI = 26
C_EPS = 27
C_ONE = 28
C_IDENT = 32
C_PERM = 160
C_AMASK = 288
C_M2 = 544
C_RST = 672
NCST = 1184


def _const_table():
    c = np.zeros((128, NCST), np.float32)
    p = np.arange(128)
    e = p % 64
    half = 8
    invf = ROPE_THETA ** (-np.arange(half, dtype=np.float64) * (2.0 / 16.0))
    c[:, C_INVF] = np.where(e < 16, invf[e % 8], 0.0)
    c[:, C_SGN] = np.where(e < 8, -1.0, np.where(e < 16, 1.0, 0.0))
    c[:, C_HPI] = np.pi / 2
    c[:, C_EPS] = EPS
    c[:, C_ONE] = 1.0
    c[:, C_IDENT:C_IDENT + 128] = np.eye(128)
    perm = np.zeros((128, 128), np.float32)
    for m in range(128):
        em = m % 64
        if em < 8:
            perm[m + 8, m] = 1.0
        elif em < 16:
            perm[m - 8, m] = 1.0
    c[:, C_PERM:C_PERM + 128] = perm
    k = np.arange(128)[:, None]
    q = np.arange(128)[None, :]
    am = np.concatenate([(q >= k), (q <= k)], axis=1).astype(np.float32)
    c[:, C_AMASK:C_AMASK + 256] = am
    s = np.arange(128)[:, None]
    t = np.arange(128)[None, :]
    c[:, C_M2:C_M2 + 128] = ((s // 64 == t // 64) & (s <= t)).astype(np.float32)
    rst = np.ones(512, np.float32)
    rst[0::64] = 0.0
    c[:, C_RST:C_RST + 512] = rst[None, :]
    return c


class Sched:
    COMPUTE = ("pe", "act", "dve", "pool")

    def __init__(self, nc, stack):
        self.nc = nc
        self.stack = stack
        self.streams = {k: [] for k in ("pe", "act", "dve", "pool", "sp")}
        self.sems = {}
        self.cnt = {}
        self.sigs = {k: [] for k in self.COMPUTE}
        self.waited = {k: {} for k in self.streams}
        self.last_w = {}
        self.readers = {}
        for k in self.COMPUTE:
            self._sem(k)

    def _sem(self, name):
        if name not in self.sems:
            self.sems[name] = self.stack.enter_context(self.nc.semaphore("s_" + "".join(ch if ch.isalnum() else "_" for ch in str(name))))
            self.cnt[name] = 0
        return self.sems[name]

    def _resolve(self, ev):
        kind = ev[0]
        if kind == "dma":
            return ev[1], ev[2]
        eng, seq = ev[1], ev[2]
        best = None
        for s, v in reversed(self.sigs[eng]):
            if s >= seq:
                best = v
            else:
                break
        if best is not None:
            return eng, best
        rec = self.streams[eng][-1]
        assert rec["seq"] >= seq
        self.cnt[eng] += 1
        rec["inc"] = True
        self.sigs[eng].append((rec["seq"], self.cnt[eng]))
        return eng, self.cnt[eng]

    def _collect(self, eng, reads, writes, acc):
        evs = []
        for r in reads:
            ev = self.last_w.get(r)
            if ev is not None:
                evs.append(ev)
        for w in writes:
            ev = self.last_w.get(w)
            if ev is not None and not (acc and ev[0] == "c" and ev[1] == eng):
                evs.append(ev)
            for ev in self.readers.get(w, ()):
                evs.append(ev)
        waits = {}
        for ev in evs:
            name, val = self._resolve(ev)
            if self.waited[eng].get(name, 0) >= val:
                continue
            waits[name] = max(waits.get(name, 0), val)
        for name, val in waits.items():
            self.waited[eng][name] = val
        return list(waits.items())

    def _commit(self, ev, reads, writes):
        for r in reads:
            self.readers.setdefault(r, []).append(ev)
        for w in writes:
            self.last_w[w] = ev
            self.readers[w] = []

    def op(self, eng, method, R=(), W=(), sig=None, acc=False, **kw):
        if sig is None:
            sig = (eng != "pe")
        reads, writes = list(R), list(W)
        fn = (lambda e, method=method, kw=kw: getattr(e, method)(**kw))
        waits = self._collect(eng, reads, writes, acc)
        seq = len(self.streams[eng])
        rec = {"fn": fn, "waits": waits, "inc": False, "seq": seq, "dma": None}
        self.streams[eng].append(rec)
        if sig:
            self.cnt[eng] += 1
            rec["inc"] = True
            self.sigs[eng].append((seq, self.cnt[eng]))
        self._commit(("c", eng, seq), reads, writes)

    def dma(self, q, key, R=(), W=(), **kw):
        reads, writes = list(R), list(W)
        fn = (lambda e, kw=kw: e.dma_start(**kw))
        name = ("dma", key)
        self._sem(name)
        waits = self._collect(q, reads, writes, False)
        self.cnt[name] += 16
        rec = {"fn": fn, "waits": waits, "inc": False, "seq": len(self.streams[q]), "dma": name}
        self.streams[q].append(rec)
        self._commit(("dma", name, self.cnt[name]), reads, writes)

    def barrier(self):
        for eng in self.COMPUTE:
            if self.streams[eng]:
                rec = self.streams[eng][-1]
                if not rec["inc"]:
                    self.cnt[eng] += 1
                    rec["inc"] = True
                    self.sigs[eng].append((rec["seq"], self.cnt[eng]))
        for eng in self.streams:
            waits = []
            for name, val in self.cnt.items():
                if val <= 0 or name == eng:
                    continue
                if self.waited[eng].get(name, 0) >= val:
                    continue
                self.waited[eng][name] = val
                waits.append((name, val))
            self.streams[eng].append({"fn": None, "waits": waits, "inc": False, "seq": len(self.streams[eng]), "dma": None})

    def final_wait(self, q):
        waits = {}
        for name, val in self.cnt.items():
            if isinstance(name, tuple) and name[0] == "dma" and val > 0:
                waits[name] = val
        rec = {"fn": None, "waits": list(waits.items()), "inc": False, "seq": len(self.streams[q]), "dma": None}
        self.streams[q].append(rec)

    def emit(self):
        nc = self.nc
        sems = self.sems

        def replay(name):
            def run(e):
                for rec in self.streams[name]:
                    for sname, val in rec["waits"]:
                        e.wait_ge(sems[sname], val)
                    if rec["fn"] is None:
                        continue
                    ins = rec["fn"](e)
                    if rec["dma"] is not None:
                        ins.then_inc(sems[rec["dma"]], 16)
                    elif rec["inc"]:
                        ins.then_inc(sems[name], 1)
            return run

        with nc.Block() as block:
            block.tensor(replay("pe"))
            block.scalar(replay("act"))
            block.vector(replay("dve"))
            block.gpsimd(replay("pool"))
            block.sync(replay("sp"))


class Ctx:
    pass


def build(S=4096, stages=("H", "A", "B", "C"), dbg=()):
    g = Ctx()
    g.S = S
    g.NST = S // 512
    g.NT = S // 128
    g.stages = stages
    nc = bass.Bass("TRN2", target_bir_lowering=False)
    g.nc = nc
    x_d = nc.dram_tensor("x", [S, D], F32, kind="ExternalInput").ap()
    g.pos_d = nc.dram_tensor("pos", [128, S], I32, kind="ExternalInput").ap()
    win_d = nc.dram_tensor("w_in", [D, 4096], F32, kind="ExternalInput").ap()
    wout_d = nc.dram_tensor("w_out", [D, D], F32, kind="ExternalInput").ap()
    cst_d = nc.dram_tensor("cst", [128, NCST], F32, kind="ExternalInput").ap()
    g.fnw_d = nc.dram_tensor("fnw", [128, D], F32, kind="ExternalInput").ap()
    out_d = nc.dram_tensor("out", [S, D], F32, kind="ExternalOutput").ap()
    part_d = out_d if "C" in stages else nc.dram_tensor("part_scr", [S, D], F32, kind="ExternalOutput").ap()
    g.hnT_d = nc.dram_tensor("hnT_scr", [g.NST, 128, 8 * 512], BF16, kind="Internal").ap()
    g.sg_d = nc.dram_tensor("sg_scr", [4, 128, S], BF16, kind="Internal").ap()
    g.dbg_d = {}
    for name, shape in dbg:
        g.dbg_d[name] = nc.dram_tensor(name, list(shape), F32, kind="ExternalOutput").ap()

    g.x_v = x_d.rearrange("(n j p) d -> n p j d", j=4, p=128)
    g.part_v = part_d.rearrange("(n j p) d -> n p j d", j=4, p=128)
    g.out_v = out_d.rearrange("(n j p) d -> n p j d", j=4, p=128)
    g.win_v = win_d.rearrange("(dc p) c -> dc p c", p=128)
    g.wout_v = wout_d.rearrange("(mc p) c -> mc p c", p=128)

    stack = contextlib.ExitStack()
    with stack:
        S_ = Sched(nc, stack)
        g.S_ = S_

        def sb(name, shape, dt):
            return stack.enter_context(nc.sbuf_tensor(name, list(shape), dt))

        g.CST = CST = sb("CST", [128, NCST], F32)
        g.identb = identb = sb("identb", [128, 128], BF16)
        g.onesb = onesb = sb("onesb", [128, 128], BF16)
        g.m2b = m2b = sb("m2b", [128, 4, 128], BF16)
        g.lbA = lbA = sb("lbA", [128, 4], F32)
        g.lbB = lbB = sb("lbB", [128, 4], F32)
        tmp4 = sb("tmp4", [128, 4], F32)
        tmp4b = sb("tmp4b", [128, 4], F32)

        S_.dma("sp", "cst", W=["CST"], out=CST[:], in_=cst_d)
        S_.op("dve", "tensor_copy", R=["CST"], W=["identb"], out=identb[:], in_=CST[:, C_IDENT:C_IDENT + 128])
        S_.op("dve", "memset", W=["onesb"], ap=onesb[:], constant=1.0)
        for r in range(4):
            S_.op("dve", "tensor_copy", R=["CST"], W=[("m2b", r)], out=m2b[:, r, :], in_=CST[:, C_M2:C_M2 + 128])
        S_.op("dve", "tensor_tensor", R=["CST"], W=["tmp4"], out=tmp4[:], in0=CST[:, C_LB0:C_LB0 + 4], in1=CST[:, C_LB1:C_LB1 + 4], op=ALU.subtract)
        S_.op("act", "activation", R=["tmp4"], W=["tmp4b"], out=tmp4b[:], in_=tmp4[:], func=AF.Tanh, scale=0.5)
        S_.op("dve", "tensor_scalar", R=["tmp4b"], W=["lbA"], out=lbA[:], in0=tmp4b[:], scalar1=-0.25, scalar2=0.25, op0=ALU.mult, op1=ALU.add)
        S_.op("dve", "tensor_scalar", R=["tmp4b"], W=["lbB"], out=lbB[:], in0=tmp4b[:], scalar1=0.25, scalar2=0.75, op0=ALU.mult, op1=ALU.add)
        g.eps_ap = CST[:, C_EPS:C_EPS + 1]

        if "H" in stages:
            _pass_h(g)
        if "A" in stages:
            S_.barrier()
            _pass_ab(g)

        S_.final_wait("sp")
        S_.emit()
    return nc


def _pass_h(g):
    nc = g.nc; S_ = g.S_; NST = g.NST; CST = g.CST; identb = g.identb; onesb = g.onesb; m2b = g.m2b
    lbA = g.lbA; lbB = g.lbB; eps_ap = g.eps_ap

    hs = contextlib.ExitStack()
    with hs:
        def sbh(name, shape, dt):
            return hs.enter_context(nc.sbuf_tensor(name, list(shape), dt))

        def psh(name, shape, dt):
            return hs.enter_context(nc.psum_tensor(name, list(shape), dt))

        WH = sbh("WH", [128, 8, 2048], BF16)
        WOh = sbh("WOh", [128, 4, 1024], BF16)
        xt = [sbh(f"xt{i}", [128, 4, 1024], F32) for i in range(2)]
        junk = sbh("junk", [128, 1024], BF16)
        ss = sbh("ss", [128, 4], F32)
        lnr4 = sbh("lnr4", [128, 4], F32)
        rstd = sbh("rstd", [128, 4], F32)
        hn = sbh("hn", [128, 4, 1024], BF16)
        hnT = sbh("hnT", [128, 8, 512], BF16)
        TH = sbh("TH", [128, 4, 512], F32)
        QS = sbh("QS", [128, 4, 512], F32)
        SG = sbh("SG", [128, 4, 512], BF16)
        Vt = sbh("Vt", [128, 4, 512], BF16)
        KK = sbh("KK", [128, 4, 512], F32)
        CUM = sbh("CUM", [128, 4, 512], F32)
        E2 = sbh("E2", [128, 4, 512], F32)
        qdec = sbh("qdec", [128, 4, 512], BF16)
        kinvb = sbh("kinvb", [128, 4, 512], BF16)
        kend = sbh("kend", [128, 4, 512], BF16)
        kendT = sbh("kendT", [128, 4, 4, 128], BF16)
        attm = sbh("attm", [128, 4, 512], BF16)
        stS = [sbh(f"stS{i}", [128, 8, 128], F32) for i in range(4)]
        stC = sbh("stC", [128, 4, 128], F32)
        stB = [sbh(f"stB{i}", [128, 8, 128], BF16) for i in range(4)]
        SQ = sbh("SQ", [128, 512], BF16)
        LNR = sbh("LNR", [128, 512], F32)
        Rr = sbh("Rr", [128, 512], F32)
        Y1 = sbh("Y1", [128, 512], F32)
        yh = sbh("yh", [128, 4, 512], BF16)

        ptr = psh("ptr", [128, 1024], BF16)
        pp = [psh(f"pp{i}", [128, 512], F32) for i in range(2)]
        pa = psh("pa", [128, 512], F32)
        pkv = [psh(f"pkv{i}", [128, 4, 128], F32) for i in range(2)]
        po = [psh(f"po{i}", [128, 512], F32) for i in range(2)]

        PPT_LIST = [(pp[0], ("pp", 0)), (pp[1], ("pp", 1)), (pa, "pa"), (po[0], ("po", 0)), (po[1], ("po", 1))]
        TRB = [(ptr[:], "ptr"),
               (pkv[0].bitcast(BF16)[:].rearrange("p a b -> p (a b)"), ("pkv", 0)),
               (pkv[1].bitcast(BF16)[:].rearrange("p a b -> p (a b)"), ("pkv", 1)),
               (pp[0].bitcast(BF16)[:], ("pp", 0))]
        TH2 = TH[:].rearrange("p h t -> p (h t)")
        KK2 = KK[:].rearrange("p h t -> p (h t)")
        CUM2 = CUM[:].rearrange("p h t -> p (h t)")
        E22 = E2[:].rearrange("p h t -> p (h t)")
        QS2 = QS[:].rearrange("p h t -> p (h t)")
        THr = [("TH", h) for h in range(4)]
        KKr = [("KK", h) for h in range(4)]
        CUMr = [("CUM", h) for h in range(4)]
        E2r = [("E2", h) for h in range(4)]
        QSr = [("QS", h) for h in range(4)]

        ppi = [0]
        PPT = PPT_LIST

        def next_pp():
            i = ppi[0] % len(PPT)
            ppi[0] += 1
            return i

        def xr(b, j=None):
            return [("xt", b, jj) for jj in (range(4) if j is None else [j])]

        def load_x(st):
            b = st % 2
            S_.dma("sp", ("xt", b), W=xr(b), out=xt[b][:], in_=g.x_v[st])

        hnTr = [("hnT", j) for j in range(4)]

        def xnorm_pre(st):
            b = st % 2
            for j in range(4):
                S_.op("dve", "scalar_tensor_tensor", R=xr(b, j), W=["junk", ("ss", j)],
                      out=junk[:], in0=xt[b][:, j, :], scalar=1.0, in1=xt[b][:, j, :], op0=ALU.mult, op1=ALU.mult, accum_out=ss[:, j:j + 1])
            S_.op("act", "activation", R=[("ss", j) for j in range(4)] + ["CST"], W=["lnr4"],
                  out=lnr4[:], in_=ss[:], func=AF.Ln, scale=1.0 / D, bias=eps_ap)
            S_.op("act", "activation", R=["lnr4"], W=["rstd"], out=rstd[:], in_=lnr4[:], func=AF.Exp, scale=-0.5)
            for j in range(4):
                S_.op("dve", "tensor_scalar", R=xr(b, j) + ["rstd"], W=[("hn", j)],
                      out=hn[:, j, :], in0=xt[b][:, j, :], scalar1=rstd[:, j:j + 1], scalar2=None, op0=ALU.mult)

        def xnorm_T(st, j):
            tbk, tbr = TRB[j]
            for dc in range(8):
                S_.op("pe", "transpose", R=[("hn", j), "identb"], W=[tbr], acc=True, sig=(dc == 7),
                      out=tbk[:, dc * 128:(dc + 1) * 128], in_=hn[:, j, dc * 128:(dc + 1) * 128], identity=identb[:])
            S_.op("act", "activation", R=[tbr], W=[("hnT", j)],
                  out=hnT[:, :, j * 128:(j + 1) * 128], in_=tbk.rearrange("p (c t) -> p c t", t=128), func=AF.Copy)

        def xnorm_spill(st):
            if "A" in g.stages or "S" in g.stages:
                S_.dma("sp", "hnTo", R=hnTr, W=[("hnT_d", st)], out=g.hnT_d[st], in_=hnT[:].rearrange("p c t -> p (c t)"))

        def proj_fm(col0):
            i = next_pp()
            for dc in range(8):
                S_.op("pe", "matmul", R=[("WH", dc)] + hnTr, W=[PPT[i][1]], acc=True, sig=(dc == 7),
                      out=PPT[i][0][:], lhsT=WH[:, dc, col0:col0 + 128], rhs=hnT[:, dc, :], start=(dc == 0), stop=(dc == 7))
            return i

        def fproj():
            for h in range(4):
                i = proj_fm(512 + h * 128)
                S_.op("act", "activation", R=[PPT[i][1]], W=[("TH", h)], out=TH[:, h, :], in_=PPT[i][0][:], func=AF.Tanh, scale=0.5)
            for h in range(4):
                S_.op("dve", "tensor_scalar", R=[("TH", h), "lbA", "lbB"], W=[("TH", h)],
                      out=TH[:, h, :], in0=TH[:, h, :], scalar1=lbA[:, h:h + 1], scalar2=lbB[:, h:h + 1], op0=ALU.mult, op1=ALU.add)
                S_.op("dve", "tensor_scalar", R=[("TH", h)], W=[("KK", h)], out=KK[:, h, :], in0=TH[:, h, :], scalar1=-1.0, scalar2=1.0, op0=ALU.mult, op1=ALU.add)


        load_x(0)
        if NST > 1:
            load_x(1)
        xnorm_pre(0)
        for j in range(4):
            xnorm_T(0, j)
        xnorm_spill(0)

        stg = [(CUM2, CUMr), (E22, E2r), (KK2, KKr), (QS2, QSr)]
        for dc in range(8):
            w, wr = stg[dc % 4]
            S_.dma("sp" if dc % 2 == 0 else "act", ("wst", dc % 4), W=wr, out=w, in_=g.win_v[dc][:, 2048:4096])
            S_.op("dve", "tensor_scalar", R=wr + ["CST"], W=[("WH", dc)],
                  out=WH[:, dc, :], in0=w, scalar1=CST[:, C_G + dc:C_G + dc + 1], scalar2=None, op0=ALU.mult)
        for h in range(4):
            w, wr = stg[h % 4]
            S_.dma("sp" if h % 2 == 0 else "act", ("wst", h % 4), W=wr, out=w[:, 0:1024], in_=g.wout_v[4 + h])
            S_.op("dve", "tensor_copy", R=wr, W=[("WOh", h)], out=WOh[:, h, :], in_=w[:, 0:1024])
        S_.op("pool", "memset", W=[("stC", h) for h in range(4)], ap=stC[:], constant=0.0)

        for st in range(NST):
            b = st % 2
            nxt = st + 1 < NST
            fproj()
            for h in range(4):
                i = proj_fm(0 + h * 128)
                S_.op("act", "activation", R=[PPT[i][1]], W=[("QS", h)], out=QS[:, h, :], in_=PPT[i][0][:], func=AF.Silu)
            for h in range(4):
                i = proj_fm(1536 + h * 128)
                S_.op("act", "activation", R=[PPT[i][1]], W=[("SG", h)], out=SG[:, h, :], in_=PPT[i][0][:], func=AF.Silu)
            if nxt:
                xnorm_pre(st + 1)
            for h in range(4):
                S_.op("act", "activation", R=[("TH", h)], W=[("TH", h)], out=TH[:, h, :], in_=TH[:, h, :], func=AF.Ln)
                S_.op("dve", "tensor_tensor_scan", R=[("TH", h), "CST"], W=[("CUM", h)],
                      out=CUM[:, h, :], data0=CST[:, C_RST:C_RST + 512], data1=TH[:, h, :], initial=0.0, op0=ALU.mult, op1=ALU.add)
            for j in range(4):
                h = j
                i = next_pp()
                for dc in range(8):
                    S_.op("pe", "matmul", R=[("WH", dc), ("hnT", j)], W=[PPT[i][1]], acc=True, sig=(dc == 7),
                          out=PPT[i][0][:], lhsT=hnT[:, dc, j * 128:(j + 1) * 128], rhs=WH[:, dc, 1024:1536], start=(dc == 0), stop=(dc == 7))
                S_.op("dve", "tensor_copy", R=[PPT[i][1]], W=[("Vt", j)], out=Vt[:, j, :], in_=PPT[i][0][:])
                S_.op("act", "activation", R=[("CUM", h)], W=[("TH", h)], out=TH[:, h, :], in_=CUM[:, h, :], func=AF.Exp)
                S_.op("act", "activation", R=[("CUM", h)], W=[("E2", h)], out=E2[:, h, :], in_=CUM[:, h, :], func=AF.Exp, scale=-1.0)
                S_.op("dve", "tensor_tensor", R=[("QS", h), ("TH", h)], W=[("qdec", h)], out=qdec[:, h, :], in0=QS[:, h, :], in1=TH[:, h, :], op=ALU.mult)
                S_.op("dve", "tensor_tensor", R=[("KK", h), ("E2", h)], W=[("E2", h)], out=E2[:, h, :], in0=KK[:, h, :], in1=E2[:, h, :], op=ALU.mult)
            for h in range(4):
                S_.op("dve", "tensor_copy", R=[("E2", h)], W=[("kinvb", h)], out=kinvb[:, h, :], in_=E2[:, h, :])
                cd_bc = bass.AP(TH, h * 512 + 63, [[2048, 128], [64, 8], [0, 64]])
                S_.op("dve", "tensor_tensor", R=[("E2", h), ("TH", h)], W=[("kend", h)],
                      out=kend[:, h, :].rearrange("p (c t) -> p c t", t=64), in0=E2[:, h, :].rearrange("p (c t) -> p c t", t=64), in1=cd_bc, op=ALU.mult)
            if nxt:
                for j in range(4):
                    xnorm_T(st + 1, j)
                xnorm_spill(st + 1)

            for rnd in range(2):
                for hh in range(2):
                    h = rnd * 2 + hh
                    for j in range(4):
                        col = (hh * 4 + j) * 128
                        S_.op("pe", "transpose", R=[("kend", h), "identb"], W=[TRB[rnd][1]], acc=True, sig=(hh == 1 and j == 3),
                              out=TRB[rnd][0][:, col:col + 128], in_=kend[:, h, j * 128:(j + 1) * 128], identity=identb[:])
                S_.op("act", "activation", R=[TRB[rnd][1]], W=[("kendT", rnd * 2), ("kendT", rnd * 2 + 1)],
                      out=kendT[:, rnd * 2:rnd * 2 + 2, :, :].rearrange("p h j e -> p (h j e)"), in_=TRB[rnd][0], func=AF.Copy)

            def stage1(h):
                sbuf = h % 4
                for j in range(4):
                    S_.op("pe", "matmul", R=[("kinvb", h), ("qdec", h)], W=["pa"], acc=True, sig=(j == 3),
                          out=pa[:, j * 128:(j + 1) * 128], lhsT=kinvb[:, h, j * 128:(j + 1) * 128], rhs=qdec[:, h, j * 128:(j + 1) * 128], start=(j == 0), stop=(j == 3))
                S_.op("dve", "tensor_tensor", R=["pa"] + [("m2b", r) for r in range(4)], W=[("attm", h)],
                      out=attm[:, h, :], in0=pa[:], in1=m2b[:].rearrange("p c t -> p (c t)"), op=ALU.mult)
                S_.op("act", "activation", R=[("stC", h)], W=[("stB", sbuf, 0)], out=stB[sbuf][:, 0, :], in_=stC[:, h, :], func=AF.Copy)
                for c in range(8):
                    j = c // 2
                    r0 = (c % 2) * 64
                    pk = pkv[c % 2]
                    S_.op("pe", "matmul", R=[("kendT", h), ("Vt", j)], W=[("pkv", c % 2)], acc=True, sig=(c >= 6),
                          out=pk[:, c // 2, :], lhsT=kendT[r0:r0 + 64, h, j, :], rhs=Vt[r0:r0 + 64, j, h * 128:(h + 1) * 128], start=True, stop=True)
                for c in range(8):
                    src = stC[:, h, :] if c == 0 else stS[sbuf][:, c, :]
                    srcr = ("stC", h) if c == 0 else ("stS", sbuf, c)
                    dst = stC[:, h, :] if c == 7 else stS[sbuf][:, c + 1, :]
                    dstr = ("stC", h) if c == 7 else ("stS", sbuf, c + 1)
                    S_.op("dve", "scalar_tensor_tensor", R=[("pkv", c % 2), srcr, ("TH", h)], W=[dstr],
                          out=dst, in0=src, scalar=TH[:, h, c * 64 + 63:c * 64 + 64], in1=pkv[c % 2][:, c // 2, :], op0=ALU.mult, op1=ALU.add)
                S_.op("act", "activation", R=[("stS", sbuf, c) for c in range(1, 8)], W=[("stB", sbuf, 1)],
                      out=stB[sbuf][:, 1:8, :], in_=stS[sbuf][:, 1:8, :], func=AF.Copy)

            def stage2(h):
                sbuf = h % 4
                pob = po[h % 2]
                por = ("po", h % 2)
                for c in range(8):
                    j = c // 2
                    if c % 2 == 0:
                        S_.op("pe", "matmul", R=[("Vt", j), ("attm", h)], W=[por], acc=True,
                              out=pob[:, j * 128:(j + 1) * 128], lhsT=Vt[:, j, h * 128:(h + 1) * 128], rhs=attm[:, h, j * 128:(j + 1) * 128], start=(c == 0), stop=False)
                    S_.op("pe", "matmul", R=[("stB", sbuf, 0), ("stB", sbuf, 1), ("qdec", h)], W=[por], acc=True, sig=(c == 7),
                          out=pob[:, c * 64:(c + 1) * 64], lhsT=stB[sbuf][:, c, :], rhs=qdec[:, h, c * 64:(c + 1) * 64], start=False, stop=(c == 7))
                S_.op("act", "activation", R=[por], W=["SQ"], out=SQ[:], in_=pob[:], func=AF.Square)
                S_.op("pe", "matmul", R=["onesb", "SQ"], W=[("pp", 1)], sig=True, out=pp[1][:], lhsT=onesb[:], rhs=SQ[:], start=True, stop=True)
                S_.op("act", "activation", R=[("pp", 1), "CST"], W=["LNR"], out=LNR[:], in_=pp[1][:], func=AF.Ln, scale=1.0 / 128, bias=eps_ap)
                S_.op("act", "activation", R=["LNR"], W=["Rr"], out=Rr[:], in_=LNR[:], func=AF.Exp, scale=-0.5)
                S_.op("dve", "scalar_tensor_tensor", R=[por, "Rr", "CST"], W=["Y1"],
                      out=Y1[:], in0=pob[:], scalar=CST[:, C_HW + h:C_HW + h + 1], in1=Rr[:], op0=ALU.mult, op1=ALU.mult)
                S_.op("dve", "tensor_tensor", R=["Y1", ("SG", h)], W=[("yh", h)], out=yh[:, h, :], in0=Y1[:], in1=SG[:, h, :], op=ALU.mult)

            if "h1" in g.stages:
                pass
            elif "h2" in g.stages:
                for h in range(4):
                    stage1(h)
            else:
                stage1(0)
                stage1(1)
                stage1(2)
                stage1(3)
                stage2(0)
                stage2(1)
                stage2(2)
                stage2(3)

            for j in range(4 if ("h1" not in g.stages and "h2" not in g.stages) else 0):
                for dh in range(2):
                    i = next_pp()
                    for h in range(4):
                        S_.op("pe", "matmul", R=[("yh", h), ("WOh", h)], W=[PPT[i][1]], acc=True, sig=(h == 3),
                              out=PPT[i][0][:], lhsT=yh[:, h, j * 128:(j + 1) * 128], rhs=WOh[:, h, dh * 512:(dh + 1) * 512], start=(h == 0), stop=(h == 3))
                    S_.op("dve", "tensor_tensor", R=[PPT[i][1], ("xt", b, j)], W=[("xt", b, j)],
                          out=xt[b][:, j, dh * 512:(dh + 1) * 512], in0=PPT[i][0][:], in1=xt[b][:, j, dh * 512:(dh + 1) * 512], op=ALU.add)
            S_.dma("sp", ("parto", b), R=xr(b), W=[("out", st)], out=g.part_v[st], in_=xt[b][:])
            if st + 2 < NST:
                load_x(st + 2)


def _pass_ab(g):
    nc = g.nc; S_ = g.S_; NST = g.NST; CST = g.CST; identb = g.identb; onesb = g.onesb; eps_ap = g.eps_ap
    S = g.S
    c1, c2, c3 = _split_2pi()
    qs = contextlib.ExitStack()
    with qs:
        qT = qs.enter_context(nc.sbuf_tensor("qT", [128, 4, S], BF16))
        kT = qs.enter_context(nc.sbuf_tensor("kT", [128, 4, S], BF16))
        vT = qs.enter_context(nc.sbuf_tensor("vT", [128, 4, S], BF16))

        as_ = contextlib.ExitStack()
        with as_:
            def sba(name, shape, dt):
                return as_.enter_context(nc.sbuf_tensor(name, list(shape), dt))

            def psa(name, shape, dt):
                return as_.enter_context(nc.psum_tensor(name, list(shape), dt))

            WA = sba("WA", [128, 8, 2048], BF16)
            wst = [sba(f"wsta{i}", [128, 1024], F32) for i in range(4)]
            permb = sba("permb", [128, 128], BF16)
            hnT = [sba(f"hnTa{i}", [128, 8, 512], BF16) for i in range(2)]
            posi = sba("posi", [128, 512], I32)
            posf = sba("posf", [128, 512], F32)
            ang = sba("ang", [128, 512], F32)
            ki = sba("ki", [128, 512], I32)
            kf = sba("kf", [128, 512], F32)
            r1 = sba("r1", [128, 512], F32)
            r2 = sba("r2", [128, 512], F32)
            aab = sba("aab", [128, 512], F32)
            COSb = [sba(f"COS{i}", [128, 512], F32) for i in range(2)]
            SINb = [sba(f"SIN{i}", [128, 512], F32) for i in range(2)]
            Qb = [sba(f"Qb{i}", [128, 512], BF16) for i in range(2)]
            T1 = [sba(f"T1{i}", [128, 512], F32) for i in range(2)]
            T2 = [sba(f"T2{i}", [128, 512], F32) for i in range(2)]
            sgt = [sba(f"sgt{i}", [128, 512], BF16) for i in range(2)]
            pp = [psa(f"ppa{i}", [128, 512], F32) for i in range(6)]
            pq = [psa(f"pqa{i}", [128, 512], F32) for i in range(2)]

            S_.op("dve", "tensor_copy", R=["CST"], W=["permb"], out=permb[:], in_=CST[:, C_PERM:C_PERM + 128])
            ppi = [0]
            tb = [0]

            def load_hnT(st):
                hb_ = st % 2
                S_.dma("sp", ("hnTa", hb_), R=[("hnT_d", st)], W=[("hnTa", hb_)], out=hnT[hb_][:].rearrange("p c t -> p (c t)"), in_=g.hnT_d[st])

            def tables(st):
                tsl_ = slice(st * 512, (st + 1) * 512)
                COS_, SIN_ = COSb[st % 2], SINb[st % 2]
                cn, sn = ("COS", st % 2), ("SIN", st % 2)
                S_.dma("sp", "posi", W=["posi"], out=posi[:], in_=g.pos_d[:, tsl_])
                S_.op("dve", "tensor_copy", R=["posi"], W=["posf"], out=posf[:], in_=posi[:])
                S_.op("dve", "tensor_scalar", R=["posf", "CST"], W=["ang"], out=ang[:], in0=posf[:], scalar1=CST[:, C_INVF:C_INVF + 1], scalar2=None, op0=ALU.mult)
                S_.op("dve", "tensor_scalar", R=["ang"], W=["ki"], out=ki[:], in0=ang[:], scalar1=float(1.0 / TWO_PI), scalar2=None, op0=ALU.mult)
                S_.op("dve", "tensor_copy", R=["ki"], W=["kf"], out=kf[:], in_=ki[:])
                S_.op("dve", "scalar_tensor_tensor", R=["kf", "ang"], W=["r1"], out=r1[:], in0=kf[:], scalar=-c1, in1=ang[:], op0=ALU.mult, op1=ALU.add)
                S_.op("dve", "scalar_tensor_tensor", R=["kf", "r1"], W=["r2"], out=r2[:], in0=kf[:], scalar=-c2, in1=r1[:], op0=ALU.mult, op1=ALU.add)
                S_.op("dve", "scalar_tensor_tensor", R=["kf", "r2"], W=["r1"], out=r1[:], in0=kf[:], scalar=-c3, in1=r2[:], op0=ALU.mult, op1=ALU.add)
                S_.op("dve", "tensor_scalar", R=["r1"], W=["r2"], out=r2[:], in0=r1[:], scalar1=float(np.pi), scalar2=float(-np.pi), op0=ALU.min, op1=ALU.max)
                S_.op("dve", "scalar_tensor_tensor", R=["r2"], W=["aab"], out=aab[:], in0=r2[:], scalar=-1.0, in1=r2[:], op0=ALU.mult, op1=ALU.max)
                S_.op("act", "activation", R=["aab", "CST"], W=[cn], out=COS_[:], in_=aab[:], func=AF.Sin, scale=-1.0, bias=CST[:, C_HPI:C_HPI + 1])
                S_.op("act", "activation", R=["r2", "CST"], W=[sn], out=SIN_[:], in_=r2[:], func=AF.Sin, scale=CST[:, C_SGN:C_SGN + 1])

            load_hnT(0)
            tables(0)
            for dc in range(8):
                for hf in range(2):
                    wi = (dc * 2 + hf) % 4
                    w = wst[wi]
                    S_.dma("sp" if wi % 2 == 0 else "act", ("wsta", wi), W=[("wsta", wi)], out=w[:], in_=g.win_v[dc][:, hf * 1024:(hf + 1) * 1024])
                    S_.op("dve", "tensor_scalar", R=[("wsta", wi), "CST"], W=[("WA", dc)],
                          out=WA[:, dc, hf * 1024:(hf + 1) * 1024], in0=w[:], scalar1=CST[:, C_G + dc:C_G + dc + 1], scalar2=None, op0=ALU.mult)

            for st in range(NST if "a1" not in g.stages else 0):
                hb = st % 2
                tsl = slice(st * 512, (st + 1) * 512)
                if st + 1 < NST:
                    load_hnT(st + 1)
                COS, SIN = COSb[st % 2], SINb[st % 2]
                cosn, sinn = ("COS", st % 2), ("SIN", st % 2)

                hnTr = [("hnTa", hb)]
                if "a2" in g.stages:
                    continue

                def proj(col0):
                    i = ppi[0] % 6
                    ppi[0] += 1
                    for dc in range(8):
                        S_.op("pe", "matmul", R=[("WA", dc)] + hnTr, W=[("ppa", i)], acc=True, sig=(dc == 7),
                              out=pp[i][:], lhsT=WA[:, dc, col0:col0 + 128], rhs=hnT[hb][:, dc, :], start=(dc == 0), stop=(dc == 7))
                    return i

                pendq = []

                def rot(i_, t_, dst_, dname_, c_):
                    S_.op("pe", "matmul", R=["permb", ("Qb", t_)], W=[("pqa", t_)], sig=True,
                          out=pq[t_][:], lhsT=permb[:], rhs=Qb[t_][:], start=True, stop=True)
                    S_.op("dve", "tensor_tensor", R=[("Qb", t_), cosn], W=[("T1", t_)], out=T1[t_][:], in0=Qb[t_][:], in1=COS[:], op=ALU.mult)
                    S_.op("dve", "tensor_tensor", R=[("pqa", t_), sinn], W=[("T2", t_)], out=T2[t_][:], in0=pq[t_][:], in1=SIN[:], op=ALU.mult)
                    S_.op("pool", "tensor_tensor", R=[("T1", t_), ("T2", t_)], W=[(dname_, c_, st)], out=dst_[:, c_, tsl], in0=T1[t_][:], in1=T2[t_][:], op=ALU.add)

                for which, dst in ((0, qT), (1, kT)):
                    dname = "qT" if which == 0 else "kT"
                    for c in range(4):
                        i = proj(which * 512 + c * 128)
                        t = tb[0] % 2
                        tb[0] += 1
                        S_.op("act", "activation", R=[("ppa", i)], W=[("Qb", t)], out=Qb[t][:], in_=pp[i][:], func=AF.Copy)
                        if pendq:
                            rot(*pendq.pop())
                        pendq.append((i, t, dst, dname, c))
                if st + 1 < NST:
                    tables(st + 1)
                for c in range(4):
                    i = proj(1024 + c * 128)
                    S_.op("act", "activation", R=[("ppa", i)], W=[("vT", c, st)], out=vT[:, c, tsl], in_=pp[i][:], func=AF.Copy)
                    if pendq:
                        rot(*pendq.pop())
                for c in range(4 if "a5" not in g.stages else 0):
                    i = proj(1536 + c * 128)
                    t = (st * 4 + c) % 2
                    S_.op("act", "activation", R=[("ppa", i)], W=[("sgt", t)], out=sgt[t][:], in_=pp[i][:], func=AF.Silu)
                    S_.dma("sp", ("sgo", t), R=[("sgt", t)], W=[("sg_d", c, st)], out=g.sg_d[c][:, tsl], in_=sgt[t][:])

        if "B" not in g.stages:
            return
        S_.barrier()
        bs = contextlib.ExitStack()
        with bs:
            def sbb(name, shape, dt):
                return bs.enter_context(nc.sbuf_tensor(name, list(shape), dt))

            def psb_(name, shape, dt):
                return bs.enter_context(nc.psum_tensor(name, list(shape), dt))

            NVW, NPT, PVLAG = 5, 5, 2
            acc = sbb("accXY", [128, 2, S], F32)
            maskb = sbb("maskb", [128, 2, 2, 256], BF16)
            VW = [sbb(f"VW{i}", [128, 2, 256], BF16) for i in range(NVW)]
            PT = [sbb(f"PT{i}", [128, 2, 512], BF16) for i in range(NPT)]
            scx = sbb("scx", [128, 2], F32)
            SQXb = [sbb(f"SQX{i}", [128, 512], BF16) for i in range(2)]
            SQYb = [sbb(f"SQY{i}", [128, 512], BF16) for i in range(2)]
            LNR = sbb("LNRb", [128, 512], F32)
            Rr = sbb("Rrb", [128, 512], F32)
            Y1 = sbb("Y1b", [128, 512], F32)
            sgl = [sbb(f"sgl{i}", [128, 512], BF16) for i in range(2)]

            ptv = psb_("ptv", [128, 512], BF16)
            psc = [psb_(f"psc{i}", [128, 2, 512], F32) for i in range(2)]
            pxy = [psb_(f"pxy{i}", [128, 2, 256], F32) for i in range(2)]
            pU = psb_("pU", [128, 512], F32)
            PTV = [(ptv[:, 0:256], "ptv"), (pU.bitcast(BF16)[:, 0:256], "pU")]

            for hh in range(2):
                for kb in range(2):
                    S_.op("dve", "tensor_copy", R=["CST"], W=["maskb"], out=maskb[:, hh, kb, :], in_=CST[:, C_AMASK:C_AMASK + 256])
            for i in range(NVW):
                S_.op("pool", "memset", W=[("VW", i)], ap=VW[i][:], constant=1.0)
            se = float(np.sqrt(EPS))
            S_.op("pool", "memset", W=["scx"], ap=scx[0:64, 0:1], constant=1.0)
            S_.op("pool", "memset", W=["scx"], ap=scx[64:128, 0:1], constant=se)
            S_.op("pool", "memset", W=["scx"], ap=scx[0:64, 1:2], constant=se)
            S_.op("pool", "memset", W=["scx"], ap=scx[64:128, 1:2], constant=1.0)

            vwi = [0]
            pti = [0]
            gi = [0]
            SC = 1.0 / 8.0

            for c in range(4):
                first_pat = True
                pend = []
                for d in PATTERNS:
                    sub_len = S // d
                    n_blk = sub_len // 128
                    for r in range(d):
                        def tok(j, nb=1, r=r, d=d):
                            s0 = r + d * 128 * j
                            return slice(s0, s0 + d * (128 * nb - 1) + 1, d) if d > 1 else slice(s0, s0 + 128 * nb)

                        grp = {}

                        def emit_pv(j0, nkb, pbuf, vws, grp=grp, tok=tok, n_blk=n_blk, first_pat=first_pat):
                            for kb in range(nkb):
                                j = j0 + kb
                                gq = j // 2
                                if j % 2 == 0:
                                    segs = [(gq, 0, 0, min(2, n_blk - j))]
                                else:
                                    segs = [(gq, 1, 0, 1)]
                                    if j + 1 < n_blk:
                                        segs.append((gq + 1, 0, 1, 1))
                                for (gg, qpos, poff_blk, nb) in segs:
                                    if gg not in grp:
                                        tot = 2 * ((1 if gg > 0 else 0) + 1 + (1 if 2 * gg + 1 < n_blk else 0))
                                        grp[gg] = [gi[0] % 2, 0, tot]
                                        gi[0] += 1
                                    for hh in range(2):
                                        bank, cnt, tot = grp[gg]
                                        qoff = qpos * 128
                                        poff = kb * 256 + poff_blk * 128
                                        S_.op("pe", "matmul", R=[("VW", vws[kb]), ("PT", pbuf)], W=[("pxy", bank)], acc=True, sig=(cnt == tot - 1),
                                              out=pxy[bank][:, hh, qoff:qoff + nb * 128], lhsT=VW[vws[kb]][:, kb, hh * 128:(hh + 1) * 128],
                                              rhs=PT[pbuf][:, hh, poff:poff + nb * 128], start=(cnt == 0), stop=(cnt == tot - 1), skip_group_check=True)
                                        grp[gg][1] += 1
                                        if grp[gg][1] == tot:
                                            nqb = min(2, n_blk - 2 * gg)
                                            dstap = acc[:, :, tok(2 * gg, nqb)]
                                            if first_pat:
                                                S_.op("dve", "tensor_copy", R=[("pxy", bank)], W=[("acc", c)], out=dstap, in_=pxy[bank][:, :, 0:nqb * 128])
                                            else:
                                                S_.op("dve", "tensor_tensor", R=[("pxy", bank), ("acc", c)], W=[("acc", c)], out=dstap, in0=pxy[bank][:, :, 0:nqb * 128], in1=dstap, op=ALU.add)

                        for j0 in range(0, n_blk, 2):
                            nkb = min(2, n_blk - j0)
                            sbuf = (j0 // 2) % 2
                            pbuf = pti[0] % NPT
                            pti[0] += 1
                            vws = []
                            nqs = []
                            for kb in range(nkb):
                                j = j0 + kb
                                nq = 2 if j + 1 < n_blk else 1
                                nqs.append(nq)
                                for hh in range(2):
                                    S_.op("pe", "matmul", R=[("kT", c, s_) for s_ in range(NST)] + [("qT", c, s_) for s_ in range(NST)], W=[("psc", sbuf)], acc=True, sig=(kb == nkb - 1 and hh == 1),
                                          out=psc[sbuf][:, hh, kb * 256:kb * 256 + nq * 128], lhsT=kT[hh * 64:(hh + 1) * 64, c, tok(j)],
                                          rhs=qT[hh * 64:(hh + 1) * 64, c, tok(j, nq)], start=True, stop=True)
                            tvh = pti[0] % 2
                            for kb in range(nkb):
                                j = j0 + kb
                                S_.op("pe", "transpose", R=[("vT", c, s_) for s_ in range(NST)] + ["identb"], W=[PTV[tvh][1]], acc=True, sig=(kb == nkb - 1),
                                      out=PTV[tvh][0][:, kb * 128:(kb + 1) * 128], in_=vT[:, c, tok(j)], identity=identb[:])
                            ncol = (nkb - 1) * 256 + nqs[-1] * 128
                            S_.op("act", "activation", R=[("psc", sbuf)], W=[("PT", pbuf)], out=PT[pbuf][:, :, 0:ncol], in_=psc[sbuf][:, :, 0:ncol], func=AF.Exp, scale=SC)
                            meng = "dve"
                            S_.op(meng, "tensor_tensor", R=[("PT", pbuf), "maskb"], W=[("PT", pbuf)], out=PT[pbuf][:, :, 0:ncol], in0=PT[pbuf][:, :, 0:ncol],
                                  in1=maskb[:].rearrange("p h k q -> p h (k q)")[:, :, 0:ncol], op=ALU.mult)
                            v = vwi[0] % NVW
                            vwi[0] += 1
                            vws = [v] * nkb
                            dst = bass.AP(VW[v], 0, [[512, 128], [256, nkb], [192, 2], [1, 64]])
                            S_.op("act", "activation", R=[PTV[tvh][1]], W=[("VW", v)], out=dst,
                                  in_=PTV[tvh][0][:, 0:nkb * 128].rearrange("p (k h e) -> p k h e", h=2, e=64), func=AF.Copy)
                            pend.append((emit_pv, (j0, nkb, pbuf, vws)))
                            if len(pend) > PVLAG:
                                f_, a_ = pend.pop(0)
                                f_(*a_)
                    first_pat = False
                while pend:
                    f_, a_ = pend.pop(0)
                    f_(*a_)

                def norm_sq(st):
                    tsl_ = slice(st * 512, (st + 1) * 512)
                    t_ = st % 2
                    S_.op("act", "activation", R=[("acc", c), "scx"], W=[("SQX", t_)], out=SQXb[t_][:], in_=acc[:, 0, tsl_], func=AF.Square, scale=scx[:, 0:1])
                    S_.op("act", "activation", R=[("acc", c), "scx"], W=[("SQY", t_)], out=SQYb[t_][:], in_=acc[:, 1, tsl_], func=AF.Square, scale=scx[:, 1:2])

                norm_sq(0)
                for st in range(NST):
                    tsl = slice(st * 512, (st + 1) * 512)
                    t = st % 2
                    S_.dma("sp", ("sgl", t), R=[("sg_d", c, st)], W=[("sgl", t)], out=sgl[t][:], in_=g.sg_d[c][:, tsl])
                    if st + 1 < NST:
                        norm_sq(st + 1)
                    S_.op("pe", "matmul", R=["onesb", ("SQX", t)], W=["pU"], out=pU[0:64, :], lhsT=onesb[:, 0:64], rhs=SQXb[t][:], start=True, stop=True)
                    S_.op("pe", "matmul", R=["onesb", ("SQY", t)], W=["pU"], sig=True, out=pU[64:128, :], lhsT=onesb[:, 0:64], rhs=SQYb[t][:], start=True, stop=True)
                    S_.op("act", "activation", R=["pU"], W=["LNRb"], out=LNR[:], in_=pU[:], func=AF.Ln, scale=1.0 / 64)
                    S_.op("act", "activation", R=["LNRb"], W=["Rrb"], out=Rr[:], in_=LNR[:], func=AF.Exp, scale=-0.5)
                    S_.op("dve", "scalar_tensor_tensor", R=[("acc", c), "Rrb", "CST"], W=["Y1b"],
                          out=Y1[0:64, :], in0=acc[0:64, 0, tsl], scalar=CST[0:64, C_AW + c:C_AW + c + 1], in1=Rr[0:64, :], op0=ALU.mult, op1=ALU.mult)
                    S_.op("dve", "scalar_tensor_tensor", R=[("acc", c), "Rrb", "CST"], W=["Y1b"],
                          out=Y1[64:128, :], in0=acc[64:128, 1, tsl], scalar=CST[64:128, C_AW + c:C_AW + c + 1], in1=Rr[64:128, :], op0=ALU.mult, op1=ALU.mult)
                    S_.op("pool", "tensor_tensor", R=["Y1b", ("sgl", t)], W=[("qT", c, st)], out=qT[:, c, tsl], in0=Y1[:], in1=sgl[t][:], op=ALU.mult)

        if "C" in g.stages:
            S_.barrier()
            _pass_c_inner(g, qT)


def _pass_c_inner(g, qT):
    nc = g.nc; S_ = g.S_; NST = g.NST; CST = g.CST; eps_ap = g.eps_ap
    cs = contextlib.ExitStack()
    with cs:
        def sbc(name, shape, dt):
            return cs.enter_context(nc.sbuf_tensor(name, list(shape), dt))

        def psc_(name, shape, dt):
            return cs.enter_context(nc.psum_tensor(name, list(shape), dt))

        WOa = sbc("WOa", [128, 4, 1024], BF16)
        wst = [sbc(f"wstc{i}", [128, 1024], F32) for i in range(2)]
        fnw = sbc("fnw_sb", [128, D], F32)
        zt = [sbc(f"zt{i}", [128, 4, 1024], F32) for i in range(2)]
        junk = sbc("junkc", [128, 1024], BF16)
        ss = sbc("ssc", [128, 4], F32)
        lnr4 = sbc("lnr4c", [128, 4], F32)
        rstd = sbc("rstdc", [128, 4], F32)
        pc = [psc_(f"pc{i}", [128, 512], F32) for i in range(8)]

        def load_part(st):
            b = st % 2
            S_.dma("sp", ("zt", b), R=[("out", st)], W=[("zt", b, j) for j in range(4)], out=zt[b][:], in_=g.part_v[st])

        for c in range(4):
            w = wst[c % 2]
            S_.dma("act", ("wstc", c % 2), W=[("wstc", c % 2)], out=w[:], in_=g.wout_v[c])
            S_.op("dve", "tensor_copy", R=[("wstc", c % 2)], W=[("WOa", c)], out=WOa[:, c, :], in_=w[:])
        S_.dma("sp", "fnw", W=["fnw"], out=fnw[:], in_=g.fnw_d)

        pci = [0]
        load_part(0)
        for st in range(NST):
            b = st % 2
            if st + 1 < NST:
                load_part(st + 1)
            for j in range(4):
                tok = slice(st * 512 + j * 128, st * 512 + (j + 1) * 128)
                for dh in range(2):
                    i = pci[0] % 8
                    pci[0] += 1
                    for c in range(4):
                        S_.op("pe", "matmul", R=[("qT", c, st), ("WOa", c)], W=[("pc", i)], acc=True, sig=(c == 3),
                              out=pc[i][:], lhsT=qT[:, c, tok], rhs=WOa[:, c, dh * 512:(dh + 1) * 512], start=(c == 0), stop=(c == 3))
                    S_.op("dve", "tensor_tensor", R=[("pc", i), ("zt", b, j)], W=[("zt", b, j)],
                          out=zt[b][:, j, dh * 512:(dh + 1) * 512], in0=pc[i][:], in1=zt[b][:, j, dh * 512:(dh + 1) * 512], op=ALU.add)
                S_.op("act", "activation", R=[("zt", b, j)], W=["junkc", ("ssc", j)],
                      out=junk[:], in_=zt[b][:, j, :], func=AF.Square, accum_out=ss[:, j:j + 1])
            S_.op("act", "activation", R=[("ssc", j) for j in range(4)] + ["CST"], W=["lnr4c"], out=lnr4[:], in_=ss[:], func=AF.Ln, scale=1.0 / D, bias=eps_ap)
            S_.op("act", "activation", R=["lnr4c"], W=["rstdc"], out=rstd[:], in_=lnr4[:], func=AF.Exp, scale=-0.5)
            for j in range(4):
                S_.op("dve", "scalar_tensor_tensor", R=[("zt", b, j), "rstdc", "fnw"], W=[("zt", b, j)],
                      out=zt[b][:, j, :], in0=zt[b][:, j, :], scalar=rstd[:, j:j + 1], in1=fnw[:], op0=ALU.mult, op1=ALU.mult)
            S_.dma("sp", ("outo", b), R=[("zt", b, j) for j in range(4)], W=[("out", st)], out=g.out_v[st], in_=zt[b][:])


def prep_core_inputs(inputs, b, S=4096):
    cst = _const_table()
    cst[:, C_G:C_G + 8] = np.asarray(inputs["mix_norm_w"], np.float32)[0].reshape(8, 128).T
    cst[:, C_AW:C_AW + 4] = np.asarray(inputs["attn_out_norm_w"], np.float32)[0].reshape(4, 128).T
    cst[:, C_HW:C_HW + 4] = np.asarray(inputs["hgrn_out_norm_w"], np.float32)[0].reshape(4, 128).T
    lb = np.asarray(inputs["hgrn_lb_raw"], np.float32)
    cst[:, C_LB0:C_LB0 + 4] = lb[0].reshape(4, 128).T
    cst[:, C_LB1:C_LB1 + 4] = lb[1].reshape(4, 128).T
    pos = np.asarray(inputs["positions"])[b, :S].astype(np.int32)
    return {
        "x": np.ascontiguousarray(np.asarray(inputs["x"], np.float32)[b, :S]),
        "pos": np.ascontiguousarray(np.broadcast_to(pos[None, :], (128, S))),
        "w_in": np.ascontiguousarray(np.asarray(inputs["w_in"], np.float32)[0]),
        "w_out": np.ascontiguousarray(np.asarray(inputs["w_out"], np.float32)[0]),
        "cst": cst,
        "fnw": np.ascontiguousarray(np.broadcast_to(np.asarray(inputs["final_norm_w"], np.float32)[None, :], (128, D))),
    }


_NC_CACHE = {}


def kernel(x, positions, w_in, w_out, mix_norm_w, attn_out_norm_w, hgrn_out_norm_w, hgrn_lb_raw, final_norm_w):
    inputs = dict(x=x, positions=positions, w_in=w_in, w_out=w_out, mix_norm_w=mix_norm_w,
                  attn_out_norm_w=attn_out_norm_w, hgrn_out_norm_w=hgrn_out_norm_w,
                  hgrn_lb_raw=hgrn_lb_raw, final_norm_w=final_norm_w)
    B, S, _ = np.asarray(x).shape
    if S not in _NC_CACHE:
        _NC_CACHE[S] = build(S)
    nc = _NC_CACHE[S]
    in_maps = [prep_core_inputs(inputs, b, S) for b in range(B)]
    res = run_bass_kernel_spmd(nc, in_maps, core_ids=list(range(B)))
    return np.stack([np.asarray(r["out"], np.float32) for r in res.results], axis=0)
```
